# Optimizing a Trainium2 kernel written in Bass

```python
import math
import jax
import jax.numpy as jnp
from jax import lax
import numpy as np

D_MODEL = 1024
BATCH = 4
SEQ = 8192
DEPTH = 1

GRID_W = 64
CTX_LEN = 256

DA_HEADS = 8
DA_HEAD_DIM = 64
DA_V_DIM = 2 * DA_HEAD_DIM
DA_WIDTH = DA_HEADS * 2 * DA_HEAD_DIM
DA_OUT = DA_HEADS * DA_V_DIM
Q_BLOCK = 128
ROPE_BASE = 10000.0

DN_HEADS = 8
DN_HEAD_DIM = 128
DN_WIDTH = DN_HEADS * DN_HEAD_DIM
DN_CONV = 7
DN_CHUNK = 64

D_FF = 4 * D_MODEL
N_ADA = 6

PROJ_SIZES = (DA_WIDTH, DA_WIDTH, DA_OUT,
              DN_WIDTH, DN_WIDTH, DN_WIDTH, DN_WIDTH,
              DN_HEADS, DN_HEADS, DN_HEADS, DN_HEADS,
              D_MODEL, D_MODEL)
PROJ_DIM = 2 * DA_WIDTH + DA_OUT + 4 * DN_WIDTH + 4 * DN_HEADS + 2 * D_MODEL

DEEPNORM_ALPHA = (2 * DEPTH) ** 0.25
DEEPNORM_BETA = (8 * DEPTH) ** -0.25
LN_EPS = 1e-5
RMS_EPS = 1e-6

kernel_name = 'hybrid_diffattn_gdn_dit_block'


def layer_norm(x, gain=None, bias=None):
    xf = x.astype(jnp.float32)
    xc = xf - jnp.mean(xf, axis=-1, keepdims=True)
    y = xc * lax.rsqrt(jnp.mean(xc * xc, axis=-1, keepdims=True) + LN_EPS)
    if gain is not None:
        y = y * gain.astype(jnp.float32) + bias.astype(jnp.float32)
    return y.astype(x.dtype)


def rms_norm(x, gain):
    xf = x.astype(jnp.float32)
    y = xf * lax.rsqrt(jnp.mean(xf * xf, axis=-1, keepdims=True) + RMS_EPS)
    return (y * gain.astype(jnp.float32)).astype(x.dtype)


def l2_normalize(x):
    xf = x.astype(jnp.float32)
    return (xf * lax.rsqrt(jnp.sum(xf * xf, axis=-1, keepdims=True) + RMS_EPS)).astype(x.dtype)


def modulate(h, shift, scale):
    return h * (1.0 + scale) + shift


def split_proj(p):
    cuts = [int(i) for i in np.cumsum(PROJ_SIZES)[:-1]]
    return jnp.split(p, cuts, axis=-1)


def axial_rope(seq_len, dtype):
    rows = seq_len // GRID_W
    row = jnp.repeat(jnp.arange(rows, dtype=jnp.float32), GRID_W)
    col = jnp.tile(jnp.arange(GRID_W, dtype=jnp.float32), rows)
    half = DA_HEAD_DIM // 2
    inv_freq = ROPE_BASE ** (-jnp.arange(0, half, 2, dtype=jnp.float32) / half)
    ang_r = row[:, None] * inv_freq
    ang_c = col[:, None] * inv_freq
    ang = jnp.concatenate([ang_r, ang_r, ang_c, ang_c], axis=-1)
    return jnp.cos(ang).astype(dtype), jnp.sin(ang).astype(dtype)


def apply_rope(t, cos, sin):
    r1, r2, c1, c2 = jnp.split(t, 4, axis=-1)
    rot = jnp.concatenate([-r2, r1, -c2, c1], axis=-1)
    return t * cos[None, :, None, None, :] + rot * sin[None, :, None, None, :]


def depthwise_conv_centred(x, w):
    ch = x.shape[-1]
    pad = w.shape[0] // 2
    return lax.conv_general_dilated(x, w[:, None, :].astype(x.dtype), window_strides=(1,),
                                    padding=[(pad, pad)], dimension_numbers=('NWC', 'WIO', 'NWC'),
                                    feature_group_count=ch)


def diff_softmax_attend(q, k, v, lam):
    b, length = q.shape[0], q.shape[1]
    n_blk = length // Q_BLOCK
    qb = jnp.moveaxis(q.reshape(b, n_blk, Q_BLOCK, DA_HEADS, 2, DA_HEAD_DIM), 1, 0)

    def attend_block(q_blk):
        s = jnp.einsum('bqhmd,bkhmd->bhmqk', q_blk, k).astype(jnp.float32)
        p = jax.nn.softmax(s, axis=-1)
        a = p[:, :, 0] - lam * p[:, :, 1]
        return jnp.einsum('bhqk,bkhe->bqhe', a.astype(v.dtype), v)

    o = lax.map(attend_block, qb)
    return jnp.moveaxis(o, 0, 1).reshape(b, length, DA_HEADS, DA_V_DIM)


def diff_attn_branch(pc, px, lam_params, subln_g, lam_init, cos, sin, need_ctx):
    scale = DA_HEAD_DIM ** -0.5
    lp = lam_params.astype(jnp.float32)
    lam = jnp.exp(jnp.sum(lp[0] * lp[1])) - jnp.exp(jnp.sum(lp[2] * lp[3])) + lam_init

    def qk_heads(t):
        return t.reshape(t.shape[0], t.shape[1], DA_HEADS, 2, DA_HEAD_DIM)

    def v_heads(t):
        return t.reshape(t.shape[0], t.shape[1], DA_HEADS, DA_V_DIM)

    def finish(o):
        o = rms_norm(o, subln_g) * (1.0 - lam_init)
        return o.reshape(o.shape[0], o.shape[1], DA_OUT)

    q_x = apply_rope(qk_heads(px[0]), cos, sin) * scale
    k_x = apply_rope(qk_heads(px[1]), cos, sin)
    k_c = qk_heads(pc[1])
    v_c = v_heads(pc[2])
    k_all = jnp.concatenate([k_c, k_x], axis=1)
    v_all = jnp.concatenate([v_c, v_heads(px[2])], axis=1)
    out_x = finish(diff_softmax_attend(q_x, k_all, v_all, lam))
    out_c = finish(diff_softmax_attend(qk_heads(pc[0]) * scale, k_c, v_c, lam)) if need_ctx else None
    return out_c, out_x


def gated_delta_rule(q, k, v, g, beta, s0):
    out_dtype = v.dtype
    b, length, h, _ = q.shape
    dv = v.shape[-1]
    n_chunk = length // DN_CHUNK

    def chunk(t):
        t = t.astype(jnp.float32).reshape(b, n_chunk, DN_CHUNK, h, -1)
        return jnp.moveaxis(t, 3, 1)

    qc, kc, vc = chunk(q), chunk(k), chunk(v)
    gc = chunk(g[..., None])[..., 0]
    bc = chunk(beta[..., None])[..., 0]
    cum_g = jnp.cumsum(gc, axis=-1)
    pos = jnp.arange(DN_CHUNK)
    incl = pos[:, None] >= pos[None, :]
    strict = pos[:, None] > pos[None, :]
    decay = jnp.exp(jnp.where(incl, cum_g[..., :, None] - cum_g[..., None, :], -jnp.inf))
    kb = kc * bc[..., None]
    a_mat = jnp.where(strict, jnp.einsum('bhnid,bhnjd->bhnij', kb, kc) * decay, 0.0)
    eye = jnp.eye(DN_CHUNK, dtype=jnp.float32)
    t_mat = lax.linalg.triangular_solve(eye + a_mat, jnp.broadcast_to(eye, a_mat.shape),
                                        left_side=True, lower=True, unit_diagonal=True)
    u_val = jnp.einsum('bhnij,bhnjd->bhnid', t_mat, vc * bc[..., None])
    w_key = jnp.einsum('bhnij,bhnjd->bhnid', t_mat, kb * jnp.exp(cum_g)[..., None])
    qk = jnp.where(incl, jnp.einsum('bhnid,bhnjd->bhnij', qc, kc) * decay, 0.0)
    q_dec = qc * jnp.exp(cum_g)[..., None]
    k_dec = kc * jnp.exp(cum_g[..., -1:] - cum_g)[..., None]
    g_tot = jnp.exp(cum_g[..., -1])

    def step(state, xs):
        u_i, w_i, qk_i, q_i, k_i, g_i = xs
        v_new = u_i - jnp.einsum('bhid,bhde->bhie', w_i, state)
        o_i = jnp.einsum('bhid,bhde->bhie', q_i, state) + jnp.einsum('bhij,bhje->bhie', qk_i, v_new)
        state = state * g_i[..., None, None] + jnp.einsum('bhid,bhie->bhde', k_i, v_new)
        return state, o_i

    xs = tuple(jnp.moveaxis(t, 2, 0) for t in (u_val, w_key, qk, q_dec, k_dec, g_tot))
    s_fin, o = lax.scan(step, s0.astype(jnp.float32), xs)
    o = jnp.moveaxis(jnp.moveaxis(o, 0, 2), 1, 3).reshape(b, length, h, dv)
    return o.astype(out_dtype), s_fin


def deltanet_branch(pc, px, conv_w, a_log, dt_bias, norm_g, need_ctx):
    a_log = a_log.astype(jnp.float32)
    dt_bias = dt_bias.astype(jnp.float32)

    def prep(p):
        qkv = jax.nn.silu(depthwise_conv_centred(jnp.concatenate([p[3], p[4], p[5]], axis=-1), conv_w))
        q, k, v = jnp.split(qkv, 3, axis=-1)
        b, length = q.shape[0], q.shape[1]
        hd = lambda t: t.reshape(b, length, DN_HEADS, DN_HEAD_DIM)
        q = l2_normalize(hd(q)) * DN_HEAD_DIM ** -0.5
        k = l2_normalize(hd(k))
        g_f = -jnp.exp(a_log[0]) * jax.nn.softplus(p[7].astype(jnp.float32) + dt_bias[0])
        g_b = -jnp.exp(a_log[1]) * jax.nn.softplus(p[9].astype(jnp.float32) + dt_bias[1])
        beta_f = jax.nn.sigmoid(p[8].astype(jnp.float32))
        beta_b = jax.nn.sigmoid(p[10].astype(jnp.float32))
        return q, k, hd(v), p[6], g_f, beta_f, g_b, beta_b

    def gated_out(o, z):
        b, length = o.shape[0], o.shape[1]
        z = z.reshape(b, length, DN_HEADS, DN_HEAD_DIM)
        return (rms_norm(o, norm_g) * jax.nn.silu(z)).reshape(b, length, DN_WIDTH)

    flip = lambda t: t[:, ::-1]
    qc, kc, vc, zc, gfc, bfc, gbc, bbc = prep(pc)
    qx, kx, vx, zx, gfx, bfx, gbx, bbx = prep(px)
    zero = jnp.zeros((qx.shape[0], DN_HEADS, DN_HEAD_DIM, DN_HEAD_DIM), jnp.float32)
    oc_f, sc_f = gated_delta_rule(qc, kc, vc, gfc, bfc, zero)
    ox_f, _ = gated_delta_rule(qx, kx, vx, gfx, bfx, sc_f)
    oc_b, sc_b = gated_delta_rule(flip(qc), flip(kc), flip(vc), flip(gbc), flip(bbc), zero)
    ox_b, _ = gated_delta_rule(flip(qx), flip(kx), flip(vx), flip(gbx), flip(bbx), sc_b)
    out_x = gated_out(ox_f + flip(ox_b), zx)
    out_c = gated_out(oc_f + flip(oc_b), zc) if need_ctx else None
    return out_c, out_x


def hybrid_mixer(u_c, u_x, w_in, da_lambda, da_subln_g, dn_conv_w, dn_a_log, dn_dt_bias, dn_norm_g,
                 w_br_da, w_br_dn, w_o, lam_init, cos, sin, need_ctx):
    pc = split_proj(u_c @ w_in)
    px = split_proj(u_x @ w_in)
    da_c, da_x = diff_attn_branch(pc, px, da_lambda, da_subln_g, lam_init, cos, sin, need_ctx)
    dn_c, dn_x = deltanet_branch(pc, px, dn_conv_w, dn_a_log, dn_dt_bias, dn_norm_g, need_ctx)

    def merge(p, o_da, o_dn):
        y = jax.nn.sigmoid(p[11]) * (o_da @ w_br_da) + jax.nn.sigmoid(p[12]) * (o_dn @ w_br_dn)
        return y @ w_o

    y_x = merge(px, da_x, dn_x)
    y_c = merge(pc, da_c, dn_c) if need_ctx else None
    return y_c, y_x


def sq_relu_mlp(u, w1, w2):
    return jnp.square(jax.nn.relu(u @ w1)) @ w2


def setup_inputs(seed: int = 0) -> dict:
    key = jax.random.key(seed)
    ks = jax.random.split(key, 24)
    f32 = jnp.float32
    nrm = lambda k, shape, std: jax.random.normal(k, shape, f32) * std
    dt = jnp.exp(jax.random.uniform(ks[11], (DEPTH, 2, DN_HEADS), f32, math.log(1e-3), math.log(1e-1)))
    return {
        'x': nrm(ks[0], (BATCH, SEQ, D_MODEL), 1.0),
        'c': nrm(ks[1], (BATCH, D_MODEL), 1.0),
        'ctx': nrm(ks[2], (BATCH, CTX_LEN, D_MODEL), 1.0),
        'c_ctx': nrm(ks[3], (D_MODEL,), 1.0),
        'w_ada': nrm(ks[4], (DEPTH, D_MODEL, N_ADA * D_MODEL), 0.5 * D_MODEL ** -0.5),
        'b_ada': nrm(ks[5], (DEPTH, N_ADA * D_MODEL), 0.02),
        'w_in': nrm(ks[6], (DEPTH, D_MODEL, PROJ_DIM), D_MODEL ** -0.5),
        'da_lambda': nrm(ks[7], (DEPTH, 4, DA_HEAD_DIM), 0.1),
        'da_subln_g': 1.0 + nrm(ks[8], (DEPTH, DA_V_DIM), 0.02),
        'dn_conv_w': nrm(ks[9], (DEPTH, DN_CONV, 3 * DN_WIDTH), DN_CONV ** -0.5),
        'dn_a_log': jnp.log(jax.random.uniform(ks[10], (DEPTH, 2, DN_HEADS), f32, 1.0, 16.0)),
        'dn_dt_bias': dt + jnp.log(-jnp.expm1(-dt)),
        'dn_norm_g': 1.0 + nrm(ks[12], (DEPTH, DN_HEAD_DIM), 0.02),
        'w_br_da': nrm(ks[13], (DEPTH, DA_OUT, D_MODEL), DA_OUT ** -0.5),
        'w_br_dn': nrm(ks[14], (DEPTH, DN_WIDTH, D_MODEL), DN_WIDTH ** -0.5),
        'w_o': nrm(ks[15], (DEPTH, D_MODEL, D_MODEL), DEEPNORM_BETA * D_MODEL ** -0.5),
        'ln1_g': 1.0 + nrm(ks[16], (DEPTH, D_MODEL), 0.02),
        'ln1_b': nrm(ks[17], (DEPTH, D_MODEL), 0.02),
        'ln2_g': 1.0 + nrm(ks[18], (DEPTH, D_MODEL), 0.02),
        'ln2_b': nrm(ks[19], (DEPTH, D_MODEL), 0.02),
        'w_mlp1': nrm(ks[20], (DEPTH, D_MODEL, D_FF), D_MODEL ** -0.5),
        'w_mlp2': nrm(ks[21], (DEPTH, D_FF, D_MODEL), DEEPNORM_BETA * D_FF ** -0.5),
    }


def reference(x, c, ctx, c_ctx, w_ada, b_ada, w_in, da_lambda, da_subln_g, dn_conv_w, dn_a_log,
              dn_dt_bias, dn_norm_g, w_br_da, w_br_dn, w_o, ln1_g, ln1_b, ln2_g, ln2_b, w_mlp1, w_mlp2):
    cos, sin = axial_rope(x.shape[1], x.dtype)
    h_x = layer_norm(x)
    h_c = layer_norm(ctx)
    s_c = jax.nn.silu(c)
    s_cc = jax.nn.silu(c_ctx)
    for l in range(DEPTH):
        need_ctx = l < DEPTH - 1
        lam_init = 0.8 - 0.6 * math.exp(-0.3 * l)
        mx = jnp.split((s_c @ w_ada[l] + b_ada[l])[:, None, :], N_ADA, axis=-1)
        mc = jnp.split((s_cc @ w_ada[l] + b_ada[l])[None, None, :], N_ADA, axis=-1)
        y_c, y_x = hybrid_mixer(modulate(h_c, mc[0], mc[1]), modulate(h_x, mx[0], mx[1]), w_in[l],
                                da_lambda[l], da_subln_g[l], dn_conv_w[l], dn_a_log[l], dn_dt_bias[l],
                                dn_norm_g[l], w_br_da[l], w_br_dn[l], w_o[l], lam_init, cos, sin, need_ctx)
        h_x = layer_norm(DEEPNORM_ALPHA * h_x + mx[2] * y_x, ln1_g[l], ln1_b[l])
        h_x = layer_norm(DEEPNORM_ALPHA * h_x
                         + mx[5] * sq_relu_mlp(modulate(h_x, mx[3], mx[4]), w_mlp1[l], w_mlp2[l]),
                         ln2_g[l], ln2_b[l])
        if need_ctx:
            h_c = layer_norm(DEEPNORM_ALPHA * h_c + mc[2] * y_c, ln1_g[l], ln1_b[l])
            h_c = layer_norm(DEEPNORM_ALPHA * h_c
                             + mc[5] * sq_relu_mlp(modulate(h_c, mc[3], mc[4]), w_mlp1[l], w_mlp2[l]),
                             ln2_g[l], ln2_b[l])
    return h_x
```

```python
import contextlib
import math
import numpy as np
import concourse.bass as bass
import concourse.mybir as mybir
from concourse.bass_utils import run_bass_kernel_spmd

F32 = mybir.dt.float32
BF16 = mybir.dt.bfloat16
AF = mybir.ActivationFunctionType
ALU = mybir.AluOpType
AX = mybir.AxisListType

D = 1024
KC = 8
ENGS = ("tensor", "vector", "scalar", "gpsimd", "sync")
ST_ENG = "sync"

OQ, OQP, OK_, OKP, OV, ODQ, ODK, ODV, OZ, OG, OM1, OM2, NW = (
    0, 1024, 2048, 3072, 4096, 5120, 6144, 7168, 8192, 9216, 9248, 10272, 11296)


class Buf:
    __slots__ = ("name", "writers", "readers", "gen")

    def __init__(self, name):
        self.name = name
        self.writers = []
        self.readers = []
        self.gen = -1


class TL:
    __slots__ = ("ap", "b")

    def __init__(self, ap, name):
        self.ap = ap
        self.b = Buf(name)

    def __getitem__(self, k):
        return self.ap[k]


class Op:
    __slots__ = ("eng", "fn", "deps", "dma", "sem", "val", "needed", "idx")


_GEN = [0]


class Prog:
    def __init__(self, nc):
        self.nc = nc
        self.ops = {e: [] for e in ENGS}
        self.nops = 0
        self.dma_sems = {}
        _GEN[0] += 1
        self.gen = _GEN[0]

    def op(self, eng, fn, reads=(), writes=(), dma_key=None, partial=False):
        o = Op()
        o.eng = eng
        o.fn = fn
        o.dma = dma_key
        o.needed = False
        o.idx = self.nops
        self.nops += 1
        deps = []
        for t in list(reads) + list(writes):
            b = t.b
            if b.gen != self.gen:
                b.gen = self.gen
                b.writers = []
                b.readers = []
        for t in reads:
            deps.extend(t.b.writers)
        for t in writes:
            deps.extend(t.b.readers)
            if not partial:
                deps.extend(t.b.writers)
        o.deps = deps
        for d in deps:
            d.needed = True
        for t in reads:
            t.b.readers.append(o)
        for t in writes:
            if partial:
                t.b.writers.append(o)
            else:
                t.b.writers = [o]
                t.b.readers = []
        self.ops[eng].append(o)
        return o

    def dma(self, out_ap, in_ap, reads=(), writes=(), key=None, eng="sync", partial=False, **kw):
        if key is None:
            key = (writes[0].b.name if writes else reads[0].b.name)
        return self.op(eng, lambda e: e.dma_start(out=out_ap, in_=in_ap, **kw), reads, writes,
                       dma_key=key, partial=partial)

    def finalize(self, stack):
        nc = self.nc
        pool = SEMPOOL[id(nc)]
        allops = sorted((o for e in ENGS for o in self.ops[e]), key=lambda o: o.idx)
        dmap = {}
        for o in allops:
            if o.dma is not None:
                if o.dma not in dmap:
                    dmap[o.dma] = len(dmap)
                    if len(pool["d"]) < len(dmap):
                        pool["d"].append([pool["stack"].enter_context(nc.semaphore("dq%d" % len(pool["d"]))), 0])
                ent = pool["d"][dmap[o.dma]]
                ent[1] += 16
                o.sem, o.val = ent[0], ent[1]
            elif o.needed:
                ent = pool["e"][o.eng]
                ent[1] += 1
                o.sem, o.val = ent[0], ent[1]
            else:
                o.sem, o.val = None, 0
        used = [pool["d"][i] for i in range(len(dmap))]
        block = stack.enter_context(nc.Block())

        def make(ename):
            ops = self.ops[ename]

            def body(eng):
                seen = {}
                for o in ops:
                    waits = {}
                    for d in o.deps:
                        if d.sem is None:
                            continue
                        k = id(d.sem)
                        if d.val > seen.get(k, 0) and d.val > waits.get(k, (None, 0))[1]:
                            waits[k] = (d.sem, d.val)
                    for k, (s, v) in waits.items():
                        eng.wait_ge(s, v)
                        seen[k] = v
                    ins = o.fn(eng)
                    if o.dma is not None:
                        ins.then_inc(o.sem, 16)
                    elif o.needed:
                        ins.then_inc(o.sem, 1)
                if ename == "sync":
                    for (s, v) in used:
                        if v > 0:
                            eng.wait_ge(s, v)

            return body

        block.tensor(make("tensor"))
        block.vector(make("vector"))
        block.scalar(make("scalar"))
        block.gpsimd(make("gpsimd"))
        block.sync(make("sync"))


SEMPOOL = {}


def init_sempool(nc, stack):
    SEMPOOL[id(nc)] = {"stack": stack, "d": [],
                       "e": {e: [stack.enter_context(nc.semaphore("se_" + e)), 0] for e in ENGS}}


class Ring:
    def __init__(self, tiles):
        self.tiles = tiles
        self.i = 0

    def next(self):
        t = self.tiles[self.i % len(self.tiles)]
        self.i += 1
        return t


class Cfg:
    def __init__(self, S=8192, C=256, debug=False, phases="012345"):
        self.S, self.C, self.debug, self.phases = S, C, debug, phases
        self.T = S // 2
        self.NT = C + S
        self.NTP = self.NT + 9


def host_consts(cfg):
    c = {}
    c["ident"] = np.eye(128, dtype=np.float32)
    i = np.arange(64)[:, None]
    j = np.arange(64)[None, :]
    f = np.float32
    NEG = -30000.0
    mats = [
        np.eye(64),
        (i <= j), (i >= j),
        -(i <= j).astype(f), -(i >= j).astype(f),
        np.ones((64, 64)), -np.ones((64, 64)),
        np.where(i > j, 0.0, NEG),
        np.where(j >= i, 0.0, NEG),
        np.where(i < j, 0.0, NEG),
        np.where(j <= i, 0.0, NEG),
        np.zeros((64, 64)), np.zeros((64, 64)),
    ]
    c["dnc"] = np.ascontiguousarray(np.concatenate([np.asarray(m_, f) for m_ in mats], axis=1))
    return c


def rope_tables(cfg, h):
    S, C = cfg.S, cfg.C
    half = 32
    inv = (10000.0 ** (-np.arange(0, half, 2, dtype=np.float32) / half)).astype(np.float32)
    t = np.arange(S, dtype=np.float32)
    row = np.floor(t / 64.0).astype(np.float32)
    col = (t - row * 64.0).astype(np.float32)
    ang_r = row[:, None] * inv[None, :]
    ang_c = col[:, None] * inv[None, :]
    ang = np.concatenate([ang_r, ang_r, ang_c, ang_c], axis=-1).astype(np.float32)
    cos = np.cos(ang).astype(np.float32)
    sin = np.sin(ang).astype(np.float32)
    sign = np.repeat(np.array([-1.0, 1.0, -1.0, 1.0], np.float32), 16)
    sin = sin * sign[None, :]
    if h == 1:
        cos = cos[::-1]
        sin = sin[::-1]
    cos = np.concatenate([np.ones((C, 64), np.float32), cos], 0)
    sin = np.concatenate([np.zeros((C, 64), np.float32), sin], 0)
    cosT = np.ascontiguousarray(np.concatenate([cos.T, cos.T], 0))
    sinT = np.ascontiguousarray(np.concatenate([sin.T, sin.T], 0))
    return cosT, sinT


def relayout_win(w_in, h):
    cuts = np.cumsum([1024, 1024, 1024, 1024, 1024, 1024, 1024, 8, 8, 8, 8, 1024, 1024])[:-1]
    p = np.split(w_in, cuts, axis=1)

    def perm(w):
        w4 = w.reshape(w.shape[0], 16, 4, 16)
        return w4[:, :, [1, 0, 3, 2], :].reshape(w.shape[0], 1024)

    if h == 0:
        gates = [p[7], p[9], p[8], p[10]]
    else:
        gates = [p[9], p[7], p[10], p[8]]
    cols = [p[0], perm(p[0]), p[1], perm(p[1]), p[2], p[3], p[4], p[5], p[6]] + gates + [p[11], p[12]]
    return np.ascontiguousarray(np.concatenate(cols, axis=1))


def core_inputs(cfg, inputs, b, h):
    S, C = cfg.S, cfg.C
    x = inputs["x"][b]
    ctx = inputs["ctx"][b]
    if h == 1:
        x = x[::-1]
        ctx = ctx[::-1]
    m = {}
    m["xs"] = np.ascontiguousarray(np.concatenate([ctx, x], 0))
    m["cvec"] = np.ascontiguousarray(np.stack([inputs["c"][b], inputs["c_ctx"]], 0))
    m["w_ada"] = np.ascontiguousarray(inputs["w_ada"][0])
    m["b_ada"] = np.ascontiguousarray(inputs["b_ada"][0][None, :])
    m["win"] = relayout_win(inputs["w_in"][0], h)
    cw = inputs["dn_conv_w"][0]
    if h == 1:
        cw = cw[::-1]
    m["convw"] = np.ascontiguousarray(cw.T)
    al = inputs["dn_a_log"][0]
    dtb = inputs["dn_dt_bias"][0]
    if h == 1:
        al = al[::-1]
        dtb = dtb[::-1]
    m["alog"] = np.ascontiguousarray(al.reshape(1, 16))
    m["dtb"] = np.ascontiguousarray(dtb.reshape(1, 16))
    m["dnnorm"] = np.ascontiguousarray(inputs["dn_norm_g"][0])
    m["w_brda"] = np.ascontiguousarray(inputs["w_br_da"][0])
    m["w_brdn"] = np.ascontiguousarray(inputs["w_br_dn"][0])
    m["w_o"] = np.ascontiguousarray(inputs["w_o"][0])
    m["w_m1"] = np.ascontiguousarray(inputs["w_mlp1"][0])
    m["w_m2"] = np.ascontiguousarray(inputs["w_mlp2"][0])
    m["lnp"] = np.ascontiguousarray(np.stack([inputs["ln1_g"][0], inputs["ln1_b"][0], inputs["ln2_g"][0], inputs["ln2_b"][0]], 0))
    m["dalam"] = np.ascontiguousarray(inputs["da_lambda"][0].reshape(256))
    m["subln"] = np.ascontiguousarray(inputs["da_subln_g"][0])
    cosT, sinT = rope_tables(cfg, h)
    m["cosT"] = cosT
    m["sinT"] = sinT
    for k, v in host_consts(cfg).items():
        m[k] = v
    return m


def build(cfg):
    S, C, T, NT, NTP = cfg.S, cfg.C, cfg.T, cfg.NT, cfg.NTP
    dbg = cfg.debug
    nc = bass.Bass("TRN2", target_bir_lowering=False)

    def din(name, shape, dt=F32):
        return TL(nc.dram_tensor(name, list(shape), dt, kind="ExternalInput").ap(), name)

    def dscr(name, shape, dt):
        kind = "ExternalOutput" if dbg else "Internal"
        return TL(nc.dram_tensor(name, list(shape), dt, kind=kind).ap(), name)

    xs = din("xs", [NT, D])
    cvec = din("cvec", [2, D])
    w_ada = din("w_ada", [D, 6 * D])
    b_ada = din("b_ada", [1, 6 * D])
    win = din("win", [D, NW])
    convw = din("convw", [3072, 7])
    alog = din("alog", [1, 16])
    dtb = din("dtb", [1, 16])
    cosT = din("cosT", [128, NT])
    sinT = din("sinT", [128, NT])
    ident_d = din("ident", [128, 128])
    dalam = din("dalam", [256])
    subln = din("subln", [128])
    ODA_T = dscr("ODA_T", [D, T], BF16)
    ODN_T = dscr("ODN_T", [D, T], BF16)
    OA_s = dscr("OA_s", [T, D], F32)
    dnc = din("dnc", [64, 13 * 64])
    w_brda = din("w_brda", [D, D])
    w_brdn = din("w_brdn", [D, D])
    w_o = din("w_o", [D, D])
    w_m1 = din("w_m1", [D, 4 * D])
    w_m2 = din("w_m2", [4 * D, D])
    lnp = din("lnp", [4, D])
    H1_s = dscr("H1_s", [T, D], F32)
    U2T_s = dscr("U2T_s", [D, T], BF16)
    out_d = TL(nc.dram_tensor("out", [T, D], F32, kind="ExternalOutput").ap(), "out")
    dnnorm = din("dnnorm", [128])

    QT_da = dscr("QT_da", [8, 128, T], BF16)
    KT_da = dscr("KT_da", [8, 128, NT], BF16)
    V_da = dscr("V_da", [8, 128, NT // 128, 128], BF16)
    QT_dn = dscr("QT_dn", [8, 128, NT], BF16)
    KT_dn = dscr("KT_dn", [8, 128, NT], BF16)
    K_tok = dscr("K_tok", [8, NT, 128], BF16)
    V_tok = dscr("V_tok", [8, NT, 128], BF16)
    G_s = dscr("G_s", [NT, 32], F32)
    Z_s = dscr("Z_s", [T, D], F32)
    MG_s = dscr("MG_s", [2, D, T], F32)

    def pos(t):
        return 3 + t if t < C else 6 + t

    groups = [(0, C, False)] + [(C + 512 * g, 512, g < T // 512) for g in range(S // 512)]

    top = contextlib.ExitStack()
    with top:
        init_sempool(nc, top)
        uniq = [0]

        def sb(st, name, shape, dt=F32):
            uniq[0] += 1
            name = "%s_%d" % (name, uniq[0])
            return TL(st.enter_context(nc.sbuf_tensor(name, list(shape), dt)), name)

        def ps(st, name, shape, dt=F32):
            uniq[0] += 1
            name = "%s_%d" % (name, uniq[0])
            return TL(st.enter_context(nc.psum_tensor(name, list(shape), dt)), name)

        ident = sb(top, "ident_sb", [128, 128])
        identb = sb(top, "identb", [128, 128], BF16)
        modT = sb(top, "modT", [128, 48, 2])
        sc1 = sb(top, "sc1", [128, 8, 2])
        sc2 = sb(top, "sc2", [128, 8, 2])
        grow = sb(top, "grow", [128, 2, D])
        epsb = sb(top, "epsb", [128, 4])

        if "0" in cfg.phases:
            with contextlib.ExitStack() as st:
                P = Prog(nc)
                cv = sb(st, "cv", [2, D])
                i2 = sb(st, "i2", [2, 2])
                ones1 = sb(st, "ones1", [1, 128])
                bada = sb(st, "bada", [1, 6 * D])
                scT = sb(st, "scT", [128, 8, 2])
                scB = sb(st, "scB", [128, 8, 128])
                wst = Ring([sb(st, "wst%d" % i, [128, 8, 512]) for i in range(2)])
                pcv = ps(st, "pcv", [128, 8, 2])
                pmod = ps(st, "pmod", [128, 48, 2])
                pbias = ps(st, "pbias", [128, 48])
                prow = Ring([ps(st, "prow%d" % i, [128, 512]) for i in range(2)])
                brow = sb(st, "brow", [128, 2, D])
                P.dma(ident.ap[:], ident_d.ap, writes=[ident])
                P.dma(cv.ap[:], cvec.ap, writes=[cv])
                P.dma(bada.ap[:], b_ada.ap, writes=[bada])
                P.dma(brow.ap[:, 0, :], b_ada.ap[0, 2 * D:3 * D].partition_broadcast(128), writes=[brow], partial=True, key="brow")
                P.dma(brow.ap[:, 1, :], b_ada.ap[0, 5 * D:6 * D].partition_broadcast(128), writes=[brow], partial=True, key="brow")
                P.op("gpsimd", lambda e: e.tensor_copy(out=identb.ap[:], in_=ident.ap[:]), [ident], [identb])
                P.op("vector", lambda e: e.tensor_copy(out=i2.ap[:], in_=ident.ap[0:2, 0:2]), [ident], [i2])
                P.op("vector", lambda e: e.memset(ones1.ap[:], 1.0), [], [ones1])
                for ci_, v_ in enumerate((1e-5, 1e-6, 128e-6, 1.0)):
                    P.op("vector", lambda e, ci_=ci_, v_=v_: e.memset(epsb.ap[:, ci_:ci_ + 1], v_), [], [epsb], partial=True)
                for k in range(KC):
                    P.op("tensor", lambda e, k=k: e.matmul(pcv.ap[:, k, :], lhsT=cv.ap[0:2, k * 128:(k + 1) * 128],
                                                           rhs=i2.ap[:], start=True, stop=True),
                         [cv, i2], [pcv], partial=True)
                for j in range(48):
                    P.op("tensor", lambda e, j=j: e.matmul(pbias.ap[:, j:j + 1], lhsT=bada.ap[0:1, j * 128:(j + 1) * 128],
                                                           rhs=ones1.ap[0:1, 0:1], start=True, stop=True),
                         [bada, ones1], [pbias], partial=True)
                P.op("scalar", lambda e: e.activation(out=scT.ap[:], in_=pcv.ap[:], func=AF.Silu), [pcv], [scT])
                P.op("vector", lambda e: e.tensor_copy(out=scB.ap[:], in_=scT.ap[:, :, 0:1].to_broadcast([128, 8, 128])),
                     [scT], [scB])
                for g in range(12):
                    w = wst.next()
                    P.dma(w.ap[:], w_ada.ap[:, g * 512:(g + 1) * 512].rearrange("(k p) c -> p k c", p=128), writes=[w])
                    for j4 in range(4):
                        j = g * 4 + j4
                        for k in range(KC):
                            P.op("tensor", lambda e, w=w, j=j, j4=j4, k=k: e.matmul(
                                pmod.ap[:, j, :], lhsT=w.ap[:, k, j4 * 128:(j4 + 1) * 128], rhs=scT.ap[:, k, :],
                                start=(k == 0), stop=(k == KC - 1)), [w, scT], [pmod], partial=True)
                    if g in (4, 5, 10, 11):
                        pr = prow.next()
                        for k in range(KC):
                            P.op("tensor", lambda e, w=w, k=k, pr=pr: e.matmul(
                                pr.ap[:], lhsT=scB.ap[:, k, :], rhs=w.ap[:, k, :], start=(k == 0), stop=(k == KC - 1)),
                                 [w, scB], [pr], partial=(k > 0))
                        gi, hf = (0, g - 4) if g < 6 else (1, g - 10)
                        P.op("vector", lambda e, pr=pr, gi=gi, hf=hf: e.tensor_tensor(
                            out=grow.ap[:, gi, hf * 512:(hf + 1) * 512], in0=pr.ap[:],
                            in1=brow.ap[:, gi, hf * 512:(hf + 1) * 512], op=ALU.add), [pr, brow], [grow], partial=True)
                P.op("vector", lambda e: e.tensor_copy(out=modT.ap[:], in_=pmod.ap[:]), [pmod], [modT])
                P.op("vector", lambda e: e.tensor_tensor(out=modT.ap[:], in0=modT.ap[:],
                                                         in1=pbias.ap[:].unsqueeze(2).to_broadcast([128, 48, 2]), op=ALU.add),
                     [modT, pbias], [modT])
                P.op("vector", lambda e: e.tensor_scalar_add(out=sc1.ap[:], in0=modT.ap[:, 8:16, :], scalar1=1.0),
                     [modT], [sc1])
                P.op("vector", lambda e: e.tensor_scalar_add(out=sc2.ap[:], in0=modT.ap[:, 32:40, :], scalar1=1.0),
                     [modT], [sc2])
                if dbg:
                    d_mod = dscr("d_mod", [128, 48, 2], F32)
                    d_grow = dscr("d_grow", [128, 2, D], F32)
                    P.dma(d_mod.ap, modT.ap[:], reads=[modT], writes=[d_mod])
                    P.dma(d_grow.ap, grow.ap[:], reads=[grow], writes=[d_grow])
                P.finalize(st)

        ph1 = contextlib.ExitStack()
        with ph1:
            uT = sb(ph1, "uT", [128, KC, NTP], BF16)
            if "1" in cfg.phases:
                with contextlib.ExitStack() as st:
                    P = Prog(nc)
                    xt = Ring([sb(st, "xt%d" % i, [128, D]) for i in range(3)])
                    xn = Ring([sb(st, "xn%d" % i, [128, D], BF16) for i in range(2)])
                    stt = Ring([sb(st, "stt%d" % i, [128, 2, 6]) for i in range(2)])
                    mv = Ring([sb(st, "mv%d" % i, [128, 2]) for i in range(2)])
                    rs = Ring([sb(st, "rs%d" % i, [128, 2]) for i in range(2)])
                    ptr = Ring([ps(st, "ptr%d" % i, [128, KC, 128], BF16) for i in range(3)])
                    for (c0, n) in ((0, 3), (3 + C, 3), (6 + NT, 3)):
                        P.op("gpsimd", lambda e, c0=c0, n=n: e.memset(uT.ap[:, :, c0:c0 + n], 0.0), [], [uT], partial=True)
                    for i in range(NT // 128):
                        t0 = 128 * i
                        s = 1 if t0 < C else 0
                        x_ = xt.next(); xn_ = xn.next(); st_ = stt.next(); mv_ = mv.next(); rs_ = rs.next(); pt = ptr.next()
                        P.dma(x_.ap[:], xs.ap[t0:t0 + 128, :], writes=[x_])
                        for hh in range(2):
                            P.op("vector", lambda e, x_=x_, st_=st_, hh=hh: e.bn_stats(
                                out=st_.ap[:, hh, :], in_=x_.ap[:, hh * 512:(hh + 1) * 512]), [x_], [st_], partial=(hh > 0))
                        P.op("vector", lambda e, st_=st_, mv_=mv_: e.bn_aggr(out=mv_.ap[:], in_=st_.ap[:].rearrange("p a b -> p (a b)")), [st_], [mv_])
                        P.op("scalar", lambda e, mv_=mv_, rs_=rs_: e.activation(
                            out=rs_.ap[:, 0:1], in_=mv_.ap[:, 1:2], func=AF.Ln, bias=epsb.ap[:, 0:1]), [mv_, epsb], [rs_])
                        P.op("scalar", lambda e, rs_=rs_: e.activation(
                            out=rs_.ap[:, 0:1], in_=rs_.ap[:, 0:1], func=AF.Exp, scale=-0.5), [rs_], [rs_])
                        P.op("vector", lambda e, mv_=mv_, rs_=rs_: e.tensor_scalar(
                            out=rs_.ap[:, 1:2], in0=mv_.ap[:, 0:1], scalar1=-1.0, scalar2=rs_.ap[:, 0:1],
                            op0=ALU.mult, op1=ALU.mult), [mv_, rs_], [rs_], partial=True)
                        P.op("scalar", lambda e, x_=x_, xn_=xn_, rs_=rs_: e.activation(
                            out=xn_.ap[:], in_=x_.ap[:], func=AF.Identity, scale=rs_.ap[:, 0:1], bias=rs_.ap[:, 1:2]),
                             [x_, rs_], [xn_])
                        for k in range(KC):
                            P.op("tensor", lambda e, xn_=xn_, pt=pt, k=k: e.transpose(
                                pt.ap[:, k, :], xn_.ap[:, k * 128:(k + 1) * 128], identb.ap[:]),
                                 [xn_, identb], [pt], partial=(k > 0))
                        col = pos(t0)
                        for k in range(KC):
                            P.op("vector", lambda e, pt=pt, k=k, col=col, s=s: e.tensor_scalar(
                                out=uT.ap[:, k, col:col + 128], in0=pt.ap[:, k, :], scalar1=sc1.ap[:, k, s:s + 1],
                                scalar2=modT.ap[:, k, s:s + 1], op0=ALU.mult, op1=ALU.add),
                                 [pt, sc1, modT], [uT], partial=True)
                    if dbg:
                        d_uT = dscr("d_uT", [128, KC, NTP], BF16)
                        P.dma(d_uT.ap, uT.ap[:], reads=[uT], writes=[d_uT])
                    P.finalize(st)

            if "2" in cfg.phases:
                with contextlib.ExitStack() as st:
                    P = Prog(nc)
                    phase1b(nc, cfg, P, st, sb, ps, locals())
                    P.finalize(st)
        if "3" in cfg.phases:
            with contextlib.ExitStack() as st:
                P = Prog(nc)
                phase2(nc, cfg, P, st, sb, ps, locals())
                P.finalize(st)
        if "4" in cfg.phases:
            with contextlib.ExitStack() as st:
                P = Prog(nc)
                phase3(nc, cfg, P, st, sb, ps, locals())
                P.finalize(st)
        if "5" in cfg.phases:
            with contextlib.ExitStack() as st:
                P = Prog(nc)
                phase4a(nc, cfg, P, st, sb, ps, locals())
                P.finalize(st)
            with contextlib.ExitStack() as st:
                P = Prog(nc)
                phase4b(nc, cfg, P, st, sb, ps, locals())
                P.finalize(st)
    return nc


ALPHA = 2.0 ** 0.25


def ln_tile(P, src, dst, stt, mv, rs, epsb, gb=None, tmp=None):
    for hh in range(2):
        P.op("vector", lambda e, hh=hh: e.bn_stats(out=stt.ap[:, hh, :], in_=src.ap[:, hh * 512:(hh + 1) * 512]), [src], [stt],
             partial=(hh > 0))
    P.op("vector", lambda e: e.bn_aggr(out=mv.ap[:], in_=stt.ap[:].rearrange("p a b -> p (a b)")), [stt], [mv])
    P.op("scalar", lambda e: e.activation(out=rs.ap[:, 0:1], in_=mv.ap[:, 1:2], func=AF.Ln, bias=epsb.ap[:, 0:1]), [mv, epsb], [rs])
    P.op("scalar", lambda e: e.activation(out=rs.ap[:, 0:1], in_=rs.ap[:, 0:1], func=AF.Exp, scale=-0.5), [rs], [rs])
    P.op("vector", lambda e: e.tensor_scalar(out=rs.ap[:, 1:2], in0=mv.ap[:, 0:1], scalar1=-1.0, scalar2=rs.ap[:, 0:1],
                                             op0=ALU.mult, op1=ALU.mult), [mv, rs], [rs], partial=True)
    if gb is None:
        P.op("scalar", lambda e: e.activation(out=dst.ap[:], in_=src.ap[:], func=AF.Identity, scale=rs.ap[:, 0:1], bias=rs.ap[:, 1:2]),
             [src, rs], [dst])
    else:
        P.op("scalar", lambda e: e.activation(out=tmp.ap[:], in_=src.ap[:], func=AF.Identity, scale=rs.ap[:, 0:1], bias=rs.ap[:, 1:2]),
             [src, rs], [tmp])
        P.op("gpsimd", lambda e: e.tensor_tensor(out=tmp.ap[:], in0=tmp.ap[:], in1=gb.ap[:, 0, :], op=ALU.mult), [tmp, gb], [tmp])
        P.op("vector", lambda e: e.tensor_tensor(out=dst.ap[:], in0=tmp.ap[:], in1=gb.ap[:, 1, :], op=ALU.add), [tmp, gb], [dst])


def load_weight_rows(P, wst, src, dst_of, nblk, rows_of):
    for i in range(nblk):
        w = wst.next()
        P.dma(w.ap[:], rows_of(i), writes=[w])
        dst, dap = dst_of(i)
        P.op("gpsimd", lambda e, w=w, dap=dap: e.tensor_copy(out=dap, in_=w.ap[:]), [w], [dst], partial=True)


def phase4a(nc, cfg, P, st, sb, ps, env):
    S, C, T, NT = cfg.S, cfg.C, cfg.T, cfg.NT
    xs, ODA_T, ODN_T, MG_s, H1_s, U2T_s, w_brda, w_brdn, w_o, lnp, grow, sc2, modT, epsb, identb = (env[k] for k in (
        "xs", "ODA_T", "ODN_T", "MG_s", "H1_s", "U2T_s", "w_brda", "w_brdn", "w_o", "lnp", "grow", "sc2", "modT", "epsb", "identb"))
    wst = Ring([sb(st, "wst%d" % i, [128, D]) for i in range(3)])
    wda = sb(st, "wda", [128, KC, D], BF16); wdn = sb(st, "wdn", [128, KC, D], BF16); wo = sb(st, "wo", [128, KC, D], BF16)
    for (src, dst) in ((w_brda, wda), (w_brdn, wdn), (w_o, wo)):
        load_weight_rows(P, wst, src, lambda i, dst=dst: (dst, dst.ap[:, i, :]), KC, lambda i, src=src: src.ap[i * 128:(i + 1) * 128, :])
    ln1 = sb(st, "ln1", [128, 2, D])
    P.dma(ln1.ap[:, 0, :], lnp.ap[0, :].partition_broadcast(128), writes=[ln1], partial=True, key="ln1")
    P.dma(ln1.ap[:, 1, :], lnp.ap[1, :].partition_broadcast(128), writes=[ln1], partial=True, key="ln1")
    oda = Ring([sb(st, "oda%d" % i, [128, KC, 512], BF16) for i in range(2)])
    odn = Ring([sb(st, "odn%d" % i, [128, KC, 512], BF16) for i in range(2)])
    mg = Ring([sb(st, "mg%d" % i, [128, 2, 512]) for i in range(2)])
    t1 = Ring([sb(st, "t1_%d" % i, [128, 512]) for i in range(2)])
    t2 = Ring([sb(st, "t2_%d" % i, [128, 512]) for i in range(2)])
    yT = Ring([sb(st, "yT%d" % i, [128, KC, 512], BF16) for i in range(2)])
    xt = Ring([sb(st, "xt%d" % i, [128, D]) for i in range(2)])
    hx = Ring([sb(st, "hx%d" % i, [128, D]) for i in range(1)])
    vv = Ring([sb(st, "vv%d" % i, [128, D]) for i in range(2)])
    h1 = Ring([sb(st, "h1_%d" % i, [128, D]) for i in range(2)])
    h1b = Ring([sb(st, "h1b%d" % i, [128, D], BF16) for i in range(1)])
    tmp = Ring([sb(st, "tmp%d" % i, [128, D]) for i in range(1)])
    u2 = Ring([sb(st, "u2_%d" % i, [128, KC, 128], BF16) for i in range(2)])
    stt = Ring([sb(st, "stt%d" % i, [128, 2, 6]) for i in range(4)])
    mv = Ring([sb(st, "mv%d" % i, [128, 2]) for i in range(4)])
    rs = Ring([sb(st, "rs%d" % i, [128, 2]) for i in range(4)])
    PDA = Ring([ps(st, "PDA%d" % i, [128, 512]) for i in range(2)])
    PDN = Ring([ps(st, "PDN%d" % i, [128, 512]) for i in range(2)])
    PY = Ring([ps(st, "PY%d" % i, [128, 1024]) for i in range(1)])
    PT = Ring([ps(st, "PT%d" % i, [128, KC, 128], BF16) for i in range(2)])

    for g in range(T // 512):
        t0 = g * 512
        a_ = oda.next(); n_ = odn.next(); y_ = yT.next()
        P.dma(a_.ap[:], ODA_T.ap[:, t0:t0 + 512].rearrange("(k p) t -> p k t", p=128), writes=[a_])
        P.dma(n_.ap[:], ODN_T.ap[:, t0:t0 + 512].rearrange("(k p) t -> p k t", p=128), writes=[n_])
        for c in range(KC):
            pda = PDA.next(); pdn = PDN.next(); m_ = mg.next(); a1 = t1.next(); a2 = t2.next()
            P.dma(m_.ap[:], MG_s.ap[:, c * 128:(c + 1) * 128, t0:t0 + 512].rearrange("g p t -> p g t"), writes=[m_])
            for k in range(KC):
                P.op("tensor", lambda e, k=k, c=c, pda=pda, a_=a_: e.matmul(pda.ap[:], lhsT=wda.ap[:, k, c * 128:(c + 1) * 128],
                                                                           rhs=a_.ap[:, k, :], start=(k == 0), stop=(k == KC - 1)),
                     [wda, a_], [pda], partial=(k > 0))
            for k in range(KC):
                P.op("tensor", lambda e, k=k, c=c, pdn=pdn, n_=n_: e.matmul(pdn.ap[:], lhsT=wdn.ap[:, k, c * 128:(c + 1) * 128],
                                                                           rhs=n_.ap[:, k, :], start=(k == 0), stop=(k == KC - 1)),
                     [wdn, n_], [pdn], partial=(k > 0))
            P.op("vector", lambda e, pda=pda, m_=m_, a1=a1: e.tensor_tensor(out=a1.ap[:], in0=pda.ap[:], in1=m_.ap[:, 0, :], op=ALU.mult),
                 [pda, m_], [a1])
            P.op("vector", lambda e, pdn=pdn, m_=m_, a2=a2: e.tensor_tensor(out=a2.ap[:], in0=pdn.ap[:], in1=m_.ap[:, 1, :], op=ALU.mult),
                 [pdn, m_], [a2])
            P.op("gpsimd", lambda e, a1=a1, a2=a2, y_=y_, c=c: e.tensor_tensor(out=y_.ap[:, c, :], in0=a1.ap[:], in1=a2.ap[:], op=ALU.add),
                 [a1, a2], [y_], partial=True)
        for j in range(4):
            tk = t0 + 128 * j
            py = PY.next(); x_ = xt.next(); hx_ = hx.next(); v_ = vv.next(); h1_ = h1.next(); hb_ = h1b.next(); tm_ = tmp.next()
            u2_ = u2.next(); pt = PT.next()
            for hf in range(2):
                for k in range(KC):
                    P.op("tensor", lambda e, k=k, hf=hf, py=py, y_=y_, j=j: e.matmul(
                        py.ap[:, hf * 512:(hf + 1) * 512], lhsT=y_.ap[:, k, 128 * j:128 * j + 128], rhs=wo.ap[:, k, hf * 512:(hf + 1) * 512],
                        start=(k == 0), stop=(k == KC - 1)), [y_, wo], [py], partial=not (hf == 0 and k == 0))
            P.dma(x_.ap[:], xs.ap[C + tk:C + tk + 128, :], writes=[x_])
            ln_tile(P, x_, hx_, stt.next(), mv.next(), rs.next(), epsb)
            P.op("vector", lambda e, py=py, v_=v_: e.tensor_tensor(out=v_.ap[:], in0=py.ap[:], in1=grow.ap[:, 0, :], op=ALU.mult),
                 [py, grow], [v_])
            P.op("vector", lambda e, hx_=hx_, v_=v_: e.scalar_tensor_tensor(out=v_.ap[:], in0=hx_.ap[:], scalar=ALPHA, in1=v_.ap[:],
                                                                          op0=ALU.mult, op1=ALU.add), [hx_, v_], [v_])
            ln_tile(P, v_, h1_, stt.next(), mv.next(), rs.next(), epsb, gb=ln1, tmp=tm_)
            P.dma(H1_s.ap[tk:tk + 128, :], h1_.ap[:], reads=[h1_], writes=[H1_s], partial=True, key=h1_.b.name + "s", eng=ST_ENG)
            P.op("scalar", lambda e, h1_=h1_, hb_=hb_: e.copy(out=hb_.ap[:], in_=h1_.ap[:]), [h1_], [hb_])
            for k in range(KC):
                P.op("tensor", lambda e, k=k, hb_=hb_, pt=pt: e.transpose(pt.ap[:, k, :], hb_.ap[:, k * 128:(k + 1) * 128], identb.ap[:]),
                     [hb_, identb], [pt], partial=(k > 0))
            for k in range(KC):
                P.op("vector", lambda e, k=k, pt=pt, u2_=u2_: e.tensor_scalar(
                    out=u2_.ap[:, k, :], in0=pt.ap[:, k, :], scalar1=sc2.ap[:, k, 0:1], scalar2=modT.ap[:, 24 + k, 0:1],
                    op0=ALU.mult, op1=ALU.add), [pt, sc2, modT], [u2_], partial=(k > 0))
            P.dma(U2T_s.ap[:, tk:tk + 128].rearrange("(k p) t -> p k t", p=128), u2_.ap[:], reads=[u2_], writes=[U2T_s], partial=True,
                  key=u2_.b.name + "s", eng=ST_ENG)


def phase4b(nc, cfg, P, st, sb, ps, env):
    S, C, T, NT = cfg.S, cfg.C, cfg.T, cfg.NT
    H1_s, U2T_s, w_m1, w_m2, lnp, grow, epsb, out_d = (env[k] for k in (
        "H1_s", "U2T_s", "w_m1", "w_m2", "lnp", "grow", "epsb", "out_d"))
    TG = 256
    wst = Ring([sb(st, "wst%d" % i, [128, D]) for i in range(2)])
    W1 = sb(st, "W1", [128, KC, 4 * D], BF16)
    W2 = sb(st, "W2", [128, 32, D], BF16)
    load_weight_rows(P, wst, w_m1, lambda i: (W1, W1.ap[:, i // 4, (i % 4) * D:(i % 4 + 1) * D]), 32,
                     lambda i: w_m1.ap[(i // 4) * 128:(i // 4 + 1) * 128, (i % 4) * D:(i % 4 + 1) * D])
    load_weight_rows(P, wst, w_m2, lambda i: (W2, W2.ap[:, i, :]), 32, lambda i: w_m2.ap[i * 128:(i + 1) * 128, :])
    ln2 = sb(st, "ln2", [128, 2, D])
    P.dma(ln2.ap[:, 0, :], lnp.ap[2, :].partition_broadcast(128), writes=[ln2], partial=True, key="ln2")
    P.dma(ln2.ap[:, 1, :], lnp.ap[3, :].partition_broadcast(128), writes=[ln2], partial=True, key="ln2")
    u2 = Ring([sb(st, "u2g%d" % i, [128, KC, TG], BF16) for i in range(2)])
    hT = sb(st, "hT", [128, 32, TG], BF16)
    rl = Ring([sb(st, "rl%d" % i, [128, TG], BF16) for i in range(3)])
    h1 = Ring([sb(st, "h1_%d" % i, [128, D]) for i in range(1)])
    vv = Ring([sb(st, "vv%d" % i, [128, D]) for i in range(1)])
    oo = Ring([sb(st, "oo%d" % i, [128, D]) for i in range(2)])
    tmp = Ring([sb(st, "tmp%d" % i, [128, D]) for i in range(1)])
    stt = Ring([sb(st, "stt%d" % i, [128, 2, 6]) for i in range(2)])
    mv = Ring([sb(st, "mv%d" % i, [128, 2]) for i in range(2)])
    rs = Ring([sb(st, "rs%d" % i, [128, 2]) for i in range(2)])
    PH = Ring([ps(st, "PH%d" % i, [128, 512]) for i in range(4)])
    PO = Ring([ps(st, "PO%d" % i, [128, 1024]) for i in range(2)])
    for g in range(T // TG):
        t0 = g * TG
        u_ = u2.next()
        P.dma(u_.ap[:], U2T_s.ap[:, t0:t0 + TG].rearrange("(k p) t -> p k t", p=128), writes=[u_])
        for fc in range(32):
            ph = PH.next(); r_ = rl.next()
            for k in range(KC):
                P.op("tensor", lambda e, k=k, fc=fc, ph=ph, u_=u_: e.matmul(ph.ap[:, 0:TG], lhsT=W1.ap[:, k, fc * 128:(fc + 1) * 128],
                                                                           rhs=u_.ap[:, k, :], start=(k == 0), stop=(k == KC - 1)),
                     [W1, u_], [ph], partial=(k > 0))
            P.op("scalar", lambda e, ph=ph, r_=r_: e.activation(out=r_.ap[:], in_=ph.ap[:, 0:TG], func=AF.Relu), [ph], [r_])
            P.op("gpsimd", lambda e, r_=r_, fc=fc: e.tensor_tensor(out=hT.ap[:, fc, :], in0=r_.ap[:], in1=r_.ap[:], op=ALU.mult),
                 [r_], [hT], partial=True)
        for j in range(TG // 128):
            tk = t0 + 128 * j
            po = PO.next(); h_ = h1.next(); v_ = vv.next(); o_ = oo.next(); tm_ = tmp.next()
            for hf in range(2):
                for fc in range(32):
                    P.op("tensor", lambda e, fc=fc, hf=hf, po=po, j=j: e.matmul(
                        po.ap[:, hf * 512:(hf + 1) * 512], lhsT=hT.ap[:, fc, 128 * j:128 * j + 128], rhs=W2.ap[:, fc, hf * 512:(hf + 1) * 512],
                        start=(fc == 0), stop=(fc == 31)), [hT, W2], [po], partial=not (hf == 0 and fc == 0))
            P.dma(h_.ap[:], H1_s.ap[tk:tk + 128, :], writes=[h_])
            P.op("vector", lambda e, po=po, v_=v_: e.tensor_tensor(out=v_.ap[:], in0=po.ap[:], in1=grow.ap[:, 1, :], op=ALU.mult),
                 [po, grow], [v_])
            P.op("vector", lambda e, h_=h_, v_=v_: e.scalar_tensor_tensor(out=v_.ap[:], in0=h_.ap[:], scalar=ALPHA, in1=v_.ap[:],
                                                                        op0=ALU.mult, op1=ALU.add), [h_, v_], [v_])
            ln_tile(P, v_, o_, stt.next(), mv.next(), rs.next(), epsb, gb=ln2, tmp=tm_)
            P.dma(out_d.ap[tk:tk + 128, :], o_.ap[:], reads=[o_], writes=[out_d], partial=True, key=o_.b.name + "s", eng=ST_ENG)


def phase3(nc, cfg, P, st, sb, ps, env):
    S, C, T, NT = cfg.S, cfg.C, cfg.T, cfg.NT
    QT_dn, KT_dn, K_tok, V_tok, G_s, Z_s, OA_s, ODN_T, dnc, dnnorm, identb, epsb = (env[k] for k in (
        "QT_dn", "KT_dn", "K_tok", "V_tok", "G_s", "Z_s", "OA_s", "ODN_T", "dnc", "dnnorm", "identb", "epsb"))
    NCH = NT // 64
    CH_CTX = C // 64
    CH_OWN_END = (C + T) // 64

    cst = sb(st, "cst", [64, 13 * 64])
    cm = lambda i: cst.ap[:, 64 * i:64 * (i + 1)]
    negm = sb(st, "negm", [64, 4, 8, 64])
    ones_b = sb(st, "ones_b", [64, 128], BF16)
    ones_f = sb(st, "ones_f", [64, 128])
    gnorm = sb(st, "gnorm", [128, 128])
    P.dma(cst.ap[:], dnc.ap, writes=[cst])
    P.dma(gnorm.ap[:], dnnorm.ap.partition_broadcast(128), writes=[gnorm])
    for i in range(4):
        P.op("vector", lambda e, i=i: e.tensor_copy(out=negm.ap[:, i], in_=cm(7 + i).unsqueeze(1).to_broadcast([64, 8, 64])),
             [cst], [negm], partial=True)
    P.op("vector", lambda e: e.memset(ones_b.ap[:], 1.0), [], [ones_b])
    P.op("vector", lambda e: e.memset(ones_f.ap[:], 1.0), [], [ones_f])

    Wt = [ps(st, "W%d" % i, [128, 1024]) for i in range(3)]
    Wlo = [TL(w.ap[:, 0:512], w.b.name + "lo") for w in Wt]
    Whi = [TL(w.ap[:, 512:1024], w.b.name + "hi") for w in Wt]
    N0 = ps(st, "N0", [128, 512])
    BT = ps(st, "BT", [128, 1024], BF16)

    _kr = Ring([sb(st, "kblk%d" % i, [128, 8, 512], BF16) for i in range(2)])
    _qr = Ring([sb(st, "qblk%d" % i, [128, 8, 512], BF16) for i in range(2)])
    kblk = {0: _kr, 1: _kr}
    qblk = {0: _qr, 1: _qr}
    oblk = Ring([sb(st, "oblk%d" % i, [128, 8, 512], BF16) for i in range(2)])
    NSET = 2

    def ring(name, shape, dt=F32, n=NSET):
        return Ring([sb(st, "%s%d" % (name, i), shape, dt) for i in range(n)])

    r_ktok = ring("ktok", [64, 8, 128], BF16); r_vtok = ring("vtok", [64, 8, 128], BF16); r_g = ring("g", [64, 32])
    r_G1 = ring("G1", [64, 8, 64]); r_G2 = ring("G2", [64, 8, 64])
    r_Dl = ring("Dl", [64, 8, 64]); r_Dq = ring("Dq", [64, 8, 64])
    r_sm = ring("sm", [64, 48]); r_gt = ring("gt", [128, 8])
    r_Ab = ring("Ab", [64, 8, 64]); r_A = ring("A", [64, 8, 64], BF16)
    r_X = ring("X", [64, 8, 64], BF16, 4); r_Y = ring("Y", [64, 8, 64], BF16, 4); r_R = ring("R", [64, 8, 64], BF16)
    r_vb = ring("vb", [64, 8, 128], BF16); r_kbe = ring("kbe", [64, 8, 128], BF16); r_kdec = ring("kdec", [64, 8, 128], BF16)
    r_qkT = ring("qkT", [64, 8, 64], BF16); r_rhsE = ring("rhsE", [64, 8, 64], BF16)
    r_u = ring("u", [64, 8, 128]); r_wT = ring("wT", [128, 8, 64], BF16); r_qdT = ring("qdT", [128, 8, 64], BF16)
    r_vnew = ring("vnew", [64, 8, 128], BF16)
    r_osb = ring("osb", [64, 8, 128]); r_oa = ring("oa", [64, 8, 128]); r_z = ring("z", [64, 8, 128])
    r_sq = ring("sq", [64, 8, 128]); r_ss = ring("ss", [64, 16]); r_on = ring("on", [64, 8, 128], BF16)
    Sf = [sb(st, "Sf%d" % d, [128, 8, 128]) for d in range(2)]
    Sb = [sb(st, "Sb%d" % d, [128, 8, 128], BF16) for d in range(2)]
    for d in range(2):
        P.op("gpsimd", lambda e, d=d: e.memset(Sf[d].ap[:], 0.0), [], [Sf[d]])
        P.op("gpsimd", lambda e, d=d: e.memset(Sb[d].ap[:], 0.0), [], [Sb[d]])

    blkstate = {}

    def get_blk(kind, d, ci):
        bi = ci // 8
        key = (kind, d)
        if blkstate.get(key, (None, None))[0] != bi:
            t = (kblk if kind == "k" else qblk)[d].next()
            src = KT_dn if kind == "k" else QT_dn
            t0 = bi * 512
            lo = t0 if kind == "k" else max(t0, C)
            hi = min(t0 + 512, NT) if kind == "k" else min(t0 + 512, C + T)
            P.dma(t.ap[:, :, lo - t0:hi - t0], src.ap[:, :, lo:hi].rearrange("h d t -> d h t"), writes=[t])
            blkstate[key] = (bi, t)
        return blkstate[key][1], (ci % 8) * 64

    def bcl(ap2, n):
        return ap2.unsqueeze(2).to_broadcast([ap2.shape[0], 8, n])

    def prep(ci, d, want_out):
        tok0 = ci * 64
        kb, ko = get_blk("k", d, ci)
        kT = lambda h: kb.ap[:, h, ko:ko + 64]
        if want_out:
            qb, qo = get_blk("q", d, ci)
        ktok = r_ktok.next(); vtok = r_vtok.next(); g = r_g.next()
        P.dma(ktok.ap[:], K_tok.ap[:, tok0:tok0 + 64, :].rearrange("h t d -> t h d"), writes=[ktok])
        P.dma(vtok.ap[:], V_tok.ap[:, tok0:tok0 + 64, :].rearrange("h t d -> t h d"), writes=[vtok])
        P.dma(g.ap[:], G_s.ap[tok0:tok0 + 64, :], writes=[g])
        gsel = g.ap[:, 8 * d:8 * d + 8]
        bsel = g.ap[:, 16 + 8 * d:16 + 8 * d + 8]
        Mi, NMi, NDl, NDq = (1, 3, 0, 1) if d == 0 else (2, 4, 2, 3)
        kk = N0; qk = Wlo[2]; P1 = Wlo[1]; P2 = Whi[1]; smp = Whi[2]
        for h in range(8):
            P.op("tensor", lambda e, h=h: e.matmul(kk.ap[0:64, 64 * h:64 * h + 64], lhsT=kT(h), rhs=kT(h), start=True, stop=True),
                 [kb], [kk], partial=(h > 0))
        if want_out:
            for h in range(8):
                P.op("tensor", lambda e, h=h: e.matmul(qk.ap[0:64, 64 * h:64 * h + 64], lhsT=kT(h), rhs=qb.ap[:, h, qo:qo + 64],
                                                       start=True, stop=True), [kb, qb], [qk], partial=(h > 0))
        G1 = r_G1.next(); G2 = r_G2.next(); Dl = r_Dl.next(); Dq = r_Dq.next(); sm = r_sm.next(); gt = r_gt.next()
        P.op("vector", lambda e: e.tensor_copy(out=G1.ap[:], in_=bcl(gsel, 64)), [g], [G1])
        P.op("vector", lambda e: e.tensor_tensor(out=G2.ap[:], in0=cm(Mi).unsqueeze(1).to_broadcast([64, 8, 64]), in1=bcl(gsel, 64),
                                                 op=ALU.mult), [g, cst], [G2])
        G1f = G1.ap[:].rearrange("p h j -> p (h j)"); G2f = G2.ap[:].rearrange("p h j -> p (h j)")
        P.op("tensor", lambda e: e.matmul(P1.ap[0:64, :], lhsT=cm(Mi), rhs=G1f, start=True, stop=False), [cst, G1], [P1])
        P.op("tensor", lambda e: e.matmul(P1.ap[0:64, :], lhsT=cm(6), rhs=G2f, start=False, stop=False), [cst, G2], [P1], partial=True)
        P.op("tensor", lambda e: e.matmul(P1.ap[0:64, :], lhsT=cm(0), rhs=negm.ap[:, NDl].rearrange("p h j -> p (h j)"),
                                          start=False, stop=True), [cst, negm], [P1], partial=True)
        if want_out:
            P.op("tensor", lambda e: e.matmul(P2.ap[0:64, :], lhsT=cm(NMi), rhs=G1f, start=True, stop=False), [cst, G1], [P2])
            P.op("tensor", lambda e: e.matmul(P2.ap[0:64, :], lhsT=cm(5), rhs=G2f, start=False, stop=False), [cst, G2], [P2], partial=True)
            P.op("tensor", lambda e: e.matmul(P2.ap[0:64, :], lhsT=cm(0), rhs=negm.ap[:, NDq].rearrange("p h j -> p (h j)"),
                                              start=False, stop=True), [cst, negm], [P2], partial=True)
        P.op("tensor", lambda e: e.matmul(smp.ap[0:64, 0:8], lhsT=cm(Mi), rhs=gsel, start=True, stop=True), [cst, g], [smp])
        P.op("tensor", lambda e: e.matmul(smp.ap[0:64, 8:16], lhsT=cm(5), rhs=gsel, start=True, stop=True), [cst, g], [smp], partial=True)
        P.op("tensor", lambda e: e.matmul(smp.ap[:, 16:24], lhsT=ones_f.ap[:], rhs=gsel, start=True, stop=True), [ones_f, g], [smp],
             partial=True)
        P.op("scalar", lambda e: e.activation(out=Dl.ap[:].rearrange("p h j -> p (h j)"), in_=P1.ap[0:64, :], func=AF.Exp), [P1], [Dl])
        if want_out:
            P.op("scalar", lambda e: e.activation(out=Dq.ap[:].rearrange("p h j -> p (h j)"), in_=P2.ap[0:64, :], func=AF.Exp), [P2], [Dq])
        P.op("vector", lambda e: e.tensor_copy(out=sm.ap[:, 0:16], in_=smp.ap[0:64, 0:16]), [smp], [sm])
        P.op("vector", lambda e: e.tensor_tensor(out=sm.ap[:, 8:16], in0=sm.ap[:, 8:16], in1=sm.ap[:, 0:8], op=ALU.subtract), [sm], [sm])
        P.op("scalar", lambda e: e.activation(out=sm.ap[:, 16:32], in_=sm.ap[:, 0:16], func=AF.Exp), [sm], [sm])
        P.op("scalar", lambda e: e.activation(out=gt.ap[:], in_=smp.ap[:, 16:24], func=AF.Exp), [smp], [gt])
        P.op("vector", lambda e: e.tensor_tensor(out=sm.ap[:, 32:40], in0=sm.ap[:, 16:24], in1=bsel, op=ALU.mult), [sm, g], [sm])
        ecg = sm.ap[:, 16:24]; ekd = sm.ap[:, 24:32]; be = sm.ap[:, 32:40]
        Ab = r_Ab.next(); A = r_A.next()
        P.op("gpsimd", lambda e: e.tensor_tensor(out=Ab.ap[:], in0=Dl.ap[:], in1=bcl(bsel, 64), op=ALU.mult), [Dl, g], [Ab])
        P.op("vector", lambda e: e.tensor_tensor(out=A.ap[:].rearrange("p h j -> p (h j)"), in0=kk.ap[0:64, :],
                                                 in1=Ab.ap[:].rearrange("p h j -> p (h j)"), op=ALU.mult), [kk, Ab], [A])
        vb = r_vb.next(); kbe = r_kbe.next(); kdec = r_kdec.next()
        P.op("gpsimd", lambda e: e.tensor_tensor(out=vb.ap[:], in0=vtok.ap[:], in1=bcl(bsel, 128), op=ALU.mult), [vtok, g], [vb])
        P.op("gpsimd", lambda e: e.tensor_tensor(out=kbe.ap[:], in0=ktok.ap[:], in1=bcl(be, 128), op=ALU.mult), [ktok, sm], [kbe])
        P.op("gpsimd", lambda e: e.tensor_tensor(out=kdec.ap[:], in0=ktok.ap[:], in1=bcl(ekd, 128), op=ALU.mult), [ktok, sm], [kdec])
        if want_out:
            qkT = r_qkT.next(); rhsE = r_rhsE.next()
            P.op("vector", lambda e: e.tensor_tensor(out=qkT.ap[:].rearrange("p h j -> p (h j)"), in0=qk.ap[0:64, :],
                                                     in1=Dq.ap[:].rearrange("p h j -> p (h j)"), op=ALU.mult), [qk, Dq], [qkT])
            P.op("gpsimd", lambda e: e.tensor_tensor(out=rhsE.ap[:], in0=cm(0).unsqueeze(1).to_broadcast([64, 8, 64]),
                                                     in1=bcl(ecg, 64), op=ALU.mult), [cst, sm], [rhsE])
        ATp = BT
        for h in range(8):
            P.op("tensor", lambda e, h=h: e.transpose(ATp.ap[0:64, 64 * h:64 * h + 64], A.ap[:, h, :], identb.ap[0:64, 0:64]),
                 [A, identb], [ATp], partial=(h > 0))
        X = r_X.next(); Y = r_Y.next(); R = r_R.next()
        P.op("scalar", lambda e, X=X: e.activation(out=X.ap[:].rearrange("p h j -> p (h j)"), in_=ATp.ap[0:64, 0:512], func=AF.Copy,
                                                   scale=-1.0), [ATp], [X])
        P.op("vector", lambda e, Y=Y: e.tensor_scalar_mul(out=Y.ap[:], in0=A.ap[:], scalar1=-1.0), [A], [Y])
        P.op("vector", lambda e, X=X: e.tensor_tensor(out=R.ap[:], in0=X.ap[:], in1=cm(0).unsqueeze(1).to_broadcast([64, 8, 64]),
                                                      op=ALU.add), [X, cst], [R])
        pX = Wlo[1]; pY = Whi[1]; pXR = N0
        for lvl in range(1, 6):
            Xn = r_X.next() if lvl < 5 else None
            Yn = r_Y.next()
            if lvl < 5:
                for h in range(8):
                    P.op("tensor", lambda e, h=h, X=X, Y=Y: e.matmul(pX.ap[0:64, 64 * h:64 * h + 64], lhsT=Y.ap[:, h, :], rhs=X.ap[:, h, :],
                                                                    start=True, stop=True), [X, Y], [pX], partial=(h > 0))
            for h in range(8):
                P.op("tensor", lambda e, h=h, X=X, Y=Y: e.matmul(pY.ap[0:64, 64 * h:64 * h + 64], lhsT=X.ap[:, h, :], rhs=Y.ap[:, h, :],
                                                                start=True, stop=True), [X, Y], [pY], partial=(h > 0))
            if lvl < 5:
                P.op("scalar", lambda e, Xn=Xn: e.copy(out=Xn.ap[:].rearrange("p h j -> p (h j)"), in_=pX.ap[0:64, :]), [pX], [Xn])
            P.op("vector", lambda e, Yn=Yn: e.tensor_copy(out=Yn.ap[:].rearrange("p h j -> p (h j)"), in_=pY.ap[0:64, :]), [pY], [Yn])
            for h in range(8):
                P.op("tensor", lambda e, h=h, Yn=Yn, R=R: e.matmul(pXR.ap[0:64, 64 * h:64 * h + 64], lhsT=Yn.ap[:, h, :], rhs=R.ap[:, h, :],
                                                                  start=True, stop=True), [Yn, R], [pXR], partial=(h > 0))
            P.op("vector", lambda e, R=R: e.tensor_tensor(out=R.ap[:].rearrange("p h j -> p (h j)"), in0=pXR.ap[0:64, :],
                                                          in1=R.ap[:].rearrange("p h j -> p (h j)"), op=ALU.add), [pXR, R], [R])
            X, Y = Xn, Yn
        u = r_u.next(); wT = r_wT.next()
        pu = (Wlo[0], Whi[0]); pw = Wlo[2]; pE = Whi[2]
        for h in range(8):
            P.op("tensor", lambda e, h=h: e.matmul(Wt[0].ap[0:64, 128 * h:128 * h + 128], lhsT=R.ap[:, h, :], rhs=vb.ap[:, h, :],
                                                   start=True, stop=True), [R, vb], list(pu), partial=(h > 0))
        for h in range(8):
            P.op("tensor", lambda e, h=h: e.matmul(pw.ap[:, 64 * h:64 * h + 64], lhsT=kbe.ap[:, h, :], rhs=R.ap[:, h, :],
                                                   start=True, stop=True), [R, kbe], [pw], partial=(h > 0))
        P.op("scalar", lambda e: e.copy(out=u.ap[:].rearrange("p h e -> p (h e)"), in_=Wt[0].ap[0:64, :]), list(pu), [u])
        P.op("vector", lambda e: e.tensor_copy(out=wT.ap[:].rearrange("p h j -> p (h j)"), in_=pw.ap[:, :]), [pw], [wT])
        res = dict(u=u, wT=wT, kdec=kdec, gt=gt, ci=ci, d=d, want_out=want_out)
        if want_out:
            qdT = r_qdT.next()
            P.op("tensor", lambda e: e.matmul(pE.ap[:, :], lhsT=ones_b.ap[:], rhs=rhsE.ap[:].rearrange("p h j -> p (h j)"),
                                              start=True, stop=True), [ones_b, rhsE], [pE])
            P.op("vector", lambda e: e.tensor_tensor(out=qdT.ap[:], in0=qb.ap[:, :, qo:qo + 64],
                                                     in1=pE.ap[:, :].rearrange("p (h j) -> p h j", h=8), op=ALU.mult), [qb, pE], [qdT])
            res.update(qdT=qdT, qkT=qkT)
        return res

    def scan(pr):
        d = pr["d"]; ci = pr["ci"]; want_out = pr["want_out"]
        u, wT, kdec, gt = pr["u"], pr["wT"], pr["kdec"], pr["gt"]
        sf, sbb = Sf[d], Sb[d]
        vnew = r_vnew.next()
        pws = (Wlo[0], Whi[0]); pkv = (Wlo[1], Whi[1]); po = (Wlo[2], Whi[2])
        for h in range(8):
            P.op("tensor", lambda e, h=h: e.matmul(Wt[0].ap[0:64, 128 * h:128 * h + 128], lhsT=wT.ap[:, h, :], rhs=sbb.ap[:, h, :],
                                                   start=True, stop=True), [wT, sbb], list(pws), partial=(h > 0))
        P.op("vector", lambda e: e.tensor_tensor(out=vnew.ap[:].rearrange("p h e -> p (h e)"), in0=u.ap[:].rearrange("p h e -> p (h e)"),
                                                 in1=Wt[0].ap[0:64, :], op=ALU.subtract), [u] + list(pws), [vnew])
        for h in range(8):
            P.op("tensor", lambda e, h=h: e.matmul(Wt[1].ap[:, 128 * h:128 * h + 128], lhsT=kdec.ap[:, h, :], rhs=vnew.ap[:, h, :],
                                                   start=True, stop=True), [kdec, vnew], list(pkv), partial=(h > 0))
        if want_out:
            qdT, qkT = pr["qdT"], pr["qkT"]
            for h in range(8):
                P.op("tensor", lambda e, h=h: e.matmul(Wt[2].ap[0:64, 128 * h:128 * h + 128], lhsT=qdT.ap[:, h, :], rhs=sbb.ap[:, h, :],
                                                       start=True, stop=False), [qdT, sbb], list(po), partial=(h > 0))
                P.op("tensor", lambda e, h=h: e.matmul(Wt[2].ap[0:64, 128 * h:128 * h + 128], lhsT=qkT.ap[:, h, :], rhs=vnew.ap[:, h, :],
                                                       start=False, stop=True), [qkT, vnew], list(po), partial=True)
        P.op("gpsimd", lambda e: e.tensor_tensor(out=sf.ap[:], in0=sf.ap[:], in1=bcl(gt.ap[:, 0:8], 128), op=ALU.mult), [sf, gt], [sf])
        P.op("vector", lambda e: e.tensor_tensor(out=sf.ap[:].rearrange("p h e -> p (h e)"), in0=sf.ap[:].rearrange("p h e -> p (h e)"),
                                                 in1=Wt[1].ap[:, :], op=ALU.add), [sf] + list(pkv), [sf])
        P.op("scalar", lambda e: e.copy(out=sbb.ap[:], in_=sf.ap[:]), [sf], [sbb])
        if not want_out:
            return
        t0 = ci * 64 - C
        if d == 0:
            osb = r_osb.next()
            P.op("scalar", lambda e: e.copy(out=osb.ap[:].rearrange("p h e -> p (h e)"), in_=Wt[2].ap[0:64, :]), list(po), [osb])
            P.dma(OA_s.ap[t0:t0 + 64, :], osb.ap[:].rearrange("p h e -> p (h e)"), reads=[osb], writes=[OA_s], partial=True,
                  key=osb.b.name + "s", eng=ST_ENG)
        else:
            oa = r_oa.next(); z = r_z.next(); osb = r_osb.next(); sq = r_sq.next(); ss = r_ss.next(); on = r_on.next()
            P.dma(oa.ap[:].rearrange("p h e -> p (h e)"), OA_s.ap[t0:t0 + 64, :], reads=[OA_s], writes=[oa])
            P.dma(z.ap[:].rearrange("p h e -> p (h e)"), Z_s.ap[t0:t0 + 64, :], writes=[z])
            P.op("vector", lambda e: e.tensor_tensor(out=osb.ap[:].rearrange("p h e -> p (h e)"), in0=Wt[2].ap[0:64, :],
                                                     in1=oa.ap[:].rearrange("p h e -> p (h e)"), op=ALU.add), list(po) + [oa], [osb])
            P.op("gpsimd", lambda e: e.tensor_tensor(out=sq.ap[:], in0=osb.ap[:], in1=osb.ap[:], op=ALU.mult), [osb], [sq])
            P.op("vector", lambda e: e.reduce_sum(out=ss.ap[:, 0:8], in_=sq.ap[:], axis=AX.X), [sq], [ss])
            P.op("scalar", lambda e: e.activation(out=ss.ap[:, 0:8], in_=ss.ap[:, 0:8], func=AF.Ln, scale=1.0 / 128.0, bias=epsb.ap[0:64, 1:2]),
                 [ss, epsb], [ss])
            P.op("scalar", lambda e: e.activation(out=ss.ap[:, 0:8], in_=ss.ap[:, 0:8], func=AF.Exp, scale=-0.5), [ss], [ss])
            P.op("gpsimd", lambda e: e.tensor_tensor(out=z.ap[:], in0=z.ap[:], in1=gnorm.ap[0:64, :].unsqueeze(1).to_broadcast([64, 8, 128]),
                                                     op=ALU.mult), [z, gnorm], [z])
            P.op("vector", lambda e: e.tensor_tensor(out=osb.ap[:], in0=osb.ap[:], in1=bcl(ss.ap[:, 0:8], 128), op=ALU.mult), [osb, ss], [osb])
            P.op("vector", lambda e: e.tensor_tensor(out=on.ap[:], in0=osb.ap[:], in1=z.ap[:], op=ALU.mult), [osb, z], [on])
            cj = ci - CH_CTX
            blk_i = (cj % 8)
            if blk_i == 7 or "ob" not in scan.__dict__:
                scan.ob = oblk.next()
            ob_ = scan.ob
            for h in range(8):
                P.op("tensor", lambda e, h=h: e.transpose(BT.ap[:, 64 * h:64 * h + 64], on.ap[:, h, :], identb.ap[0:64, 0:64]),
                     [on, identb], [BT], partial=(h > 0))
            P.op("scalar", lambda e: e.copy(out=ob_.ap[:, :, 64 * blk_i:64 * blk_i + 64], in_=BT.ap[:, 0:512].rearrange("p (h j) -> p h j", h=8)),
                 [BT], [ob_], partial=True)
            if blk_i == 0:
                tb = (cj // 8) * 512
                P.dma(ODN_T.ap[:, tb:tb + 512].rearrange("(h d) t -> d h t", h=8), ob_.ap[:], reads=[ob_], writes=[ODN_T], partial=True,
                      key=ob_.b.name + "s", eng=ST_ENG)

    seqA = [(ci, 0, False) for ci in range(CH_CTX)] + [(ci, 0, True) for ci in range(CH_CTX, CH_OWN_END)]
    seqB = ([(ci, 1, False) for ci in range(CH_CTX - 1, -1, -1)] + [(ci, 1, False) for ci in range(NCH - 1, CH_OWN_END - 1, -1)]
            + [(ci, 1, True) for ci in range(CH_OWN_END - 1, CH_CTX - 1, -1)])
    for seq in (seqA, seqB):
        prev = None
        for item in seq:
            pr = prep(*item)
            if prev is not None:
                scan(prev)
            prev = pr
        scan(prev)


def phase2(nc, cfg, P, st, sb, ps, env):
    S, C, T, NT = cfg.S, cfg.C, cfg.T, cfg.NT
    NKT = NT // 128
    NQG = T // 512
    QT_da, KT_da, V_da, ODA_T, dalam, subln, identb, epsb = (env[k] for k in (
        "QT_da", "KT_da", "V_da", "ODA_T", "dalam", "subln", "identb", "epsb"))
    ktb = Ring([sb(st, "ktb%d" % i, [128, NT], BF16) for i in range(2)])
    vtb = Ring([sb(st, "vtb%d" % i, [128, NKT, 130], BF16) for i in range(2)])
    qtb = Ring([sb(st, "qtb%d" % i, [128, T], BF16) for i in range(2)])
    SP = Ring([ps(st, "SP%d" % i, [128, 1024]) for i in range(2)])
    ACC = ps(st, "ACC", [128, 3, 512])
    PTr = ps(st, "PTr", [128, 4, 128], BF16)
    pexp = Ring([sb(st, "pexp%d" % i, [128, 1024], BF16) for i in range(3)])
    lam_t = sb(st, "lam_t", [128, 256])
    lam_p = sb(st, "lam_p", [128, 2, 64])
    lam_s = sb(st, "lam_s", [128, 4])
    gsub = sb(st, "gsub", [128, 128])
    rr = Ring([sb(st, "rr%d" % i, [128, 4]) for i in range(4)])
    tt = Ring([sb(st, "tt%d" % i, [128, 128]) for i in range(2)])
    oo = Ring([sb(st, "oo%d" % i, [128, 128]) for i in range(2)])
    junk = sb(st, "junk", [128, 128])
    onb = Ring([sb(st, "onb%d" % i, [128, 128], BF16) for i in range(4)])
    otb = Ring([sb(st, "otb%d" % i, [128, 512], BF16) for i in range(2)])

    def acc(m, j):
        a = m * 4 + j
        return ACC.ap[:, a // 3, (a % 3) * 160:(a % 3) * 160 + 129]

    lam_init = 0.8 - 0.6 * math.exp(-0.3 * 0)
    P.dma(lam_t.ap[:], dalam.ap.partition_broadcast(128), writes=[lam_t])
    P.dma(gsub.ap[:], subln.ap.partition_broadcast(128), writes=[gsub])
    P.op("vector", lambda e: e.tensor_tensor(out=lam_p.ap[:, 0, :], in0=lam_t.ap[:, 0:64], in1=lam_t.ap[:, 64:128], op=ALU.mult),
         [lam_t], [lam_p], partial=True)
    P.op("vector", lambda e: e.tensor_tensor(out=lam_p.ap[:, 1, :], in0=lam_t.ap[:, 128:192], in1=lam_t.ap[:, 192:256], op=ALU.mult),
         [lam_t], [lam_p], partial=True)
    P.op("vector", lambda e: e.reduce_sum(out=lam_s.ap[:, 0:2], in_=lam_p.ap[:], axis=AX.X), [lam_p], [lam_s])
    P.op("scalar", lambda e: e.activation(out=lam_s.ap[:, 0:2], in_=lam_s.ap[:, 0:2], func=AF.Exp), [lam_s], [lam_s])
    P.op("vector", lambda e: e.tensor_tensor(out=lam_s.ap[:, 2:3], in0=lam_s.ap[:, 1:2], in1=lam_s.ap[:, 0:1], op=ALU.subtract),
         [lam_s], [lam_s])
    P.op("vector", lambda e: e.tensor_scalar_add(out=lam_s.ap[:, 2:3], in0=lam_s.ap[:, 2:3], scalar1=-lam_init), [lam_s], [lam_s])
    P.op("vector", lambda e: e.tensor_scalar_mul(out=gsub.ap[:], in0=gsub.ap[:], scalar1=1.0 - lam_init), [gsub], [gsub])
    for v in vtb.tiles:
        P.op("gpsimd", lambda e, v=v: e.memset(v.ap[:, :, 128:130], 1.0), [], [v])

    for h in range(8):
        kt_ = ktb.next(); vt_ = vtb.next(); qt_ = qtb.next()
        P.dma(kt_.ap[:], KT_da.ap[h], writes=[kt_])
        P.dma(vt_.ap[:, :, 0:128], V_da.ap[h], writes=[vt_], partial=True)
        P.dma(qt_.ap[:], QT_da.ap[h], writes=[qt_])
        for qg in range(NQG):
            q0 = qg * 512
            sps = {}
            pes = {}
            for kt in range(NKT + 1):
                if kt < NKT:
                    sp = SP.next(); pe_ = pexp.next()
                    sps[kt] = sp; pes[kt] = pe_
                    for m in range(2):
                        P.op("tensor", lambda e, sp=sp, m=m, kt=kt, kt_=kt_, qt_=qt_, q0=q0: e.matmul(
                            sp.ap[:, m * 512:(m + 1) * 512], lhsT=kt_.ap[64 * m:64 * m + 64, kt * 128:(kt + 1) * 128],
                            rhs=qt_.ap[64 * m:64 * m + 64, q0:q0 + 512], start=True, stop=True),
                             [kt_, qt_], [sp], partial=(m > 0))
                    P.op("scalar", lambda e, sp=sp, pe_=pe_: e.activation(out=pe_.ap[:], in_=sp.ap[:], func=AF.Exp, scale=0.125),
                         [sp], [pe_])
                if kt >= 1:
                    k1 = kt - 1
                    pe_ = pes.pop(k1)
                    for m in range(2):
                        for j in range(4):
                            P.op("tensor", lambda e, pe_=pe_, m=m, j=j, k1=k1, vt_=vt_: e.matmul(
                                acc(m, j), lhsT=pe_.ap[:, m * 512 + j * 128:m * 512 + (j + 1) * 128], rhs=vt_.ap[:, k1, 0:129],
                                start=(k1 == 0 and (m * 4 + j) % 3 == 0), stop=(k1 == NKT - 1), skip_group_check=True),
                                 [pe_, vt_], [ACC], partial=not (k1 == 0 and m == 0 and j == 0))
            ot_ = otb.next()
            for j in range(4):
                r_ = rr.next(); t_ = tt.next(); o_ = oo.next(); on_ = onb.next()
                P.op("vector", lambda e, r_=r_, j=j: e.reciprocal(out=r_.ap[:, 0:1], in_=acc(0, j)[:, 128:129]), [ACC], [r_])
                P.op("vector", lambda e, r_=r_, j=j: e.reciprocal(out=r_.ap[:, 1:2], in_=acc(1, j)[:, 128:129]), [ACC], [r_], partial=True)
                P.op("vector", lambda e, r_=r_: e.tensor_tensor(out=r_.ap[:, 1:2], in0=r_.ap[:, 1:2], in1=lam_s.ap[:, 2:3], op=ALU.mult),
                     [r_, lam_s], [r_])
                P.op("scalar", lambda e, t_=t_, r_=r_, j=j: e.activation(out=t_.ap[:], in_=acc(0, j)[:, 0:128], func=AF.Copy,
                                                                        scale=r_.ap[:, 0:1]), [ACC, r_], [t_])
                P.op("vector", lambda e, t_=t_, r_=r_, o_=o_, j=j: e.scalar_tensor_tensor(
                    out=o_.ap[:], in0=acc(1, j)[:, 0:128], scalar=r_.ap[:, 1:2], in1=t_.ap[:], op0=ALU.mult, op1=ALU.add),
                     [ACC, r_, t_], [o_])
                P.op("scalar", lambda e, o_=o_, r_=r_: e.activation(out=junk.ap[:], in_=o_.ap[:], func=AF.Square,
                                                                   accum_out=r_.ap[:, 2:3]), [o_], [junk, r_], partial=True)
                P.op("scalar", lambda e, r_=r_: e.activation(out=r_.ap[:, 2:3], in_=r_.ap[:, 2:3], func=AF.Ln, scale=1.0 / 128.0,
                                                            bias=epsb.ap[:, 1:2]), [r_, epsb], [r_])
                P.op("scalar", lambda e, r_=r_: e.activation(out=r_.ap[:, 2:3], in_=r_.ap[:, 2:3], func=AF.Exp, scale=-0.5), [r_], [r_])
                P.op("vector", lambda e, o_=o_, r_=r_, on_=on_: e.scalar_tensor_tensor(
                    out=on_.ap[:], in0=o_.ap[:], scalar=r_.ap[:, 2:3], in1=gsub.ap[:], op0=ALU.mult, op1=ALU.mult),
                     [o_, r_, gsub], [on_])
                P.op("tensor", lambda e, on_=on_, j=j: e.transpose(PTr.ap[:, j, :], on_.ap[:], identb.ap[:]), [on_, identb], [PTr],
                     partial=(j > 0))
            P.op("vector", lambda e, ot_=ot_: e.tensor_copy(out=ot_.ap[:], in_=PTr.ap[:].rearrange("p a b -> p (a b)")), [PTr], [ot_])
            P.dma(ODA_T.ap[h * 128:(h + 1) * 128, q0:q0 + 512], ot_.ap[:], reads=[ot_], writes=[ODA_T], partial=True,
                  key=ot_.b.name + "s", eng=ST_ENG)


def phase1b(nc, cfg, P, st, sb, ps, env):
    S, C, T, NT, NTP = cfg.S, cfg.C, cfg.T, cfg.NT, cfg.NTP
    uT, ident, identb, win, convw, alog, dtb, cosT, sinT, epsb = (env[k] for k in (
        "uT", "ident", "identb", "win", "convw", "alog", "dtb", "cosT", "sinT", "epsb"))
    QT_da, KT_da, V_da, QT_dn, KT_dn, K_tok, V_tok, G_s, Z_s, MG_s = (env[k] for k in (
        "QT_da", "KT_da", "V_da", "QT_dn", "KT_dn", "K_tok", "V_tok", "G_s", "Z_s", "MG_s"))
    groups, pos = env["groups"], env["pos"]
    NTI = NT // 128

    wst = Ring([sb(st, "wst%d" % i, [128, KC, 128]) for i in range(2)])
    wbf = Ring([sb(st, "wbf%d" % i, [128, KC, 512], BF16) for i in range(2)])
    PA = Ring([ps(st, "PA%d" % i, [128, 1024]) for i in range(2)])
    PB = Ring([ps(st, "PB%d" % i, [128, 512]) for i in range(2)])
    PC = ps(st, "PC", [128, 512])
    PT = ps(st, "PT", [128, 4, 128], BF16)
    tabc = Ring([sb(st, "tabc%d" % i, [128, 512]) for i in range(2)])
    tabs = Ring([sb(st, "tabs%d" % i, [128, 512]) for i in range(2)])
    fa = Ring([sb(st, "fa%d" % i, [128, 512]) for i in range(2)])
    fb = Ring([sb(st, "fb%d" % i, [128, 512]) for i in range(2)])
    ob = Ring([sb(st, "ob%d" % i, [128, 512], BF16) for i in range(2)])
    preb = Ring([sb(st, "preb%d" % i, [128, 520], BF16) for i in range(2)])
    sqb = Ring([sb(st, "sqb%d" % i, [128, 512], BF16) for i in range(2)])
    tok = Ring([sb(st, "tok%d" % i, [128, 4, 128], BF16) for i in range(2)])
    cw = Ring([sb(st, "cw%d" % i, [128, 7]) for i in range(2)])
    dg = Ring([sb(st, "dg%d" % i, [128, 7, 128], BF16) for i in range(2)])
    ones_k = sb(st, "ones_k", [128, 128], BF16)
    ones_q = sb(st, "ones_q", [128, 128], BF16)
    gpar = sb(st, "gpar", [128, 2, 16])
    gpre = sb(st, "gpre", [128, NTI, 32])
    ga = sb(st, "ga", [128, NTI, 16])

    P.op("vector", lambda e: e.memset(ones_k.ap[:], 1.0), [], [ones_k])
    P.op("vector", lambda e: e.memset(ones_q.ap[:], 128.0), [], [ones_q])
    P.dma(gpar.ap[:, 0, :], alog.ap[0, :].partition_broadcast(128), writes=[gpar], partial=True, key="gpar")
    P.dma(gpar.ap[:, 1, :], dtb.ap[0, :].partition_broadcast(128), writes=[gpar], partial=True, key="gpar")
    P.op("scalar", lambda e: e.activation(out=gpar.ap[:, 0, :], in_=gpar.ap[:, 0, :], func=AF.Exp), [gpar], [gpar])
    P.op("vector", lambda e: e.tensor_scalar_mul(out=gpar.ap[:, 0, :], in0=gpar.ap[:, 0, :], scalar1=-1.0), [gpar], [gpar])

    def load_w(c0, ncols):
        wb = wbf.next()
        for j in range((ncols + 127) // 128):
            n = min(128, ncols - 128 * j)
            w = wst.next()
            P.dma(w.ap[:, :, 0:n], win.ap[:, c0 + 128 * j:c0 + 128 * j + n].rearrange("(k p) c -> p k c", p=128), writes=[w])
            P.op("gpsimd", lambda e, w=w, wb=wb, j=j, n=n: e.tensor_copy(out=wb.ap[:, :, 128 * j:128 * j + n], in_=w.ap[:, :, 0:n]),
                 [w], [wb], partial=(j > 0))
        return wb

    def sigm_parts(src_ap, tmp, L):
        P.op("scalar", lambda e: e.activation(out=tmp.ap[:, 0:L], in_=src_ap, func=AF.Exp, scale=-1.0), [], [tmp])
        P.op("vector", lambda e: e.tensor_scalar_add(out=tmp.ap[:, 0:L], in0=tmp.ap[:, 0:L], scalar1=1.0), [tmp], [tmp])
        P.op("vector", lambda e: e.reciprocal(out=tmp.ap[:, 0:L], in_=tmp.ap[:, 0:L]), [tmp], [tmp])

    for (kind, o_nat, o_perm, dst) in (("q", OQ, OQP, QT_da), ("k", OK_, OKP, KT_da)):
        for h in range(8):
            wn = load_w(o_nat + 128 * h, 128)
            wp = load_w(o_perm + 128 * h, 128)
            for (g0, L, own) in groups:
                if kind == "q" and not own:
                    continue
                pa = PA.next(); tc_ = tabc.next(); ts_ = tabs.next(); a1 = fa.next(); a2 = fb.next(); o_ = ob.next()
                P.dma(tc_.ap[:, 0:L], cosT.ap[:, g0:g0 + L], writes=[tc_])
                P.dma(ts_.ap[:, 0:L], sinT.ap[:, g0:g0 + L], writes=[ts_])
                c0 = pos(g0)
                for k in range(KC):
                    P.op("tensor", lambda e, k=k, pa=pa, wn=wn, c0=c0, L=L: e.matmul(
                        pa.ap[:, 0:L], lhsT=wn.ap[:, k, 0:128], rhs=uT.ap[:, k, c0:c0 + L],
                        start=(k == 0), stop=(k == KC - 1)), [wn, uT], [pa], partial=(k > 0))
                for k in range(KC):
                    P.op("tensor", lambda e, k=k, pa=pa, wp=wp, c0=c0, L=L: e.matmul(
                        pa.ap[:, 512:512 + L], lhsT=wp.ap[:, k, 0:128], rhs=uT.ap[:, k, c0:c0 + L],
                        start=(k == 0), stop=(k == KC - 1)), [wp, uT], [pa], partial=True)
                P.op("vector", lambda e, pa=pa, tc_=tc_, a1=a1, L=L: e.tensor_tensor(
                    out=a1.ap[:, 0:L], in0=pa.ap[:, 0:L], in1=tc_.ap[:, 0:L], op=ALU.mult), [pa, tc_], [a1])
                P.op("vector", lambda e, pa=pa, ts_=ts_, a2=a2, L=L: e.tensor_tensor(
                    out=a2.ap[:, 0:L], in0=pa.ap[:, 512:512 + L], in1=ts_.ap[:, 0:L], op=ALU.mult), [pa, ts_], [a2])
                P.op("gpsimd", lambda e, a1=a1, a2=a2, o_=o_, L=L: e.tensor_tensor(
                    out=o_.ap[:, 0:L], in0=a1.ap[:, 0:L], in1=a2.ap[:, 0:L], op=ALU.add), [a1, a2], [o_])
                off = g0 - C if kind == "q" else g0
                P.dma(dst.ap[h, :, off:off + L], o_.ap[:, 0:L], reads=[o_], writes=[dst], partial=True,
                      key=o_.b.name + "s", eng=ST_ENG)

    for (kind, o_w) in (("v", OV), ("z", OZ)):
        for half in range(2):
            wb = load_w(o_w + 512 * half, 512)
            for (g0, L, own) in groups:
                if kind == "z" and not own:
                    continue
                for j in range(L // 128):
                    tk = g0 + 128 * j
                    c0 = pos(tk)
                    pb = PB.next()
                    for k in range(KC):
                        P.op("tensor", lambda e, k=k, pb=pb, wb=wb, c0=c0: e.matmul(
                            pb.ap[:], lhsT=uT.ap[:, k, c0:c0 + 128], rhs=wb.ap[:, k, :],
                            start=(k == 0), stop=(k == KC - 1)), [wb, uT], [pb], partial=(k > 0))
                    if kind == "v":
                        o_ = ob.next()
                        P.op("scalar", lambda e, pb=pb, o_=o_: e.copy(out=o_.ap[:], in_=pb.ap[:]), [pb], [o_])
                        P.dma(V_da.ap[4 * half:4 * half + 4, :, tk // 128, :].rearrange("h t e -> t h e"),
                              o_.ap[:].rearrange("t (h e) -> t h e", h=4), reads=[o_], writes=[V_da], partial=True,
                              key=o_.b.name + "s", eng=ST_ENG)
                    else:
                        t_ = fa.next(); o_ = fb.next()
                        P.op("scalar", lambda e, pb=pb, t_=t_: e.activation(out=t_.ap[:], in_=pb.ap[:], func=AF.Exp, scale=-1.0),
                                  [pb], [t_])
                        P.op("vector", lambda e, t_=t_: e.tensor_scalar_add(out=t_.ap[:], in0=t_.ap[:], scalar1=1.0), [t_], [t_])
                        P.op("vector", lambda e, t_=t_: e.reciprocal(out=t_.ap[:], in_=t_.ap[:]), [t_], [t_])
                        P.op("vector", lambda e, pb=pb, t_=t_, o_=o_: e.tensor_tensor(out=o_.ap[:], in0=pb.ap[:], in1=t_.ap[:],
                                                                                     op=ALU.mult), [pb, t_], [o_])
                        P.dma(Z_s.ap[tk - C:tk - C + 128, 512 * half:512 * half + 512], o_.ap[:], reads=[o_],
                              writes=[Z_s], partial=True, key=o_.b.name + "s", eng=ST_ENG)

    for gi, o_w in enumerate((OM1, OM2)):
        for cb in range(8):
            wb = load_w(o_w + 128 * cb, 128)
            for (g0, L, own) in groups:
                if not own:
                    continue
                pa = PA.next(); o_ = fa.next()
                c0 = pos(g0)
                for k in range(KC):
                    P.op("tensor", lambda e, k=k, pa=pa, wb=wb, c0=c0, L=L: e.matmul(
                        pa.ap[:, 0:L], lhsT=wb.ap[:, k, 0:128], rhs=uT.ap[:, k, c0:c0 + L],
                        start=(k == 0), stop=(k == KC - 1)), [wb, uT], [pa], partial=(k > 0))
                P.op("scalar", lambda e, pa=pa, o_=o_, L=L: e.activation(out=o_.ap[:, 0:L], in_=pa.ap[:, 0:L], func=AF.Exp,
                                                                       scale=-1.0), [pa], [o_])
                P.op("vector", lambda e, o_=o_, L=L: e.tensor_scalar_add(out=o_.ap[:, 0:L], in0=o_.ap[:, 0:L], scalar1=1.0), [o_], [o_])
                P.op("vector", lambda e, o_=o_, L=L: e.reciprocal(out=o_.ap[:, 0:L], in_=o_.ap[:, 0:L]), [o_], [o_])
                P.dma(MG_s.ap[gi, 128 * cb:128 * cb + 128, g0 - C:g0 - C + L], o_.ap[:, 0:L], reads=[o_],
                      writes=[MG_s], partial=True, key=o_.b.name + "s", eng=ST_ENG)

    wb = load_w(OG, 32)
    for i in range(NTI):
        c0 = pos(128 * i)
        pb = PB.next()
        for k in range(KC):
            P.op("tensor", lambda e, k=k, pb=pb, c0=c0, wb=wb: e.matmul(
                pb.ap[:, 0:32], lhsT=uT.ap[:, k, c0:c0 + 128], rhs=wb.ap[:, k, 0:32],
                start=(k == 0), stop=(k == KC - 1)), [wb, uT], [pb], partial=(k > 0))
        P.op("vector", lambda e, pb=pb, i=i: e.tensor_copy(out=gpre.ap[:, i, :], in_=pb.ap[:, 0:32]), [pb], [gpre], partial=True)
    gy_ap = gpre.ap[:, :, 0:16]
    gb_ap = gpre.ap[:, :, 16:32]
    P.op("vector", lambda e: e.tensor_tensor(out=gy_ap, in0=gy_ap, in1=gpar.ap[:, 1:2, :].to_broadcast([128, NTI, 16]), op=ALU.add),
         [gpre, gpar], [gpre])
    P.op("scalar", lambda e: e.activation(out=ga.ap[:], in_=gy_ap, func=AF.Abs), [gpre], [ga])
    P.op("scalar", lambda e: e.activation(out=ga.ap[:], in_=ga.ap[:], func=AF.Exp, scale=-1.0), [ga], [ga])
    P.op("scalar", lambda e: e.activation(out=ga.ap[:], in_=ga.ap[:], func=AF.Ln, bias=epsb.ap[:, 3:4]), [ga, epsb], [ga])
    P.op("vector", lambda e: e.scalar_tensor_tensor(out=ga.ap[:], in0=gy_ap, scalar=0.0, in1=ga.ap[:], op0=ALU.max, op1=ALU.add),
         [gpre, ga], [ga])
    P.op("vector", lambda e: e.tensor_tensor(out=gy_ap, in0=ga.ap[:], in1=gpar.ap[:, 0:1, :].to_broadcast([128, NTI, 16]), op=ALU.mult),
         [ga, gpar], [gpre])
    P.op("scalar", lambda e: e.activation(out=gb_ap, in_=gb_ap, func=AF.Exp, scale=-1.0), [gpre], [gpre])
    P.op("vector", lambda e: e.tensor_scalar_add(out=gb_ap, in0=gb_ap, scalar1=1.0), [gpre], [gpre])
    P.op("vector", lambda e: e.reciprocal(out=gb_ap, in_=gb_ap), [gpre], [gpre])
    P.dma(G_s.ap[:, :].rearrange("(i p) c -> p i c", p=128), gpre.ap[:], reads=[gpre], writes=[G_s], key="gouts", eng=ST_ENG)

    for ci, (kind, o_w) in enumerate((("q", ODQ), ("k", ODK), ("v", ODV))):
        for h in range(8):
            wb = load_w(o_w + 128 * h, 128)
            cw_ = cw.next(); dg_ = dg.next()
            ch0 = ci * 1024 + 128 * h
            P.dma(cw_.ap[:], convw.ap[ch0:ch0 + 128, :], writes=[cw_])
            for tp in range(7):
                P.op("vector", lambda e, tp=tp, cw_=cw_, dg_=dg_: e.tensor_scalar_mul(
                    out=dg_.ap[:, tp, :], in0=ident.ap[:], scalar1=cw_.ap[:, tp:tp + 1]),
                     [cw_, ident], [dg_], partial=(tp > 0))
            for (g0, L, own) in groups:
                if kind == "q" and not own:
                    continue
                pa = PA.next(); pb = PB.next(); pr_ = preb.next(); s_ = fa.next(); t_ = fb.next()
                c0 = pos(g0)
                n1 = min(512, L + 6)
                for k in range(KC):
                    P.op("tensor", lambda e, k=k, pa=pa, c0=c0, n1=n1, wb=wb: e.matmul(
                        pa.ap[:, 0:n1], lhsT=wb.ap[:, k, 0:128], rhs=uT.ap[:, k, c0 - 3:c0 - 3 + n1],
                        start=(k == 0), stop=(k == KC - 1)), [wb, uT], [pa], partial=(k > 0))
                if L + 6 > 512:
                    n2 = L + 6 - 512
                    for k in range(KC):
                        P.op("tensor", lambda e, k=k, pa=pa, c0=c0, n2=n2, wb=wb: e.matmul(
                            pa.ap[:, 512:512 + n2], lhsT=wb.ap[:, k, 0:128], rhs=uT.ap[:, k, c0 + 509:c0 + 509 + n2],
                            start=(k == 0), stop=(k == KC - 1)), [wb, uT], [pa], partial=True)
                P.op("scalar", lambda e, pa=pa, pr_=pr_, L=L: e.copy(out=pr_.ap[:, 0:L + 6], in_=pa.ap[:, 0:L + 6]),
                     [pa], [pr_])
                for tp in range(7):
                    P.op("tensor", lambda e, tp=tp, pb=pb, pr_=pr_, L=L, dg_=dg_: e.matmul(
                        pb.ap[:, 0:L], lhsT=dg_.ap[:, tp, :], rhs=pr_.ap[:, tp:tp + L], start=(tp == 0), stop=(tp == 6)),
                         [dg_, pr_], [pb], partial=(tp > 0))
                P.op("scalar", lambda e, pb=pb, t_=t_, L=L: e.activation(out=t_.ap[:, 0:L], in_=pb.ap[:, 0:L], func=AF.Exp,
                                                                       scale=-1.0), [pb], [t_])
                P.op("vector", lambda e, t_=t_, L=L: e.tensor_scalar_add(out=t_.ap[:, 0:L], in0=t_.ap[:, 0:L], scalar1=1.0), [t_], [t_])
                P.op("vector", lambda e, t_=t_, L=L: e.reciprocal(out=t_.ap[:, 0:L], in_=t_.ap[:, 0:L]), [t_], [t_])
                o_ = ob.next()
                if kind in ("q", "k"):
                    sq_ = sqb.next()
                    P.op("vector", lambda e, pb=pb, t_=t_, s_=s_, L=L: e.tensor_tensor(
                        out=s_.ap[:, 0:L], in0=pb.ap[:, 0:L], in1=t_.ap[:, 0:L], op=ALU.mult), [pb, t_], [s_])
                    P.op("gpsimd", lambda e, s_=s_, sq_=sq_, L=L: e.tensor_tensor(
                        out=sq_.ap[:, 0:L], in0=s_.ap[:, 0:L], in1=s_.ap[:, 0:L], op=ALU.mult), [s_], [sq_])
                    onesm = ones_q if kind == "q" else ones_k
                    ecol = 2 if kind == "q" else 1
                    P.op("tensor", lambda e, sq_=sq_, L=L, onesm=onesm: e.matmul(
                        PC.ap[:, 0:L], lhsT=onesm.ap[:], rhs=sq_.ap[:, 0:L], start=True, stop=True), [sq_, onesm], [PC])
                    P.op("scalar", lambda e, t_=t_, L=L, ecol=ecol: e.activation(
                        out=t_.ap[:, 0:L], in_=PC.ap[:, 0:L], func=AF.Ln, bias=epsb.ap[:, ecol:ecol + 1]), [PC, epsb], [t_])
                    P.op("scalar", lambda e, t_=t_, L=L: e.activation(out=t_.ap[:, 0:L], in_=t_.ap[:, 0:L], func=AF.Exp, scale=-0.5),
                         [t_], [t_])
                    P.op("vector", lambda e, s_=s_, t_=t_, o_=o_, L=L: e.tensor_tensor(
                        out=o_.ap[:, 0:L], in0=s_.ap[:, 0:L], in1=t_.ap[:, 0:L], op=ALU.mult), [s_, t_], [o_])
                    dstT = QT_dn if kind == "q" else KT_dn
                    P.dma(dstT.ap[h, :, g0:g0 + L], o_.ap[:, 0:L], reads=[o_], writes=[dstT], partial=True,
                          key=o_.b.name + "s", eng=ST_ENG)
                else:
                    P.op("vector", lambda e, pb=pb, t_=t_, o_=o_, L=L: e.tensor_tensor(
                        out=o_.ap[:, 0:L], in0=pb.ap[:, 0:L], in1=t_.ap[:, 0:L], op=ALU.mult), [pb, t_], [o_])
                if kind in ("k", "v"):
                    tk_ = tok.next()
                    nj = L // 128
                    for j in range(nj):
                        P.op("tensor", lambda e, j=j, o_=o_: e.transpose(PT.ap[:, j, :], o_.ap[:, 128 * j:128 * j + 128],
                                                                         identb.ap[:]), [o_, identb], [PT], partial=(j > 0))
                    P.op("scalar", lambda e, tk_=tk_, nj=nj: e.copy(out=tk_.ap[:, 0:nj, :], in_=PT.ap[:, 0:nj, :]), [PT], [tk_])
                    dst = K_tok if kind == "k" else V_tok
                    P.dma(dst.ap[h, g0:g0 + L, :].rearrange("(j p) d -> p j d", p=128), tk_.ap[:, 0:nj, :], reads=[tk_],
                          writes=[dst], partial=True, key=tk_.b.name + "s", eng=ST_ENG)


def kernel(**inputs):
    inputs = {k: np.asarray(v) for k, v in inputs.items()}
    B, S, _ = inputs["x"].shape
    C = inputs["ctx"].shape[1]
    cfg = Cfg(S=S, C=C, debug=False)
    nc = build(cfg)
    in_names = {a.memorylocations[0].name for a in nc.m.functions[0].allocations
                if isinstance(a, mybir.MemoryLocationSet) and a.kind == "ExternalInput"}
    in_maps = []
    for b in range(B):
        for h in range(2):
            m = core_inputs(cfg, inputs, b, h)
            in_maps.append({k: v for k, v in m.items() if k in in_names})
    res = run_bass_kernel_spmd(nc, in_maps, core_ids=list(range(2 * B)))
    T = S // 2
    out = np.empty((B, S, D), np.float32)
    for b in range(B):
        out[b, :T] = res.results[2 * b]["out"]
        out[b, T:] = res.results[2 * b + 1]["out"][::-1]
    return out
```

```python
import contextlib
import math
import numpy as np
import concourse.bass as bass
import concourse.mybir as mybir
from concourse.bass_utils import run_bass_kernel_spmd

F32 = mybir.dt.float32
BF16 = mybir.dt.bfloat16
AF = mybir.ActivationFunctionType
ALU = mybir.AluOpType
AX = mybir.AxisListType

D = 1024
KC = 8
ENGS = ("tensor", "vector", "scalar", "gpsimd", "sync")
ST_ENG = "sync"

OQ, OQP, OK_, OKP, OV, ODQ, ODK, ODV, OZ, OG, OM1, OM2, NW = (
    0, 1024, 2048, 3072, 4096, 5120, 6144, 7168, 8192, 9216, 9248, 10272, 11296)


class Buf:
    __slots__ = ("name", "writers", "readers", "gen")

    def __init__(self, name):
        self.name = name
        self.writers = []
        self.readers = []
        self.gen = -1


class TL:
    __slots__ = ("ap", "b")

    def __init__(self, ap, name):
        self.ap = ap
        self.b = Buf(name)

    def __getitem__(self, k):
        return self.ap[k]


class Op:
    __slots__ = ("eng", "fn", "deps", "dma", "sem", "val", "needed", "idx")


_GEN = [0]


class Prog:
    def __init__(self, nc):
        self.nc = nc
        self.ops = {e: [] for e in ENGS}
        self.nops = 0
        self.dma_sems = {}
        _GEN[0] += 1
        self.gen = _GEN[0]

    def op(self, eng, fn, reads=(), writes=(), dma_key=None, partial=False):
        o = Op()
        o.eng = eng
        o.fn = fn
        o.dma = dma_key
        o.needed = False
        o.idx = self.nops
        self.nops += 1
        deps = []
        for t in list(reads) + list(writes):
            b = t.b
            if b.gen != self.gen:
                b.gen = self.gen
                b.writers = []
                b.readers = []
        for t in reads:
            deps.extend(t.b.writers)
        for t in writes:
            deps.extend(t.b.readers)
            if not partial:
                deps.extend(t.b.writers)
        o.deps = deps
        for d in deps:
            d.needed = True
        for t in reads:
            t.b.readers.append(o)
        for t in writes:
            if partial:
                t.b.writers.append(o)
            else:
                t.b.writers = [o]
                t.b.readers = []
        self.ops[eng].append(o)
        return o

    def dma(self, out_ap, in_ap, reads=(), writes=(), key=None, eng="sync", partial=False, **kw):
        if key is None:
            key = (writes[0].b.name if writes else reads[0].b.name)
        return self.op(eng, lambda e: e.dma_start(out=out_ap, in_=in_ap, **kw), reads, writes,
                       dma_key=key, partial=partial)

    def finalize(self, stack):
        nc = self.nc
        pool = SEMPOOL[id(nc)]
        allops = sorted((o for e in ENGS for o in self.ops[e]), key=lambda o: o.idx)
        dmap = {}
        for o in allops:
            if o.dma is not None:
                if o.dma not in dmap:
                    dmap[o.dma] = len(dmap)
                    if len(pool["d"]) < len(dmap):
                        pool["d"].append([pool["stack"].enter_context(nc.semaphore("dq%d" % len(pool["d"]))), 0])
                ent = pool["d"][dmap[o.dma]]
                ent[1] += 16
                o.sem, o.val = ent[0], ent[1]
            elif o.needed:
                ent = pool["e"][o.eng]
                ent[1] += 1
                o.sem, o.val = ent[0], ent[1]
            else:
                o.sem, o.val = None, 0
        used = [pool["d"][i] for i in range(len(dmap))]
        block = stack.enter_context(nc.Block())

        def make(ename):
            ops = self.ops[ename]

            def body(eng):
                seen = {}
                for o in ops:
                    waits = {}
                    for d in o.deps:
                        if d.sem is None:
                            continue
                        k = id(d.sem)
                        if d.val > seen.get(k, 0) and d.val > waits.get(k, (None, 0))[1]:
                            waits[k] = (d.sem, d.val)
                    for k, (s, v) in waits.items():
                        eng.wait_ge(s, v)
                        seen[k] = v
                    ins = o.fn(eng)
                    if o.dma is not None:
                        ins.then_inc(o.sem, 16)
                    elif o.needed:
                        ins.then_inc(o.sem, 1)
                if ename == "sync":
                    for (s, v) in used:
                        if v > 0:
                            eng.wait_ge(s, v)

            return body

        block.tensor(make("tensor"))
        block.vector(make("vector"))
        block.scalar(make("scalar"))
        block.gpsimd(make("gpsimd"))
        block.sync(make("sync"))


SEMPOOL = {}


def init_sempool(nc, stack):
    SEMPOOL[id(nc)] = {"stack": stack, "d": [],
                       "e": {e: [stack.enter_context(nc.semaphore("se_" + e)), 0] for e in ENGS}}


class Ring:
    def __init__(self, tiles):
        self.tiles = tiles
        self.i = 0

    def next(self):
        t = self.tiles[self.i % len(self.tiles)]
        self.i += 1
        return t


class Cfg:
    def __init__(self, S=8192, C=256, debug=False, phases="012345"):
        self.S, self.C, self.debug, self.phases = S, C, debug, phases
        self.T = S // 2
        self.NT = C + S
        self.NTP = self.NT + 9


def host_consts(cfg):
    c = {}
    c["ident"] = np.eye(128, dtype=np.float32)
    i = np.arange(64)[:, None]
    j = np.arange(64)[None, :]
    f = np.float32
    NEG = -30000.0
    mats = [
        np.eye(64),
        (i <= j), (i >= j),
        -(i <= j).astype(f), -(i >= j).astype(f),
        np.ones((64, 64)), -np.ones((64, 64)),
        np.where(i > j, 0.0, NEG),
        np.where(j >= i, 0.0, NEG),
        np.where(i < j, 0.0, NEG),
        np.where(j <= i, 0.0, NEG),
        np.zeros((64, 64)), np.zeros((64, 64)),
    ]
    c["dnc"] = np.ascontiguousarray(np.concatenate([np.asarray(m_, f) for m_ in mats], axis=1))
    return c


def rope_tables(cfg, h):
    S, C = cfg.S, cfg.C
    half = 32
    inv = (10000.0 ** (-np.arange(0, half, 2, dtype=np.float32) / half)).astype(np.float32)
    t = np.arange(S, dtype=np.float32)
    row = np.floor(t / 64.0).astype(np.float32)
    col = (t - row * 64.0).astype(np.float32)
    ang_r = row[:, None] * inv[None, :]
    ang_c = col[:, None] * inv[None, :]
    ang = np.concatenate([ang_r, ang_r, ang_c, ang_c], axis=-1).astype(np.float32)
    cos = np.cos(ang).astype(np.float32)
    sin = np.sin(ang).astype(np.float32)
    sign = np.repeat(np.array([-1.0, 1.0, -1.0, 1.0], np.float32), 16)
    sin = sin * sign[None, :]
    if h == 1:
        cos = cos[::-1]
        sin = sin[::-1]
    cos = np.concatenate([np.ones((C, 64), np.float32), cos], 0)
    sin = np.concatenate([np.zeros((C, 64), np.float32), sin], 0)
    cosT = np.ascontiguousarray(np.concatenate([cos.T, cos.T], 0))
    sinT = np.ascontiguousarray(np.concatenate([sin.T, sin.T], 0))
    return cosT, sinT


def relayout_win(w_in, h):
    cuts = np.cumsum([1024, 1024, 1024, 1024, 1024, 1024, 1024, 8, 8, 8, 8, 1024, 1024])[:-1]
    p = np.split(w_in, cuts, axis=1)

    def perm(w):
        w4 = w.reshape(w.shape[0], 16, 4, 16)
        return w4[:, :, [1, 0, 3, 2], :].reshape(w.shape[0], 1024)

    if h == 0:
        gates = [p[7], p[9], p[8], p[10]]
    else:
        gates = [p[9], p[7], p[10], p[8]]
    cols = [p[0], perm(p[0]), p[1], perm(p[1]), p[2], p[3], p[4], p[5], p[6]] + gates + [p[11], p[12]]
    return np.ascontiguousarray(np.concatenate(cols, axis=1))


def core_inputs(cfg, inputs, b, h):
    S, C = cfg.S, cfg.C
    x = inputs["x"][b]
    ctx = inputs["ctx"][b]
    if h == 1:
        x = x[::-1]
        ctx = ctx[::-1]
    m = {}
    m["xs"] = np.ascontiguousarray(np.concatenate([ctx, x], 0))
    m["cvec"] = np.ascontiguousarray(np.stack([inputs["c"][b], inputs["c_ctx"]], 0))
    m["w_ada"] = np.ascontiguousarray(inputs["w_ada"][0])
    m["b_ada"] = np.ascontiguousarray(inputs["b_ada"][0][None, :])
    m["win"] = relayout_win(inputs["w_in"][0], h)
    cw = inputs["dn_conv_w"][0]
    if h == 1:
        cw = cw[::-1]
    m["convw"] = np.ascontiguousarray(cw.T)
    al = inputs["dn_a_log"][0]
    dtb = inputs["dn_dt_bias"][0]
    if h == 1:
        al = al[::-1]
        dtb = dtb[::-1]
    m["alog"] = np.ascontiguousarray(al.reshape(1, 16))
    m["dtb"] = np.ascontiguousarray(dtb.reshape(1, 16))
    m["dnnorm"] = np.ascontiguousarray(inputs["dn_norm_g"][0])
    m["w_brda"] = np.ascontiguousarray(inputs["w_br_da"][0])
    m["w_brdn"] = np.ascontiguousarray(inputs["w_br_dn"][0])
    m["w_o"] = np.ascontiguousarray(inputs["w_o"][0])
    m["w_m1"] = np.ascontiguousarray(inputs["w_mlp1"][0])
    m["w_m2"] = np.ascontiguousarray(inputs["w_mlp2"][0])
    m["lnp"] = np.ascontiguousarray(np.stack([inputs["ln1_g"][0], inputs["ln1_b"][0], inputs["ln2_g"][0], inputs["ln2_b"][0]], 0))
    m["dalam"] = np.ascontiguousarray(inputs["da_lambda"][0].reshape(256))
    m["subln"] = np.ascontiguousarray(inputs["da_subln_g"][0])
    cosT, sinT = rope_tables(cfg, h)
    m["cosT"] = cosT
    m["sinT"] = sinT
    for k, v in host_consts(cfg).items():
        m[k] = v
    return m


def build(cfg):
    S, C, T, NT, NTP = cfg.S, cfg.C, cfg.T, cfg.NT, cfg.NTP
    dbg = cfg.debug
    nc = bass.Bass("TRN2", target_bir_lowering=False)

    def din(name, shape, dt=F32):
        return TL(nc.dram_tensor(name, list(shape), dt, kind="ExternalInput").ap(), name)

    def dscr(name, shape, dt):
        kind = "ExternalOutput" if dbg else "Internal"
        return TL(nc.dram_tensor(name, list(shape), dt, kind=kind).ap(), name)

    xs = din("xs", [NT, D])
    cvec = din("cvec", [2, D])
    w_ada = din("w_ada", [D, 6 * D])
    b_ada = din("b_ada", [1, 6 * D])
    win = din("win", [D, NW])
    convw = din("convw", [3072, 7])
    alog = din("alog", [1, 16])
    dtb = din("dtb", [1, 16])
    cosT = din("cosT", [128, NT])
    sinT = din("sinT", [128, NT])
    ident_d = din("ident", [128, 128])
    dalam = din("dalam", [256])
    subln = din("subln", [128])
    ODA_T = dscr("ODA_T", [D, T], BF16)
    ODN_T = dscr("ODN_T", [D, T], BF16)
    OA_s = dscr("OA_s", [T, D], F32)
    dnc = din("dnc", [64, 13 * 64])
    w_brda = din("w_brda", [D, D])
    w_brdn = din("w_brdn", [D, D])
    w_o = din("w_o", [D, D])
    w_m1 = din("w_m1", [D, 4 * D])
    w_m2 = din("w_m2", [4 * D, D])
    lnp = din("lnp", [4, D])
    H1_s = dscr("H1_s", [T, D], F32)
    U2T_s = dscr("U2T_s", [D, T], BF16)
    out_d = TL(nc.dram_tensor("out", [T, D], F32, kind="ExternalOutput").ap(), "out")
    dnnorm = din("dnnorm", [128])

    QT_da = dscr("QT_da", [8, 128, T], BF16)
    KT_da = dscr("KT_da", [8, 128, NT], BF16)
    V_da = dscr("V_da", [8, 128, NT // 128, 128], BF16)
    QT_dn = dscr("QT_dn", [8, 128, NT], BF16)
    KT_dn = dscr("KT_dn", [8, 128, NT], BF16)
    K_tok = dscr("K_tok", [8, NT, 128], BF16)
    V_tok = dscr("V_tok", [8, NT, 128], BF16)
    G_s = dscr("G_s", [NT, 32], F32)
    Z_s = dscr("Z_s", [T, D], F32)
    MG_s = dscr("MG_s", [2, D, T], F32)

    def pos(t):
        return 3 + t if t < C else 6 + t

    groups = [(0, C, False)] + [(C + 512 * g, 512, g < T // 512) for g in range(S // 512)]

    top = contextlib.ExitStack()
    with top:
        init_sempool(nc, top)
        uniq = [0]

        def sb(st, name, shape, dt=F32):
            uniq[0] += 1
            name = "%s_%d" % (name, uniq[0])
            return TL(st.enter_context(nc.sbuf_tensor(name, list(shape), dt)), name)

        def ps(st, name, shape, dt=F32):
            uniq[0] += 1
            name = "%s_%d" % (name, uniq[0])
            return TL(st.enter_context(nc.psum_tensor(name, list(shape), dt)), name)

        ident = sb(top, "ident_sb", [128, 128])
        identb = sb(top, "identb", [128, 128], BF16)
        modT = sb(top, "modT", [128, 48, 2])
        sc1 = sb(top, "sc1", [128, 8, 2])
        sc2 = sb(top, "sc2", [128, 8, 2])
        grow = sb(top, "grow", [128, 2, D])
        epsb = sb(top, "epsb", [128, 4])

        if "0" in cfg.phases:
            with contextlib.ExitStack() as st:
                P = Prog(nc)
                cv = sb(st, "cv", [2, D])
                i2 = sb(st, "i2", [2, 2])
                ones1 = sb(st, "ones1", [1, 128])
                bada = sb(st, "bada", [1, 6 * D])
                scT = sb(st, "scT", [128, 8, 2])
                scB = sb(st, "scB", [128, 8, 128])
                wst = Ring([sb(st, "wst%d" % i, [128, 8, 512]) for i in range(2)])
                pcv = ps(st, "pcv", [128, 8, 2])
                pmod = ps(st, "pmod", [128, 48, 2])
                pbias = ps(st, "pbias", [128, 48])
                prow = Ring([ps(st, "prow%d" % i, [128, 512]) for i in range(2)])
                brow = sb(st, "brow", [128, 2, D])
                P.dma(ident.ap[:], ident_d.ap, writes=[ident])
                P.dma(cv.ap[:], cvec.ap, writes=[cv])
                P.dma(bada.ap[:], b_ada.ap, writes=[bada])
                P.dma(brow.ap[:, 0, :], b_ada.ap[0, 2 * D:3 * D].partition_broadcast(128), writes=[brow], partial=True, key="brow")
                P.dma(brow.ap[:, 1, :], b_ada.ap[0, 5 * D:6 * D].partition_broadcast(128), writes=[brow], partial=True, key="brow")
                P.op("gpsimd", lambda e: e.tensor_copy(out=identb.ap[:], in_=ident.ap[:]), [ident], [identb])
                P.op("vector", lambda e: e.tensor_copy(out=i2.ap[:], in_=ident.ap[0:2, 0:2]), [ident], [i2])
                P.op("vector", lambda e: e.memset(ones1.ap[:], 1.0), [], [ones1])
                for ci_, v_ in enumerate((1e-5, 1e-6, 128e-6, 1.0)):
                    P.op("vector", lambda e, ci_=ci_, v_=v_: e.memset(epsb.ap[:, ci_:ci_ + 1], v_), [], [epsb], partial=True)
                for k in range(KC):
                    P.op("tensor", lambda e, k=k: e.matmul(pcv.ap[:, k, :], lhsT=cv.ap[0:2, k * 128:(k + 1) * 128],
                                                           rhs=i2.ap[:], start=True, stop=True),
                         [cv, i2], [pcv], partial=True)
                for j in range(48):
                    P.op("tensor", lambda e, j=j: e.matmul(pbias.ap[:, j:j + 1], lhsT=bada.ap[0:1, j * 128:(j + 1) * 128],
                                                           rhs=ones1.ap[0:1, 0:1], start=True, stop=True),
                         [bada, ones1], [pbias], partial=True)
                P.op("scalar", lambda e: e.activation(out=scT.ap[:], in_=pcv.ap[:], func=AF.Silu), [pcv], [scT])
                P.op("vector", lambda e: e.tensor_copy(out=scB.ap[:], in_=scT.ap[:, :, 0:1].to_broadcast([128, 8, 128])),
                     [scT], [scB])
                for g in range(12):
                    w = wst.next()
                    P.dma(w.ap[:], w_ada.ap[:, g * 512:(g + 1) * 512].rearrange("(k p) c -> p k c", p=128), writes=[w])
                    for j4 in range(4):
                        j = g * 4 + j4
                        for k in range(KC):
                            P.op("tensor", lambda e, w=w, j=j, j4=j4, k=k: e.matmul(
                                pmod.ap[:, j, :], lhsT=w.ap[:, k, j4 * 128:(j4 + 1) * 128], rhs=scT.ap[:, k, :],
                                start=(k == 0), stop=(k == KC - 1)), [w, scT], [pmod], partial=True)
                    if g in (4, 5, 10, 11):
                        pr = prow.next()
                        for k in range(KC):
                            P.op("tensor", lambda e, w=w, k=k, pr=pr: e.matmul(
                                pr.ap[:], lhsT=scB.ap[:, k, :], rhs=w.ap[:, k, :], start=(k == 0), stop=(k == KC - 1)),
                                 [w, scB], [pr], partial=(k > 0))
                        gi, hf = (0, g - 4) if g < 6 else (1, g - 10)
                        P.op("vector", lambda e, pr=pr, gi=gi, hf=hf: e.tensor_tensor(
                            out=grow.ap[:, gi, hf * 512:(hf + 1) * 512], in0=pr.ap[:],
                            in1=brow.ap[:, gi, hf * 512:(hf + 1) * 512], op=ALU.add), [pr, brow], [grow], partial=True)
                P.op("vector", lambda e: e.tensor_copy(out=modT.ap[:], in_=pmod.ap[:]), [pmod], [modT])
                P.op("vector", lambda e: e.tensor_tensor(out=modT.ap[:], in0=modT.ap[:],
                                                         in1=pbias.ap[:].unsqueeze(2).to_broadcast([128, 48, 2]), op=ALU.add),
                     [modT, pbias], [modT])
                P.op("vector", lambda e: e.tensor_scalar_add(out=sc1.ap[:], in0=modT.ap[:, 8:16, :], scalar1=1.0),
                     [modT], [sc1])
                P.op("vector", lambda e: e.tensor_scalar_add(out=sc2.ap[:], in0=modT.ap[:, 32:40, :], scalar1=1.0),
                     [modT], [sc2])
                if dbg:
                    d_mod = dscr("d_mod", [128, 48, 2], F32)
                    d_grow = dscr("d_grow", [128, 2, D], F32)
                    P.dma(d_mod.ap, modT.ap[:], reads=[modT], writes=[d_mod])
                    P.dma(d_grow.ap, grow.ap[:], reads=[grow], writes=[d_grow])
                P.finalize(st)

        ph1 = contextlib.ExitStack()
        with ph1:
            uT = sb(ph1, "uT", [128, KC, NTP], BF16)
            if "1" in cfg.phases:
                with contextlib.ExitStack() as st:
                    P = Prog(nc)
                    xt = Ring([sb(st, "xt%d" % i, [128, D]) for i in range(3)])
                    xn = Ring([sb(st, "xn%d" % i, [128, D], BF16) for i in range(2)])
                    stt = Ring([sb(st, "stt%d" % i, [128, 2, 6]) for i in range(2)])
                    mv = Ring([sb(st, "mv%d" % i, [128, 2]) for i in range(2)])
                    rs = Ring([sb(st, "rs%d" % i, [128, 2]) for i in range(2)])
                    ptr = Ring([ps(st, "ptr%d" % i, [128, KC, 128], BF16) for i in range(3)])
                    for (c0, n) in ((0, 3), (3 + C, 3), (6 + NT, 3)):
                        P.op("gpsimd", lambda e, c0=c0, n=n: e.memset(uT.ap[:, :, c0:c0 + n], 0.0), [], [uT], partial=True)
                    for i in range(NT // 128):
                        t0 = 128 * i
                        s = 1 if t0 < C else 0
                        x_ = xt.next(); xn_ = xn.next(); st_ = stt.next(); mv_ = mv.next(); rs_ = rs.next(); pt = ptr.next()
                        P.dma(x_.ap[:], xs.ap[t0:t0 + 128, :], writes=[x_])
                        for hh in range(2):
                            P.op("vector", lambda e, x_=x_, st_=st_, hh=hh: e.bn_stats(
                                out=st_.ap[:, hh, :], in_=x_.ap[:, hh * 512:(hh + 1) * 512]), [x_], [st_], partial=(hh > 0))
                        P.op("vector", lambda e, st_=st_, mv_=mv_: e.bn_aggr(out=mv_.ap[:], in_=st_.ap[:].rearrange("p a b -> p (a b)")), [st_], [mv_])
                        P.op("scalar", lambda e, mv_=mv_, rs_=rs_: e.activation(
                            out=rs_.ap[:, 0:1], in_=mv_.ap[:, 1:2], func=AF.Ln, bias=epsb.ap[:, 0:1]), [mv_, epsb], [rs_])
                        P.op("scalar", lambda e, rs_=rs_: e.activation(
                            out=rs_.ap[:, 0:1], in_=rs_.ap[:, 0:1], func=AF.Exp, scale=-0.5), [rs_], [rs_])
                        P.op("vector", lambda e, mv_=mv_, rs_=rs_: e.tensor_scalar(
                            out=rs_.ap[:, 1:2], in0=mv_.ap[:, 0:1], scalar1=-1.0, scalar2=rs_.ap[:, 0:1],
                            op0=ALU.mult, op1=ALU.mult), [mv_, rs_], [rs_], partial=True)
                        P.op("scalar", lambda e, x_=x_, xn_=xn_, rs_=rs_: e.activation(
                            out=xn_.ap[:], in_=x_.ap[:], func=AF.Identity, scale=rs_.ap[:, 0:1], bias=rs_.ap[:, 1:2]),
                             [x_, rs_], [xn_])
                        for k in range(KC):
                            P.op("tensor", lambda e, xn_=xn_, pt=pt, k=k: e.transpose(
                                pt.ap[:, k, :], xn_.ap[:, k * 128:(k + 1) * 128], identb.ap[:]),
                                 [xn_, identb], [pt], partial=(k > 0))
                        col = pos(t0)
                        for k in range(KC):
                            P.op("vector", lambda e, pt=pt, k=k, col=col, s=s: e.tensor_scalar(
                                out=uT.ap[:, k, col:col + 128], in0=pt.ap[:, k, :], scalar1=sc1.ap[:, k, s:s + 1],
                                scalar2=modT.ap[:, k, s:s + 1], op0=ALU.mult, op1=ALU.add),
                                 [pt, sc1, modT], [uT], partial=True)
                    if dbg:
                        d_uT = dscr("d_uT", [128, KC, NTP], BF16)
                        P.dma(d_uT.ap, uT.ap[:], reads=[uT], writes=[d_uT])
                    P.finalize(st)

            if "2" in cfg.phases:
                with contextlib.ExitStack() as st:
                    P = Prog(nc)
                    phase1b(nc, cfg, P, st, sb, ps, locals())
                    P.finalize(st)
        if "3" in cfg.phases:
            with contextlib.ExitStack() as st:
                P = Prog(nc)
                phase2(nc, cfg, P, st, sb, ps, locals())
                P.finalize(st)
        if "4" in cfg.phases:
            with contextlib.ExitStack() as st:
                P = Prog(nc)
                phase3(nc, cfg, P, st, sb, ps, locals())
                P.finalize(st)
        if "5" in cfg.phases:
            with contextlib.ExitStack() as st:
                P = Prog(nc)
                phase4a(nc, cfg, P, st, sb, ps, locals())
                P.finalize(st)
            with contextlib.ExitStack() as st:
                P = Prog(nc)
                phase4b(nc, cfg, P, st, sb, ps, locals())
                P.finalize(st)
    return nc


ALPHA = 2.0 ** 0.25


def ln_tile(P, src, dst, stt, mv, rs, epsb, gb=None, tmp=None):
    for hh in range(2):
        P.op("vector", lambda e, hh=hh: e.bn_stats(out=stt.ap[:, hh, :], in_=src.ap[:, hh * 512:(hh + 1) * 512]), [src], [stt],
             partial=(hh > 0))
    P.op("vector", lambda e: e.bn_aggr(out=mv.ap[:], in_=stt.ap[:].rearrange("p a b -> p (a b)")), [stt], [mv])
    P.op("scalar", lambda e: e.activation(out=rs.ap[:, 0:1], in_=mv.ap[:, 1:2], func=AF.Ln, bias=epsb.ap[:, 0:1]), [mv, epsb], [rs])
    P.op("scalar", lambda e: e.activation(out=rs.ap[:, 0:1], in_=rs.ap[:, 0:1], func=AF.Exp, scale=-0.5), [rs], [rs])
    P.op("vector", lambda e: e.tensor_scalar(out=rs.ap[:, 1:2], in0=mv.ap[:, 0:1], scalar1=-1.0, scalar2=rs.ap[:, 0:1],
                                             op0=ALU.mult, op1=ALU.mult), [mv, rs], [rs], partial=True)
    if gb is None:
        P.op("scalar", lambda e: e.activation(out=dst.ap[:], in_=src.ap[:], func=AF.Identity, scale=rs.ap[:, 0:1], bias=rs.ap[:, 1:2]),
             [src, rs], [dst])
    else:
        P.op("scalar", lambda e: e.activation(out=tmp.ap[:], in_=src.ap[:], func=AF.Identity, scale=rs.ap[:, 0:1], bias=rs.ap[:, 1:2]),
             [src, rs], [tmp])
        P.op("gpsimd", lambda e: e.tensor_tensor(out=tmp.ap[:], in0=tmp.ap[:], in1=gb.ap[:, 0, :], op=ALU.mult), [tmp, gb], [tmp])
        P.op("vector", lambda e: e.tensor_tensor(out=dst.ap[:], in0=tmp.ap[:], in1=gb.ap[:, 1, :], op=ALU.add), [tmp, gb], [dst])


def load_weight_rows(P, wst, src, dst_of, nblk, rows_of):
    for i in range(nblk):
        w = wst.next()
        P.dma(w.ap[:], rows_of(i), writes=[w])
        dst, dap = dst_of(i)
        P.op("gpsimd", lambda e, w=w, dap=dap: e.tensor_copy(out=dap, in_=w.ap[:]), [w], [dst], partial=True)


def phase4a(nc, cfg, P, st, sb, ps, env):
    S, C, T, NT = cfg.S, cfg.C, cfg.T, cfg.NT
    xs, ODA_T, ODN_T, MG_s, H1_s, U2T_s, w_brda, w_brdn, w_o, lnp, grow, sc2, modT, epsb, identb = (env[k] for k in (
        "xs", "ODA_T", "ODN_T", "MG_s", "H1_s", "U2T_s", "w_brda", "w_brdn", "w_o", "lnp", "grow", "sc2", "modT", "epsb", "identb"))
    wst = Ring([sb(st, "wst%d" % i, [128, D]) for i in range(3)])
    wda = sb(st, "wda", [128, KC, D], BF16); wdn = sb(st, "wdn", [128, KC, D], BF16); wo = sb(st, "wo", [128, KC, D], BF16)
    for (src, dst) in ((w_brda, wda), (w_brdn, wdn), (w_o, wo)):
        load_weight_rows(P, wst, src, lambda i, dst=dst: (dst, dst.ap[:, i, :]), KC, lambda i, src=src: src.ap[i * 128:(i + 1) * 128, :])
    ln1 = sb(st, "ln1", [128, 2, D])
    P.dma(ln1.ap[:, 0, :], lnp.ap[0, :].partition_broadcast(128), writes=[ln1], partial=True, key="ln1")
    P.dma(ln1.ap[:, 1, :], lnp.ap[1, :].partition_broadcast(128), writes=[ln1], partial=True, key="ln1")
    oda = Ring([sb(st, "oda%d" % i, [128, KC, 512], BF16) for i in range(2)])
    odn = Ring([sb(st, "odn%d" % i, [128, KC, 512], BF16) for i in range(2)])
    mg = Ring([sb(st, "mg%d" % i, [128, 2, 512]) for i in range(2)])
    t1 = Ring([sb(st, "t1_%d" % i, [128, 512]) for i in range(2)])
    t2 = Ring([sb(st, "t2_%d" % i, [128, 512]) for i in range(2)])
    yT = Ring([sb(st, "yT%d" % i, [128, KC, 512], BF16) for i in range(2)])
    xt = Ring([sb(st, "xt%d" % i, [128, D]) for i in range(2)])
    hx = Ring([sb(st, "hx%d" % i, [128, D]) for i in range(1)])
    vv = Ring([sb(st, "vv%d" % i, [128, D]) for i in range(2)])
    h1 = Ring([sb(st, "h1_%d" % i, [128, D]) for i in range(2)])
    h1b = Ring([sb(st, "h1b%d" % i, [128, D], BF16) for i in range(1)])
    tmp = Ring([sb(st, "tmp%d" % i, [128, D]) for i in range(1)])
    u2 = Ring([sb(st, "u2_%d" % i, [128, KC, 128], BF16) for i in range(2)])
    stt = Ring([sb(st, "stt%d" % i, [128, 2, 6]) for i in range(4)])
    mv = Ring([sb(st, "mv%d" % i, [128, 2]) for i in range(4)])
    rs = Ring([sb(st, "rs%d" % i, [128, 2]) for i in range(4)])
    PDA = Ring([ps(st, "PDA%d" % i, [128, 512]) for i in range(2)])
    PDN = Ring([ps(st, "PDN%d" % i, [128, 512]) for i in range(2)])
    PY = Ring([ps(st, "PY%d" % i, [128, 1024]) for i in range(1)])
    PT = Ring([ps(st, "PT%d" % i, [128, KC, 128], BF16) for i in range(2)])

    for g in range(T // 512):
        t0 = g * 512
        a_ = oda.next(); n_ = odn.next(); y_ = yT.next()
        P.dma(a_.ap[:], ODA_T.ap[:, t0:t0 + 512].rearrange("(k p) t -> p k t", p=128), writes=[a_])
        P.dma(n_.ap[:], ODN_T.ap[:, t0:t0 + 512].rearrange("(k p) t -> p k t", p=128), writes=[n_])
        for c in range(KC):
            pda = PDA.next(); pdn = PDN.next(); m_ = mg.next(); a1 = t1.next(); a2 = t2.next()
            P.dma(m_.ap[:], MG_s.ap[:, c * 128:(c + 1) * 128, t0:t0 + 512].rearrange("g p t -> p g t"), writes=[m_])
            for k in range(KC):
                P.op("tensor", lambda e, k=k, c=c, pda=pda, a_=a_: e.matmul(pda.ap[:], lhsT=wda.ap[:, k, c * 128:(c + 1) * 128],
                                                                           rhs=a_.ap[:, k, :], start=(k == 0), stop=(k == KC - 1)),
                     [wda, a_], [pda], partial=(k > 0))
            for k in range(KC):
                P.op("tensor", lambda e, k=k, c=c, pdn=pdn, n_=n_: e.matmul(pdn.ap[:], lhsT=wdn.ap[:, k, c * 128:(c + 1) * 128],
                                                                           rhs=n_.ap[:, k, :], start=(k == 0), stop=(k == KC - 1)),
                     [wdn, n_], [pdn], partial=(k > 0))
            P.op("vector", lambda e, pda=pda, m_=m_, a1=a1: e.tensor_tensor(out=a1.ap[:], in0=pda.ap[:], in1=m_.ap[:, 0, :], op=ALU.mult),
                 [pda, m_], [a1])
            P.op("vector", lambda e, pdn=pdn, m_=m_, a2=a2: e.tensor_tensor(out=a2.ap[:], in0=pdn.ap[:], in1=m_.ap[:, 1, :], op=ALU.mult),
                 [pdn, m_], [a2])
            P.op("gpsimd", lambda e, a1=a1, a2=a2, y_=y_, c=c: e.tensor_tensor(out=y_.ap[:, c, :], in0=a1.ap[:], in1=a2.ap[:], op=ALU.add),
                 [a1, a2], [y_], partial=True)
        for j in range(4):
            tk = t0 + 128 * j
            py = PY.next(); x_ = xt.next(); hx_ = hx.next(); v_ = vv.next(); h1_ = h1.next(); hb_ = h1b.next(); tm_ = tmp.next()
            u2_ = u2.next(); pt = PT.next()
            for hf in range(2):
                for k in range(KC):
                    P.op("tensor", lambda e, k=k, hf=hf, py=py, y_=y_, j=j: e.matmul(
                        py.ap[:, hf * 512:(hf + 1) * 512], lhsT=y_.ap[:, k, 128 * j:128 * j + 128], rhs=wo.ap[:, k, hf * 512:(hf + 1) * 512],
                        start=(k == 0), stop=(k == KC - 1)), [y_, wo], [py], partial=not (hf == 0 and k == 0))
            P.dma(x_.ap[:], xs.ap[C + tk:C + tk + 128, :], writes=[x_])
            ln_tile(P, x_, hx_, stt.next(), mv.next(), rs.next(), epsb)
            P.op("vector", lambda e, py=py, v_=v_: e.tensor_tensor(out=v_.ap[:], in0=py.ap[:], in1=grow.ap[:, 0, :], op=ALU.mult),
                 [py, grow], [v_])
            P.op("vector", lambda e, hx_=hx_, v_=v_: e.scalar_tensor_tensor(out=v_.ap[:], in0=hx_.ap[:], scalar=ALPHA, in1=v_.ap[:],
                                                                          op0=ALU.mult, op1=ALU.add), [hx_, v_], [v_])
            ln_tile(P, v_, h1_, stt.next(), mv.next(), rs.next(), epsb, gb=ln1, tmp=tm_)
            P.dma(H1_s.ap[tk:tk + 128, :], h1_.ap[:], reads=[h1_], writes=[H1_s], partial=True, key=h1_.b.name + "s", eng="scalar")
            P.op("scalar", lambda e, h1_=h1_, hb_=hb_: e.copy(out=hb_.ap[:], in_=h1_.ap[:]), [h1_], [hb_])
            for k in range(KC):
                P.op("tensor", lambda e, k=k, hb_=hb_, pt=pt: e.transpose(pt.ap[:, k, :], hb_.ap[:, k * 128:(k + 1) * 128], identb.ap[:]),
                     [hb_, identb], [pt], partial=(k > 0))
            for k in range(KC):
                P.op("vector", lambda e, k=k, pt=pt, u2_=u2_: e.tensor_scalar(
                    out=u2_.ap[:, k, :], in0=pt.ap[:, k, :], scalar1=sc2.ap[:, k, 0:1], scalar2=modT.ap[:, 24 + k, 0:1],
                    op0=ALU.mult, op1=ALU.add), [pt, sc2, modT], [u2_], partial=(k > 0))
            P.dma(U2T_s.ap[:, tk:tk + 128].rearrange("(k p) t -> p k t", p=128), u2_.ap[:], reads=[u2_], writes=[U2T_s], partial=True,
                  key=u2_.b.name + "s", eng="scalar")


def phase4b(nc, cfg, P, st, sb, ps, env):
    S, C, T, NT = cfg.S, cfg.C, cfg.T, cfg.NT
    H1_s, U2T_s, w_m1, w_m2, lnp, grow, epsb, out_d = (env[k] for k in (
        "H1_s", "U2T_s", "w_m1", "w_m2", "lnp", "grow", "epsb", "out_d"))
    TG = 256
    wst = Ring([sb(st, "wst%d" % i, [128, D]) for i in range(2)])
    W1 = sb(st, "W1", [128, KC, 4 * D], BF16)
    W2 = sb(st, "W2", [128, 32, D], BF16)
    load_weight_rows(P, wst, w_m1, lambda i: (W1, W1.ap[:, i // 4, (i % 4) * D:(i % 4 + 1) * D]), 32,
                     lambda i: w_m1.ap[(i // 4) * 128:(i // 4 + 1) * 128, (i % 4) * D:(i % 4 + 1) * D])
    load_weight_rows(P, wst, w_m2, lambda i: (W2, W2.ap[:, i, :]), 32, lambda i: w_m2.ap[i * 128:(i + 1) * 128, :])
    ln2 = sb(st, "ln2", [128, 2, D])
    P.dma(ln2.ap[:, 0, :], lnp.ap[2, :].partition_broadcast(128), writes=[ln2], partial=True, key="ln2")
    P.dma(ln2.ap[:, 1, :], lnp.ap[3, :].partition_broadcast(128), writes=[ln2], partial=True, key="ln2")
    u2 = Ring([sb(st, "u2g%d" % i, [128, KC, TG], BF16) for i in range(2)])
    hT = sb(st, "hT", [128, 32, TG], BF16)
    rl = Ring([sb(st, "rl%d" % i, [128, TG], BF16) for i in range(3)])
    h1 = Ring([sb(st, "h1_%d" % i, [128, D]) for i in range(1)])
    vv = Ring([sb(st, "vv%d" % i, [128, D]) for i in range(1)])
    oo = Ring([sb(st, "oo%d" % i, [128, D]) for i in range(2)])
    tmp = Ring([sb(st, "tmp%d" % i, [128, D]) for i in range(1)])
    stt = Ring([sb(st, "stt%d" % i, [128, 2, 6]) for i in range(2)])
    mv = Ring([sb(st, "mv%d" % i, [128, 2]) for i in range(2)])
    rs = Ring([sb(st, "rs%d" % i, [128, 2]) for i in range(2)])
    PH = Ring([ps(st, "PH%d" % i, [128, 512]) for i in range(4)])
    PO = Ring([ps(st, "PO%d" % i, [128, 1024]) for i in range(2)])
    for g in range(T // TG):
        t0 = g * TG
        u_ = u2.next()
        P.dma(u_.ap[:], U2T_s.ap[:, t0:t0 + TG].rearrange("(k p) t -> p k t", p=128), writes=[u_])
        for fc in range(32):
            ph = PH.next(); r_ = rl.next()
            for k in range(KC):
                P.op("tensor", lambda e, k=k, fc=fc, ph=ph, u_=u_: e.matmul(ph.ap[:, 0:TG], lhsT=W1.ap[:, k, fc * 128:(fc + 1) * 128],
                                                                           rhs=u_.ap[:, k, :], start=(k == 0), stop=(k == KC - 1)),
                     [W1, u_], [ph], partial=(k > 0))
            P.op("scalar", lambda e, ph=ph, r_=r_: e.activation(out=r_.ap[:], in_=ph.ap[:, 0:TG], func=AF.Relu), [ph], [r_])
            P.op("gpsimd", lambda e, r_=r_, fc=fc: e.tensor_tensor(out=hT.ap[:, fc, :], in0=r_.ap[:], in1=r_.ap[:], op=ALU.mult),
                 [r_], [hT], partial=True)
        for j in range(TG // 128):
            tk = t0 + 128 * j
            po = PO.next(); h_ = h1.next(); v_ = vv.next(); o_ = oo.next(); tm_ = tmp.next()
            for hf in range(2):
                for fc in range(32):
                    P.op("tensor", lambda e, fc=fc, hf=hf, po=po, j=j: e.matmul(
                        po.ap[:, hf * 512:(hf + 1) * 512], lhsT=hT.ap[:, fc, 128 * j:128 * j + 128], rhs=W2.ap[:, fc, hf * 512:(hf + 1) * 512],
                        start=(fc == 0), stop=(fc == 31)), [hT, W2], [po], partial=not (hf == 0 and fc == 0))
            P.dma(h_.ap[:], H1_s.ap[tk:tk + 128, :], writes=[h_])
            P.op("vector", lambda e, po=po, v_=v_: e.tensor_tensor(out=v_.ap[:], in0=po.ap[:], in1=grow.ap[:, 1, :], op=ALU.mult),
                 [po, grow], [v_])
            P.op("vector", lambda e, h_=h_, v_=v_: e.scalar_tensor_tensor(out=v_.ap[:], in0=h_.ap[:], scalar=ALPHA, in1=v_.ap[:],
                                                                        op0=ALU.mult, op1=ALU.add), [h_, v_], [v_])
            ln_tile(P, v_, o_, stt.next(), mv.next(), rs.next(), epsb, gb=ln2, tmp=tm_)
            P.dma(out_d.ap[tk:tk + 128, :], o_.ap[:], reads=[o_], writes=[out_d], partial=True, key=o_.b.name + "s", eng="scalar")


def phase3(nc, cfg, P, st, sb, ps, env):
    S, C, T, NT = cfg.S, cfg.C, cfg.T, cfg.NT
    QT_dn, KT_dn, K_tok, V_tok, G_s, Z_s, OA_s, ODN_T, dnc, dnnorm, identb, epsb = (env[k] for k in (
        "QT_dn", "KT_dn", "K_tok", "V_tok", "G_s", "Z_s", "OA_s", "ODN_T", "dnc", "dnnorm", "identb", "epsb"))
    NCH = NT // 64
    CH_CTX = C // 64
    CH_OWN_END = (C + T) // 64

    cst = sb(st, "cst", [64, 13 * 64])
    cm = lambda i: cst.ap[:, 64 * i:64 * (i + 1)]
    negm = sb(st, "negm", [64, 4, 8, 64])
    ones_b = sb(st, "ones_b", [64, 128], BF16)
    ones_f = sb(st, "ones_f", [64, 128])
    gnorm = sb(st, "gnorm", [128, 128])
    P.dma(cst.ap[:], dnc.ap, writes=[cst])
    P.dma(gnorm.ap[:], dnnorm.ap.partition_broadcast(128), writes=[gnorm])
    for i in range(4):
        P.op("vector", lambda e, i=i: e.tensor_copy(out=negm.ap[:, i], in_=cm(7 + i).unsqueeze(1).to_broadcast([64, 8, 64])),
             [cst], [negm], partial=True)
    P.op("vector", lambda e: e.memset(ones_b.ap[:], 1.0), [], [ones_b])
    P.op("vector", lambda e: e.memset(ones_f.ap[:], 1.0), [], [ones_f])

    Wt = [ps(st, "W%d" % i, [128, 1024]) for i in range(3)]
    Wlo = [TL(w.ap[:, 0:512], w.b.name + "lo") for w in Wt]
    Whi = [TL(w.ap[:, 512:1024], w.b.name + "hi") for w in Wt]
    N0 = ps(st, "N0", [128, 512])
    BT = ps(st, "BT", [128, 1024], BF16)

    _kr = Ring([sb(st, "kblk%d" % i, [128, 8, 512], BF16) for i in range(2)])
    _qr = Ring([sb(st, "qblk%d" % i, [128, 8, 512], BF16) for i in range(2)])
    kblk = {0: _kr, 1: _kr}
    qblk = {0: _qr, 1: _qr}
    oblk = Ring([sb(st, "oblk%d" % i, [128, 8, 512], BF16) for i in range(2)])
    NSET = 2

    def ring(name, shape, dt=F32, n=NSET):
        return Ring([sb(st, "%s%d" % (name, i), shape, dt) for i in range(n)])

    r_ktok = ring("ktok", [64, 8, 128], BF16); r_vtok = ring("vtok", [64, 8, 128], BF16); r_g = ring("g", [64, 32])
    r_G1 = ring("G1", [64, 8, 64]); r_G2 = ring("G2", [64, 8, 64])
    r_Dl = ring("Dl", [64, 8, 64]); r_Dq = ring("Dq", [64, 8, 64])
    r_sm = ring("sm", [64, 48]); r_gt = ring("gt", [128, 8])
    r_Ab = ring("Ab", [64, 8, 64]); r_A = ring("A", [64, 8, 64], BF16)
    r_X = ring("X", [64, 8, 64], BF16, 4); r_Y = ring("Y", [64, 8, 64], BF16, 4); r_R = ring("R", [64, 8, 64], BF16)
    r_vb = ring("vb", [64, 8, 128], BF16); r_kbe = ring("kbe", [64, 8, 128], BF16); r_kdec = ring("kdec", [64, 8, 128], BF16)
    r_qkT = ring("qkT", [64, 8, 64], BF16); r_rhsE = ring("rhsE", [64, 8, 64], BF16)
    r_u = ring("u", [64, 8, 128]); r_wT = ring("wT", [128, 8, 64], BF16); r_qdT = ring("qdT", [128, 8, 64], BF16)
    r_vnew = ring("vnew", [64, 8, 128], BF16)
    r_osb = ring("osb", [64, 8, 128]); r_oa = ring("oa", [64, 8, 128]); r_z = ring("z", [64, 8, 128])
    r_sq = ring("sq", [64, 8, 128]); r_ss = ring("ss", [64, 16]); r_on = ring("on", [64, 8, 128], BF16)
    Sf = [sb(st, "Sf%d" % d, [128, 8, 128]) for d in range(2)]
    Sb = [sb(st, "Sb%d" % d, [128, 8, 128], BF16) for d in range(2)]
    for d in range(2):
        P.op("gpsimd", lambda e, d=d: e.memset(Sf[d].ap[:], 0.0), [], [Sf[d]])
        P.op("gpsimd", lambda e, d=d: e.memset(Sb[d].ap[:], 0.0), [], [Sb[d]])

    blkstate = {}

    def get_blk(kind, d, ci):
        bi = ci // 8
        key = (kind, d)
        if blkstate.get(key, (None, None))[0] != bi:
            t = (kblk if kind == "k" else qblk)[d].next()
            src = KT_dn if kind == "k" else QT_dn
            t0 = bi * 512
            lo = t0 if kind == "k" else max(t0, C)
            hi = min(t0 + 512, NT) if kind == "k" else min(t0 + 512, C + T)
            P.dma(t.ap[:, :, lo - t0:hi - t0], src.ap[:, :, lo:hi].rearrange("h d t -> d h t"), writes=[t])
            blkstate[key] = (bi, t)
        return blkstate[key][1], (ci % 8) * 64

    def bcl(ap2, n):
        return ap2.unsqueeze(2).to_broadcast([ap2.shape[0], 8, n])

    def prep(ci, d, want_out):
        tok0 = ci * 64
        kb, ko = get_blk("k", d, ci)
        kT = lambda h: kb.ap[:, h, ko:ko + 64]
        if want_out:
            qb, qo = get_blk("q", d, ci)
        ktok = r_ktok.next(); vtok = r_vtok.next(); g = r_g.next()
        P.dma(ktok.ap[:], K_tok.ap[:, tok0:tok0 + 64, :].rearrange("h t d -> t h d"), writes=[ktok])
        P.dma(vtok.ap[:], V_tok.ap[:, tok0:tok0 + 64, :].rearrange("h t d -> t h d"), writes=[vtok])
        P.dma(g.ap[:], G_s.ap[tok0:tok0 + 64, :], writes=[g])
        gsel = g.ap[:, 8 * d:8 * d + 8]
        bsel = g.ap[:, 16 + 8 * d:16 + 8 * d + 8]
        Mi, NMi, NDl, NDq = (1, 3, 0, 1) if d == 0 else (2, 4, 2, 3)
        kk = N0; qk = Wlo[2]; P1 = Wlo[1]; P2 = Whi[1]; smp = Whi[2]
        for h in range(8):
            P.op("tensor", lambda e, h=h: e.matmul(kk.ap[0:64, 64 * h:64 * h + 64], lhsT=kT(h), rhs=kT(h), start=True, stop=True),
                 [kb], [kk], partial=(h > 0))
        if want_out:
            for h in range(8):
                P.op("tensor", lambda e, h=h: e.matmul(qk.ap[0:64, 64 * h:64 * h + 64], lhsT=kT(h), rhs=qb.ap[:, h, qo:qo + 64],
                                                       start=True, stop=True), [kb, qb], [qk], partial=(h > 0))
        G1 = r_G1.next(); G2 = r_G2.next(); Dl = r_Dl.next(); Dq = r_Dq.next(); sm = r_sm.next(); gt = r_gt.next()
        P.op("vector", lambda e: e.tensor_copy(out=G1.ap[:], in_=bcl(gsel, 64)), [g], [G1])
        P.op("vector", lambda e: e.tensor_tensor(out=G2.ap[:], in0=cm(Mi).unsqueeze(1).to_broadcast([64, 8, 64]), in1=bcl(gsel, 64),
                                                 op=ALU.mult), [g, cst], [G2])
        G1f = G1.ap[:].rearrange("p h j -> p (h j)"); G2f = G2.ap[:].rearrange("p h j -> p (h j)")
        P.op("tensor", lambda e: e.matmul(P1.ap[0:64, :], lhsT=cm(Mi), rhs=G1f, start=True, stop=False), [cst, G1], [P1])
        P.op("tensor", lambda e: e.matmul(P1.ap[0:64, :], lhsT=cm(6), rhs=G2f, start=False, stop=False), [cst, G2], [P1], partial=True)
        P.op("tensor", lambda e: e.matmul(P1.ap[0:64, :], lhsT=cm(0), rhs=negm.ap[:, NDl].rearrange("p h j -> p (h j)"),
                                          start=False, stop=True), [cst, negm], [P1], partial=True)
        if want_out:
            P.op("tensor", lambda e: e.matmul(P2.ap[0:64, :], lhsT=cm(NMi), rhs=G1f, start=True, stop=False), [cst, G1], [P2])
            P.op("tensor", lambda e: e.matmul(P2.ap[0:64, :], lhsT=cm(5), rhs=G2f, start=False, stop=False), [cst, G2], [P2], partial=True)
            P.op("tensor", lambda e: e.matmul(P2.ap[0:64, :], lhsT=cm(0), rhs=negm.ap[:, NDq].rearrange("p h j -> p (h j)"),
                                              start=False, stop=True), [cst, negm], [P2], partial=True)
        P.op("tensor", lambda e: e.matmul(smp.ap[0:64, 0:8], lhsT=cm(Mi), rhs=gsel, start=True, stop=True), [cst, g], [smp])
        P.op("tensor", lambda e: e.matmul(smp.ap[0:64, 8:16], lhsT=cm(5), rhs=gsel, start=True, stop=True), [cst, g], [smp], partial=True)
        P.op("tensor", lambda e: e.matmul(smp.ap[:, 16:24], lhsT=ones_f.ap[:], rhs=gsel, start=True, stop=True), [ones_f, g], [smp],
             partial=True)
        P.op("scalar", lambda e: e.activation(out=Dl.ap[:].rearrange("p h j -> p (h j)"), in_=P1.ap[0:64, :], func=AF.Exp), [P1], [Dl])
        if want_out:
            P.op("scalar", lambda e: e.activation(out=Dq.ap[:].rearrange("p h j -> p (h j)"), in_=P2.ap[0:64, :], func=AF.Exp), [P2], [Dq])
        P.op("vector", lambda e: e.tensor_copy(out=sm.ap[:, 0:16], in_=smp.ap[0:64, 0:16]), [smp], [sm])
        P.op("vector", lambda e: e.tensor_tensor(out=sm.ap[:, 8:16], in0=sm.ap[:, 8:16], in1=sm.ap[:, 0:8], op=ALU.subtract), [sm], [sm])
        P.op("scalar", lambda e: e.activation(out=sm.ap[:, 16:32], in_=sm.ap[:, 0:16], func=AF.Exp), [sm], [sm])
        P.op("scalar", lambda e: e.activation(out=gt.ap[:], in_=smp.ap[:, 16:24], func=AF.Exp), [smp], [gt])
        P.op("vector", lambda e: e.tensor_tensor(out=sm.ap[:, 32:40], in0=sm.ap[:, 16:24], in1=bsel, op=ALU.mult), [sm, g], [sm])
        ecg = sm.ap[:, 16:24]; ekd = sm.ap[:, 24:32]; be = sm.ap[:, 32:40]
        Ab = r_Ab.next(); A = r_A.next()
        P.op("gpsimd", lambda e: e.tensor_tensor(out=Ab.ap[:], in0=Dl.ap[:], in1=bcl(bsel, 64), op=ALU.mult), [Dl, g], [Ab])
        P.op("vector", lambda e: e.tensor_tensor(out=A.ap[:].rearrange("p h j -> p (h j)"), in0=kk.ap[0:64, :],
                                                 in1=Ab.ap[:].rearrange("p h j -> p (h j)"), op=ALU.mult), [kk, Ab], [A])
        vb = r_vb.next(); kbe = r_kbe.next(); kdec = r_kdec.next()
        P.op("gpsimd", lambda e: e.tensor_tensor(out=vb.ap[:], in0=vtok.ap[:], in1=bcl(bsel, 128), op=ALU.mult), [vtok, g], [vb])
        P.op("gpsimd", lambda e: e.tensor_tensor(out=kbe.ap[:], in0=ktok.ap[:], in1=bcl(be, 128), op=ALU.mult), [ktok, sm], [kbe])
        P.op("gpsimd", lambda e: e.tensor_tensor(out=kdec.ap[:], in0=ktok.ap[:], in1=bcl(ekd, 128), op=ALU.mult), [ktok, sm], [kdec])
        if want_out:
            qkT = r_qkT.next(); rhsE = r_rhsE.next()
            P.op("vector", lambda e: e.tensor_tensor(out=qkT.ap[:].rearrange("p h j -> p (h j)"), in0=qk.ap[0:64, :],
                                                     in1=Dq.ap[:].rearrange("p h j -> p (h j)"), op=ALU.mult), [qk, Dq], [qkT])
            P.op("gpsimd", lambda e: e.tensor_tensor(out=rhsE.ap[:], in0=cm(0).unsqueeze(1).to_broadcast([64, 8, 64]),
                                                     in1=bcl(ecg, 64), op=ALU.mult), [cst, sm], [rhsE])
        ATp = BT
        for h in range(8):
            P.op("tensor", lambda e, h=h: e.transpose(ATp.ap[0:64, 64 * h:64 * h + 64], A.ap[:, h, :], identb.ap[0:64, 0:64]),
                 [A, identb], [ATp], partial=(h > 0))
        X = r_X.next(); Y = r_Y.next(); R = r_R.next()
        P.op("scalar", lambda e, X=X: e.activation(out=X.ap[:].rearrange("p h j -> p (h j)"), in_=ATp.ap[0:64, 0:512], func=AF.Copy,
                                                   scale=-1.0), [ATp], [X])
        P.op("vector", lambda e, Y=Y: e.tensor_scalar_mul(out=Y.ap[:], in0=A.ap[:], scalar1=-1.0), [A], [Y])
        P.op("vector", lambda e, X=X: e.tensor_tensor(out=R.ap[:], in0=X.ap[:], in1=cm(0).unsqueeze(1).to_broadcast([64, 8, 64]),
                                                      op=ALU.add), [X, cst], [R])
        pX = Wlo[1]; pY = Whi[1]; pXR = N0
        for lvl in range(1, 6):
            Xn = r_X.next() if lvl < 5 else None
            Yn = r_Y.next()
            if lvl < 5:
                for h in range(8):
                    P.op("tensor", lambda e, h=h, X=X, Y=Y: e.matmul(pX.ap[0:64, 64 * h:64 * h + 64], lhsT=Y.ap[:, h, :], rhs=X.ap[:, h, :],
                                                                    start=True, stop=True), [X, Y], [pX], partial=(h > 0))
            for h in range(8):
                P.op("tensor", lambda e, h=h, X=X, Y=Y: e.matmul(pY.ap[0:64, 64 * h:64 * h + 64], lhsT=X.ap[:, h, :], rhs=Y.ap[:, h, :],
                                                                start=True, stop=True), [X, Y], [pY], partial=(h > 0))
            if lvl < 5:
                P.op("scalar", lambda e, Xn=Xn: e.copy(out=Xn.ap[:].rearrange("p h j -> p (h j)"), in_=pX.ap[0:64, :]), [pX], [Xn])
            P.op("vector", lambda e, Yn=Yn: e.tensor_copy(out=Yn.ap[:].rearrange("p h j -> p (h j)"), in_=pY.ap[0:64, :]), [pY], [Yn])
            for h in range(8):
                P.op("tensor", lambda e, h=h, Yn=Yn, R=R: e.matmul(pXR.ap[0:64, 64 * h:64 * h + 64], lhsT=Yn.ap[:, h, :], rhs=R.ap[:, h, :],
                                                                  start=True, stop=True), [Yn, R], [pXR], partial=(h > 0))
            P.op("vector", lambda e, R=R: e.tensor_tensor(out=R.ap[:].rearrange("p h j -> p (h j)"), in0=pXR.ap[0:64, :],
                                                          in1=R.ap[:].rearrange("p h j -> p (h j)"), op=ALU.add), [pXR, R], [R])
            X, Y = Xn, Yn
        u = r_u.next(); wT = r_wT.next()
        pu = (Wlo[0], Whi[0]); pw = Wlo[2]; pE = Whi[2]
        for h in range(8):
            P.op("tensor", lambda e, h=h: e.matmul(Wt[0].ap[0:64, 128 * h:128 * h + 128], lhsT=R.ap[:, h, :], rhs=vb.ap[:, h, :],
                                                   start=True, stop=True), [R, vb], list(pu), partial=(h > 0))
        for h in range(8):
            P.op("tensor", lambda e, h=h: e.matmul(pw.ap[:, 64 * h:64 * h + 64], lhsT=kbe.ap[:, h, :], rhs=R.ap[:, h, :],
                                                   start=True, stop=True), [R, kbe], [pw], partial=(h > 0))
        P.op("scalar", lambda e: e.copy(out=u.ap[:].rearrange("p h e -> p (h e)"), in_=Wt[0].ap[0:64, :]), list(pu), [u])
        P.op("vector", lambda e: e.tensor_copy(out=wT.ap[:].rearrange("p h j -> p (h j)"), in_=pw.ap[:, :]), [pw], [wT])
        res = dict(u=u, wT=wT, kdec=kdec, gt=gt, ci=ci, d=d, want_out=want_out)
        if want_out:
            qdT = r_qdT.next()
            P.op("tensor", lambda e: e.matmul(pE.ap[:, :], lhsT=ones_b.ap[:], rhs=rhsE.ap[:].rearrange("p h j -> p (h j)"),
                                              start=True, stop=True), [ones_b, rhsE], [pE])
            P.op("vector", lambda e: e.tensor_tensor(out=qdT.ap[:], in0=qb.ap[:, :, qo:qo + 64],
                                                     in1=pE.ap[:, :].rearrange("p (h j) -> p h j", h=8), op=ALU.mult), [qb, pE], [qdT])
            res.update(qdT=qdT, qkT=qkT)
        return res

    def scan(pr):
        d = pr["d"]; ci = pr["ci"]; want_out = pr["want_out"]
        u, wT, kdec, gt = pr["u"], pr["wT"], pr["kdec"], pr["gt"]
        sf, sbb = Sf[d], Sb[d]
        vnew = r_vnew.next()
        pws = (Wlo[0], Whi[0]); pkv = (Wlo[1], Whi[1]); po = (Wlo[2], Whi[2])
        for h in range(8):
            P.op("tensor", lambda e, h=h: e.matmul(Wt[0].ap[0:64, 128 * h:128 * h + 128], lhsT=wT.ap[:, h, :], rhs=sbb.ap[:, h, :],
                                                   start=True, stop=True), [wT, sbb], list(pws), partial=(h > 0))
        P.op("vector", lambda e: e.tensor_tensor(out=vnew.ap[:].rearrange("p h e -> p (h e)"), in0=u.ap[:].rearrange("p h e -> p (h e)"),
                                                 in1=Wt[0].ap[0:64, :], op=ALU.subtract), [u] + list(pws), [vnew])
        for h in range(8):
            P.op("tensor", lambda e, h=h: e.matmul(Wt[1].ap[:, 128 * h:128 * h + 128], lhsT=kdec.ap[:, h, :], rhs=vnew.ap[:, h, :],
                                                   start=True, stop=True), [kdec, vnew], list(pkv), partial=(h > 0))
        if want_out:
            qdT, qkT = pr["qdT"], pr["qkT"]
            for h in range(8):
                P.op("tensor", lambda e, h=h: e.matmul(Wt[2].ap[0:64, 128 * h:128 * h + 128], lhsT=qdT.ap[:, h, :], rhs=sbb.ap[:, h, :],
                                                       start=True, stop=False), [qdT, sbb], list(po), partial=(h > 0))
                P.op("tensor", lambda e, h=h: e.matmul(Wt[2].ap[0:64, 128 * h:128 * h + 128], lhsT=qkT.ap[:, h, :], rhs=vnew.ap[:, h, :],
                                                       start=False, stop=True), [qkT, vnew], list(po), partial=True)
        P.op("gpsimd", lambda e: e.tensor_tensor(out=sf.ap[:], in0=sf.ap[:], in1=bcl(gt.ap[:, 0:8], 128), op=ALU.mult), [sf, gt], [sf])
        P.op("vector", lambda e: e.tensor_tensor(out=sf.ap[:].rearrange("p h e -> p (h e)"), in0=sf.ap[:].rearrange("p h e -> p (h e)"),
                                                 in1=Wt[1].ap[:, :], op=ALU.add), [sf] + list(pkv), [sf])
        P.op("scalar", lambda e: e.copy(out=sbb.ap[:], in_=sf.ap[:]), [sf], [sbb])
        if not want_out:
            return
        t0 = ci * 64 - C
        if d == 0:
            osb = r_osb.next()
            P.op("scalar", lambda e: e.copy(out=osb.ap[:].rearrange("p h e -> p (h e)"), in_=Wt[2].ap[0:64, :]), list(po), [osb])
            P.dma(OA_s.ap[t0:t0 + 64, :], osb.ap[:].rearrange("p h e -> p (h e)"), reads=[osb], writes=[OA_s], partial=True,
                  key=osb.b.name + "s", eng=ST_ENG)
        else:
            oa = r_oa.next(); z = r_z.next(); osb = r_osb.next(); sq = r_sq.next(); ss = r_ss.next(); on = r_on.next()
            P.dma(oa.ap[:].rearrange("p h e -> p (h e)"), OA_s.ap[t0:t0 + 64, :], reads=[OA_s], writes=[oa])
            P.dma(z.ap[:].rearrange("p h e -> p (h e)"), Z_s.ap[t0:t0 + 64, :], writes=[z])
            P.op("vector", lambda e: e.tensor_tensor(out=osb.ap[:].rearrange("p h e -> p (h e)"), in0=Wt[2].ap[0:64, :],
                                                     in1=oa.ap[:].rearrange("p h e -> p (h e)"), op=ALU.add), list(po) + [oa], [osb])
            P.op("gpsimd", lambda e: e.tensor_tensor(out=sq.ap[:], in0=osb.ap[:], in1=osb.ap[:], op=ALU.mult), [osb], [sq])
            P.op("vector", lambda e: e.reduce_sum(out=ss.ap[:, 0:8], in_=sq.ap[:], axis=AX.X), [sq], [ss])
            P.op("scalar", lambda e: e.activation(out=ss.ap[:, 0:8], in_=ss.ap[:, 0:8], func=AF.Ln, scale=1.0 / 128.0, bias=epsb.ap[0:64, 1:2]),
                 [ss, epsb], [ss])
            P.op("scalar", lambda e: e.activation(out=ss.ap[:, 0:8], in_=ss.ap[:, 0:8], func=AF.Exp, scale=-0.5), [ss], [ss])
            P.op("gpsimd", lambda e: e.tensor_tensor(out=z.ap[:], in0=z.ap[:], in1=gnorm.ap[0:64, :].unsqueeze(1).to_broadcast([64, 8, 128]),
                                                     op=ALU.mult), [z, gnorm], [z])
            P.op("vector", lambda e: e.tensor_tensor(out=osb.ap[:], in0=osb.ap[:], in1=bcl(ss.ap[:, 0:8], 128), op=ALU.mult), [osb, ss], [osb])
            P.op("vector", lambda e: e.tensor_tensor(out=on.ap[:], in0=osb.ap[:], in1=z.ap[:], op=ALU.mult), [osb, z], [on])
            cj = ci - CH_CTX
            blk_i = (cj % 8)
            if blk_i == 7 or "ob" not in scan.__dict__:
                scan.ob = oblk.next()
            ob_ = scan.ob
            for h in range(8):
                P.op("tensor", lambda e, h=h: e.transpose(BT.ap[:, 64 * h:64 * h + 64], on.ap[:, h, :], identb.ap[0:64, 0:64]),
                     [on, identb], [BT], partial=(h > 0))
            P.op("scalar", lambda e: e.copy(out=ob_.ap[:, :, 64 * blk_i:64 * blk_i + 64], in_=BT.ap[:, 0:512].rearrange("p (h j) -> p h j", h=8)),
                 [BT], [ob_], partial=True)
            if blk_i == 0:
                tb = (cj // 8) * 512
                P.dma(ODN_T.ap[:, tb:tb + 512].rearrange("(h d) t -> d h t", h=8), ob_.ap[:], reads=[ob_], writes=[ODN_T], partial=True,
                      key=ob_.b.name + "s", eng=ST_ENG)

    seqA = [(ci, 0, False) for ci in range(CH_CTX)] + [(ci, 0, True) for ci in range(CH_CTX, CH_OWN_END)]
    seqB = ([(ci, 1, False) for ci in range(CH_CTX - 1, -1, -1)] + [(ci, 1, False) for ci in range(NCH - 1, CH_OWN_END - 1, -1)]
            + [(ci, 1, True) for ci in range(CH_OWN_END - 1, CH_CTX - 1, -1)])
    for seq in (seqA, seqB):
        prev = None
        for item in seq:
            pr = prep(*item)
            if prev is not None:
                scan(prev)
            prev = pr
        scan(prev)


def phase2(nc, cfg, P, st, sb, ps, env):
    S, C, T, NT = cfg.S, cfg.C, cfg.T, cfg.NT
    NKT = NT // 128
    NQG = T // 512
    QT_da, KT_da, V_da, ODA_T, dalam, subln, identb, epsb = (env[k] for k in (
        "QT_da", "KT_da", "V_da", "ODA_T", "dalam", "subln", "identb", "epsb"))
    ktb = Ring([sb(st, "ktb%d" % i, [128, NT], BF16) for i in range(2)])
    vtb = Ring([sb(st, "vtb%d" % i, [128, NKT, 130], BF16) for i in range(2)])
    qtb = Ring([sb(st, "qtb%d" % i, [128, T], BF16) for i in range(2)])
    SP = Ring([ps(st, "SP%d" % i, [128, 1024]) for i in range(2)])
    ACC = ps(st, "ACC", [128, 3, 512])
    PTr = ps(st, "PTr", [128, 4, 128], BF16)
    pexp = Ring([sb(st, "pexp%d" % i, [128, 1024], BF16) for i in range(3)])
    lam_t = sb(st, "lam_t", [128, 256])
    lam_p = sb(st, "lam_p", [128, 2, 64])
    lam_s = sb(st, "lam_s", [128, 4])
    gsub = sb(st, "gsub", [128, 128])
    rr = Ring([sb(st, "rr%d" % i, [128, 4]) for i in range(4)])
    tt = Ring([sb(st, "tt%d" % i, [128, 128]) for i in range(2)])
    oo = Ring([sb(st, "oo%d" % i, [128, 128]) for i in range(2)])
    junk = sb(st, "junk", [128, 128])
    onb = Ring([sb(st, "onb%d" % i, [128, 128], BF16) for i in range(4)])
    otb = Ring([sb(st, "otb%d" % i, [128, 512], BF16) for i in range(2)])

    def acc(m, j):
        a = m * 4 + j
        return ACC.ap[:, a // 3, (a % 3) * 160:(a % 3) * 160 + 129]

    lam_init = 0.8 - 0.6 * math.exp(-0.3 * 0)
    P.dma(lam_t.ap[:], dalam.ap.partition_broadcast(128), writes=[lam_t])
    P.dma(gsub.ap[:], subln.ap.partition_broadcast(128), writes=[gsub])
    P.op("vector", lambda e: e.tensor_tensor(out=lam_p.ap[:, 0, :], in0=lam_t.ap[:, 0:64], in1=lam_t.ap[:, 64:128], op=ALU.mult),
         [lam_t], [lam_p], partial=True)
    P.op("vector", lambda e: e.tensor_tensor(out=lam_p.ap[:, 1, :], in0=lam_t.ap[:, 128:192], in1=lam_t.ap[:, 192:256], op=ALU.mult),
         [lam_t], [lam_p], partial=True)
    P.op("vector", lambda e: e.reduce_sum(out=lam_s.ap[:, 0:2], in_=lam_p.ap[:], axis=AX.X), [lam_p], [lam_s])
    P.op("scalar", lambda e: e.activation(out=lam_s.ap[:, 0:2], in_=lam_s.ap[:, 0:2], func=AF.Exp), [lam_s], [lam_s])
    P.op("vector", lambda e: e.tensor_tensor(out=lam_s.ap[:, 2:3], in0=lam_s.ap[:, 1:2], in1=lam_s.ap[:, 0:1], op=ALU.subtract),
         [lam_s], [lam_s])
    P.op("vector", lambda e: e.tensor_scalar_add(out=lam_s.ap[:, 2:3], in0=lam_s.ap[:, 2:3], scalar1=-lam_init), [lam_s], [lam_s])
    P.op("vector", lambda e: e.tensor_scalar_mul(out=gsub.ap[:], in0=gsub.ap[:], scalar1=1.0 - lam_init), [gsub], [gsub])
    for v in vtb.tiles:
        P.op("gpsimd", lambda e, v=v: e.memset(v.ap[:, :, 128:130], 1.0), [], [v])

    def load_head(h):
        kt_ = ktb.next(); vt_ = vtb.next(); qt_ = qtb.next()
        P.dma(kt_.ap[:], KT_da.ap[h], writes=[kt_])
        P.dma(vt_.ap[:, :, 0:128], V_da.ap[h], writes=[vt_], partial=True)
        P.dma(qt_.ap[:], QT_da.ap[h], writes=[qt_])
        return kt_, vt_, qt_

    nxt = load_head(0)
    for h in range(8):
        kt_, vt_, qt_ = nxt
        for qg in range(NQG):
            if qg == min(1, NQG - 1) and h + 1 < 8:
                nxt = load_head(h + 1)
            q0 = qg * 512
            sps = {}
            pes = {}
            for kt in range(NKT + 1):
                if kt < NKT:
                    sp = SP.next(); pe_ = pexp.next()
                    sps[kt] = sp; pes[kt] = pe_
                    for m in range(2):
                        P.op("tensor", lambda e, sp=sp, m=m, kt=kt, kt_=kt_, qt_=qt_, q0=q0: e.matmul(
                            sp.ap[:, m * 512:(m + 1) * 512], lhsT=kt_.ap[64 * m:64 * m + 64, kt * 128:(kt + 1) * 128],
                            rhs=qt_.ap[64 * m:64 * m + 64, q0:q0 + 512], start=True, stop=True),
                             [kt_, qt_], [sp], partial=(m > 0))
                    P.op("scalar", lambda e, sp=sp, pe_=pe_: e.activation(out=pe_.ap[:], in_=sp.ap[:], func=AF.Exp, scale=0.125),
                         [sp], [pe_])
                if kt >= 1:
                    k1 = kt - 1
                    pe_ = pes.pop(k1)
                    for m in range(2):
                        for j in range(4):
                            P.op("tensor", lambda e, pe_=pe_, m=m, j=j, k1=k1, vt_=vt_: e.matmul(
                                acc(m, j), lhsT=pe_.ap[:, m * 512 + j * 128:m * 512 + (j + 1) * 128], rhs=vt_.ap[:, k1, 0:129],
                                start=(k1 == 0 and (m * 4 + j) % 3 == 0), stop=(k1 == NKT - 1), skip_group_check=True),
                                 [pe_, vt_], [ACC], partial=not (k1 == 0 and m == 0 and j == 0))
            ot_ = otb.next()
            for j in range(4):
                r_ = rr.next(); t_ = tt.next(); o_ = oo.next(); on_ = onb.next()
                P.op("vector", lambda e, r_=r_, j=j: e.reciprocal(out=r_.ap[:, 0:1], in_=acc(0, j)[:, 128:129]), [ACC], [r_])
                P.op("vector", lambda e, r_=r_, j=j: e.reciprocal(out=r_.ap[:, 1:2], in_=acc(1, j)[:, 128:129]), [ACC], [r_], partial=True)
                P.op("vector", lambda e, r_=r_: e.tensor_tensor(out=r_.ap[:, 1:2], in0=r_.ap[:, 1:2], in1=lam_s.ap[:, 2:3], op=ALU.mult),
                     [r_, lam_s], [r_])
                P.op("scalar", lambda e, t_=t_, r_=r_, j=j: e.activation(out=t_.ap[:], in_=acc(0, j)[:, 0:128], func=AF.Copy,
                                                                        scale=r_.ap[:, 0:1]), [ACC, r_], [t_])
                P.op("vector", lambda e, t_=t_, r_=r_, o_=o_, j=j: e.scalar_tensor_tensor(
                    out=o_.ap[:], in0=acc(1, j)[:, 0:128], scalar=r_.ap[:, 1:2], in1=t_.ap[:], op0=ALU.mult, op1=ALU.add),
                     [ACC, r_, t_], [o_])
                P.op("scalar", lambda e, o_=o_, r_=r_: e.activation(out=junk.ap[:], in_=o_.ap[:], func=AF.Square,
                                                                   accum_out=r_.ap[:, 2:3]), [o_], [junk, r_], partial=True)
                P.op("scalar", lambda e, r_=r_: e.activation(out=r_.ap[:, 2:3], in_=r_.ap[:, 2:3], func=AF.Ln, scale=1.0 / 128.0,
                                                            bias=epsb.ap[:, 1:2]), [r_, epsb], [r_])
                P.op("scalar", lambda e, r_=r_: e.activation(out=r_.ap[:, 2:3], in_=r_.ap[:, 2:3], func=AF.Exp, scale=-0.5), [r_], [r_])
                P.op("vector", lambda e, o_=o_, r_=r_, on_=on_: e.scalar_tensor_tensor(
                    out=on_.ap[:], in0=o_.ap[:], scalar=r_.ap[:, 2:3], in1=gsub.ap[:], op0=ALU.mult, op1=ALU.mult),
                     [o_, r_, gsub], [on_])
                P.op("tensor", lambda e, on_=on_, j=j: e.transpose(PTr.ap[:, j, :], on_.ap[:], identb.ap[:]), [on_, identb], [PTr],
                     partial=(j > 0))
            P.op("vector", lambda e, ot_=ot_: e.tensor_copy(out=ot_.ap[:], in_=PTr.ap[:].rearrange("p a b -> p (a b)")), [PTr], [ot_])
            P.dma(ODA_T.ap[h * 128:(h + 1) * 128, q0:q0 + 512], ot_.ap[:], reads=[ot_], writes=[ODA_T], partial=True,
                  key=ot_.b.name + "s", eng=ST_ENG)


def phase1b(nc, cfg, P, st, sb, ps, env):
    S, C, T, NT, NTP = cfg.S, cfg.C, cfg.T, cfg.NT, cfg.NTP
    uT, ident, identb, win, convw, alog, dtb, cosT, sinT, epsb = (env[k] for k in (
        "uT", "ident", "identb", "win", "convw", "alog", "dtb", "cosT", "sinT", "epsb"))
    QT_da, KT_da, V_da, QT_dn, KT_dn, K_tok, V_tok, G_s, Z_s, MG_s = (env[k] for k in (
        "QT_da", "KT_da", "V_da", "QT_dn", "KT_dn", "K_tok", "V_tok", "G_s", "Z_s", "MG_s"))
    groups, pos = env["groups"], env["pos"]
    NTI = NT // 128
    STQ = "scalar"

    wst = Ring([sb(st, "wst%d" % i, [128, KC, 128]) for i in range(2)])
    wbn = Ring([sb(st, "wbn%d" % i, [128, KC, 128], BF16) for i in range(4)])
    wbw = sb(st, "wbw", [128, KC, 512], BF16)
    PA = Ring([ps(st, "PA%d" % i, [128, 1024]) for i in range(2)])
    PB = Ring([ps(st, "PB%d" % i, [128, 512]) for i in range(2)])
    PC = ps(st, "PC", [128, 512])
    PT = ps(st, "PT", [128, 4, 128], BF16)
    tabc = Ring([sb(st, "tabc%d" % i, [128, 512]) for i in range(2)])
    tabs = Ring([sb(st, "tabs%d" % i, [128, 512]) for i in range(2)])
    fa = Ring([sb(st, "fa%d" % i, [128, 512]) for i in range(2)])
    fb = Ring([sb(st, "fb%d" % i, [128, 512]) for i in range(2)])
    ob = Ring([sb(st, "ob%d" % i, [128, 512], BF16) for i in range(2)])
    preb = Ring([sb(st, "preb%d" % i, [128, 520], BF16) for i in range(2)])
    sqb = Ring([sb(st, "sqb%d" % i, [128, 512], BF16) for i in range(2)])
    tok = Ring([sb(st, "tok%d" % i, [128, 4, 128], BF16) for i in range(2)])
    cw = Ring([sb(st, "cw%d" % i, [128, 7]) for i in range(2)])
    dg = Ring([sb(st, "dg%d" % i, [128, 7, 128], BF16) for i in range(2)])
    ones_k = sb(st, "ones_k", [128, 128], BF16)
    ones_q = sb(st, "ones_q", [128, 128], BF16)
    gpar = sb(st, "gpar", [128, 2, 16])
    gpre = sb(st, "gpre", [128, NTI, 32])
    ga = sb(st, "ga", [128, NTI, 16])

    P.op("vector", lambda e: e.memset(ones_k.ap[:], 1.0), [], [ones_k])
    P.op("vector", lambda e: e.memset(ones_q.ap[:], 128.0), [], [ones_q])
    P.dma(gpar.ap[:, 0, :], alog.ap[0, :].partition_broadcast(128), writes=[gpar], partial=True, key="gpar")
    P.dma(gpar.ap[:, 1, :], dtb.ap[0, :].partition_broadcast(128), writes=[gpar], partial=True, key="gpar")
    P.op("scalar", lambda e: e.activation(out=gpar.ap[:, 0, :], in_=gpar.ap[:, 0, :], func=AF.Exp), [gpar], [gpar])
    P.op("vector", lambda e: e.tensor_scalar_mul(out=gpar.ap[:, 0, :], in0=gpar.ap[:, 0, :], scalar1=-1.0), [gpar], [gpar])

    def load_w(c0, ncols):
        wb = wbw if ncols > 128 else wbn.next()
        for j in range((ncols + 127) // 128):
            n = min(128, ncols - 128 * j)
            w = wst.next()
            P.dma(w.ap[:, :, 0:n], win.ap[:, c0 + 128 * j:c0 + 128 * j + n].rearrange("(k p) c -> p k c", p=128), writes=[w])
            P.op("gpsimd", lambda e, w=w, wb=wb, j=j, n=n: e.tensor_copy(out=wb.ap[:, :, 128 * j:128 * j + n], in_=w.ap[:, :, 0:n]),
                 [w], [wb], partial=(j > 0))
        return wb

    heads = [(kind, o_nat, o_perm, dst, h) for (kind, o_nat, o_perm, dst) in (("q", OQ, OQP, QT_da), ("k", OK_, OKP, KT_da))
             for h in range(8)]
    items = []
    for hi, (kind, o_nat, o_perm, dst, h) in enumerate(heads):
        first = True
        for (g0, L, own) in groups:
            if kind == "q" and not own:
                continue
            items.append((hi, g0, L, first))
            first = False
    wts = {}
    tabsd = {}

    def da_weights(hi):
        kind, o_nat, o_perm, dst, h = heads[hi]
        wts[hi] = (load_w(o_nat + 128 * h, 128), load_w(o_perm + 128 * h, 128))

    def da_tables(ii):
        hi, g0, L, first = items[ii]
        tc_ = tabc.next(); ts_ = tabs.next()
        P.dma(tc_.ap[:, 0:L], cosT.ap[:, g0:g0 + L], writes=[tc_])
        P.dma(ts_.ap[:, 0:L], sinT.ap[:, g0:g0 + L], writes=[ts_])
        tabsd[ii] = (tc_, ts_)

    da_weights(0)
    da_tables(0)
    for ii, (hi, g0, L, first) in enumerate(items):
        kind, o_nat, o_perm, dst, h = heads[hi]
        if first and hi + 1 < len(heads):
            da_weights(hi + 1)
        if ii + 1 < len(items):
            da_tables(ii + 1)
        wn, wp = wts[hi]
        tc_, ts_ = tabsd.pop(ii)
        pa = PA.next(); a1 = fa.next(); a2 = fb.next(); o_ = ob.next()
        c0 = pos(g0)
        for k in range(KC):
            P.op("tensor", lambda e, k=k, pa=pa, wn=wn, c0=c0, L=L: e.matmul(
                pa.ap[:, 0:L], lhsT=wn.ap[:, k, 0:128], rhs=uT.ap[:, k, c0:c0 + L],
                start=(k == 0), stop=(k == KC - 1)), [wn, uT], [pa], partial=(k > 0))
        for k in range(KC):
            P.op("tensor", lambda e, k=k, pa=pa, wp=wp, c0=c0, L=L: e.matmul(
                pa.ap[:, 512:512 + L], lhsT=wp.ap[:, k, 0:128], rhs=uT.ap[:, k, c0:c0 + L],
                start=(k == 0), stop=(k == KC - 1)), [wp, uT], [pa], partial=True)
        P.op("vector", lambda e, pa=pa, tc_=tc_, a1=a1, L=L: e.tensor_tensor(
            out=a1.ap[:, 0:L], in0=pa.ap[:, 0:L], in1=tc_.ap[:, 0:L], op=ALU.mult), [pa, tc_], [a1])
        P.op("vector", lambda e, pa=pa, ts_=ts_, a2=a2, L=L: e.tensor_tensor(
            out=a2.ap[:, 0:L], in0=pa.ap[:, 512:512 + L], in1=ts_.ap[:, 0:L], op=ALU.mult), [pa, ts_], [a2])
        P.op("gpsimd", lambda e, a1=a1, a2=a2, o_=o_, L=L: e.tensor_tensor(
            out=o_.ap[:, 0:L], in0=a1.ap[:, 0:L], in1=a2.ap[:, 0:L], op=ALU.add), [a1, a2], [o_])
        off = g0 - C if kind == "q" else g0
        P.dma(dst.ap[h, :, off:off + L], o_.ap[:, 0:L], reads=[o_], writes=[dst], partial=True,
              key=o_.b.name + "s", eng=STQ)

    for (kind, o_w) in (("v", OV), ("z", OZ)):
        for half in range(2):
            wb = load_w(o_w + 512 * half, 512)
            for (g0, L, own) in groups:
                if kind == "z" and not own:
                    continue
                for j in range(L // 128):
                    tk = g0 + 128 * j
                    c0 = pos(tk)
                    pb = PB.next()
                    for k in range(KC):
                        P.op("tensor", lambda e, k=k, pb=pb, wb=wb, c0=c0: e.matmul(
                            pb.ap[:], lhsT=uT.ap[:, k, c0:c0 + 128], rhs=wb.ap[:, k, :],
                            start=(k == 0), stop=(k == KC - 1)), [wb, uT], [pb], partial=(k > 0))
                    if kind == "v":
                        o_ = ob.next()
                        P.op("vector", lambda e, pb=pb, o_=o_: e.tensor_copy(out=o_.ap[:], in_=pb.ap[:]), [pb], [o_])
                        P.dma(V_da.ap[4 * half:4 * half + 4, :, tk // 128, :].rearrange("h t e -> t h e"),
                              o_.ap[:].rearrange("t (h e) -> t h e", h=4), reads=[o_], writes=[V_da], partial=True,
                              key=o_.b.name + "s", eng=STQ)
                    else:
                        o_ = fb.next()
                        P.op("scalar", lambda e, pb=pb, o_=o_: e.activation(out=o_.ap[:], in_=pb.ap[:], func=AF.Silu), [pb], [o_])
                        P.dma(Z_s.ap[tk - C:tk - C + 128, 512 * half:512 * half + 512], o_.ap[:], reads=[o_],
                              writes=[Z_s], partial=True, key=o_.b.name + "s", eng=STQ)

    blocks = [(gi, o_w, cb) for gi, o_w in enumerate((OM1, OM2)) for cb in range(8)]
    wnext = load_w(blocks[0][1] + 128 * blocks[0][2], 128)
    for bi, (gi, o_w, cb) in enumerate(blocks):
        wb = wnext
        if bi + 1 < len(blocks):
            wnext = load_w(blocks[bi + 1][1] + 128 * blocks[bi + 1][2], 128)
        for (g0, L, own) in groups:
            if not own:
                continue
            pa = PA.next(); o_ = fa.next()
            c0 = pos(g0)
            for k in range(KC):
                P.op("tensor", lambda e, k=k, pa=pa, wb=wb, c0=c0, L=L: e.matmul(
                    pa.ap[:, 0:L], lhsT=wb.ap[:, k, 0:128], rhs=uT.ap[:, k, c0:c0 + L],
                    start=(k == 0), stop=(k == KC - 1)), [wb, uT], [pa], partial=(k > 0))
            P.op("scalar", lambda e, pa=pa, o_=o_, L=L: e.activation(out=o_.ap[:, 0:L], in_=pa.ap[:, 0:L], func=AF.Sigmoid), [pa], [o_])
            P.dma(MG_s.ap[gi, 128 * cb:128 * cb + 128, g0 - C:g0 - C + L], o_.ap[:, 0:L], reads=[o_],
                  writes=[MG_s], partial=True, key=o_.b.name + "s", eng=STQ)

    wb = load_w(OG, 32)
    for i in range(NTI):
        c0 = pos(128 * i)
        pb = PB.next()
        for k in range(KC):
            P.op("tensor", lambda e, k=k, pb=pb, c0=c0, wb=wb: e.matmul(
                pb.ap[:, 0:32], lhsT=uT.ap[:, k, c0:c0 + 128], rhs=wb.ap[:, k, 0:32],
                start=(k == 0), stop=(k == KC - 1)), [wb, uT], [pb], partial=(k > 0))
        P.op("vector", lambda e, pb=pb, i=i: e.tensor_copy(out=gpre.ap[:, i, :], in_=pb.ap[:, 0:32]), [pb], [gpre], partial=True)
    gy_ap = gpre.ap[:, :, 0:16]
    gb_ap = gpre.ap[:, :, 16:32]
    P.op("vector", lambda e: e.tensor_tensor(out=gy_ap, in0=gy_ap, in1=gpar.ap[:, 1:2, :].to_broadcast([128, NTI, 16]), op=ALU.add),
         [gpre, gpar], [gpre])
    P.op("scalar", lambda e: e.activation(out=gb_ap, in_=gb_ap, func=AF.Sigmoid), [gpre], [gpre])
    P.op("scalar", lambda e: e.activation(out=ga.ap[:], in_=gy_ap, func=AF.Abs), [gpre], [ga])
    P.op("scalar", lambda e: e.activation(out=ga.ap[:], in_=ga.ap[:], func=AF.Exp, scale=-1.0), [ga], [ga])
    P.op("scalar", lambda e: e.activation(out=ga.ap[:], in_=ga.ap[:], func=AF.Ln, bias=epsb.ap[:, 3:4]), [ga, epsb], [ga])
    P.op("vector", lambda e: e.scalar_tensor_tensor(out=ga.ap[:], in0=gy_ap, scalar=0.0, in1=ga.ap[:], op0=ALU.max, op1=ALU.add),
         [gpre, ga], [ga])
    P.op("vector", lambda e: e.tensor_tensor(out=gy_ap, in0=ga.ap[:], in1=gpar.ap[:, 0:1, :].to_broadcast([128, NTI, 16]), op=ALU.mult),
         [ga, gpar], [gpre])
    P.dma(G_s.ap[:, :].rearrange("(i p) c -> p i c", p=128), gpre.ap[:], reads=[gpre], writes=[G_s], key="gouts", eng=STQ)

    dheads = [(ci, kind, o_w, h) for ci, (kind, o_w) in enumerate((("q", ODQ), ("k", ODK), ("v", ODV))) for h in range(8)]
    dw = {}

    def dn_weights(di):
        ci, kind, o_w, h = dheads[di]
        wb = load_w(o_w + 128 * h, 128)
        cw_ = cw.next(); dg_ = dg.next()
        ch0 = ci * 1024 + 128 * h
        P.dma(cw_.ap[:], convw.ap[ch0:ch0 + 128, :], writes=[cw_])
        for tp in range(7):
            P.op("gpsimd", lambda e, tp=tp, cw_=cw_, dg_=dg_: e.tensor_tensor(
                out=dg_.ap[:, tp, :], in0=ident.ap[:], in1=cw_.ap[:, tp:tp + 1].to_broadcast([128, 128]), op=ALU.mult),
                 [cw_, ident], [dg_], partial=(tp > 0))
        dw[di] = (wb, dg_)

    dn_weights(0)
    for di, (ci, kind, o_w, h) in enumerate(dheads):
        if di + 1 < len(dheads):
            dn_weights(di + 1)
        wb, dg_ = dw.pop(di)
        for (g0, L, own) in groups:
            if kind == "q" and not own:
                continue
            pa = PA.next(); pb = PB.next(); pr_ = preb.next(); s_ = fa.next(); t_ = fb.next()
            c0 = pos(g0)
            n1 = min(512, L + 6)
            for k in range(KC):
                P.op("tensor", lambda e, k=k, pa=pa, c0=c0, n1=n1, wb=wb: e.matmul(
                    pa.ap[:, 0:n1], lhsT=wb.ap[:, k, 0:128], rhs=uT.ap[:, k, c0 - 3:c0 - 3 + n1],
                    start=(k == 0), stop=(k == KC - 1)), [wb, uT], [pa], partial=(k > 0))
            if L + 6 > 512:
                n2 = L + 6 - 512
                for k in range(KC):
                    P.op("tensor", lambda e, k=k, pa=pa, c0=c0, n2=n2, wb=wb: e.matmul(
                        pa.ap[:, 512:512 + n2], lhsT=wb.ap[:, k, 0:128], rhs=uT.ap[:, k, c0 + 509:c0 + 509 + n2],
                        start=(k == 0), stop=(k == KC - 1)), [wb, uT], [pa], partial=True)
            P.op("vector", lambda e, pa=pa, pr_=pr_, L=L: e.tensor_copy(out=pr_.ap[:, 0:L + 6], in_=pa.ap[:, 0:L + 6]), [pa], [pr_])
            for tp in range(7):
                P.op("tensor", lambda e, tp=tp, pb=pb, pr_=pr_, L=L, dg_=dg_: e.matmul(
                    pb.ap[:, 0:L], lhsT=dg_.ap[:, tp, :], rhs=pr_.ap[:, tp:tp + L], start=(tp == 0), stop=(tp == 6)),
                     [dg_, pr_], [pb], partial=(tp > 0))
            o_ = ob.next()
            if kind in ("q", "k"):
                sq_ = sqb.next()
                P.op("scalar", lambda e, pb=pb, s_=s_, L=L: e.activation(out=s_.ap[:, 0:L], in_=pb.ap[:, 0:L], func=AF.Silu), [pb], [s_])
                P.op("gpsimd", lambda e, s_=s_, sq_=sq_, L=L: e.tensor_tensor(
                    out=sq_.ap[:, 0:L], in0=s_.ap[:, 0:L], in1=s_.ap[:, 0:L], op=ALU.mult), [s_], [sq_])
                onesm = ones_q if kind == "q" else ones_k
                ecol = 2 if kind == "q" else 1
                P.op("tensor", lambda e, sq_=sq_, L=L, onesm=onesm: e.matmul(
                    PC.ap[:, 0:L], lhsT=onesm.ap[:], rhs=sq_.ap[:, 0:L], start=True, stop=True), [sq_, onesm], [PC])
                P.op("scalar", lambda e, t_=t_, L=L, ecol=ecol: e.activation(
                    out=t_.ap[:, 0:L], in_=PC.ap[:, 0:L], func=AF.Ln, bias=epsb.ap[:, ecol:ecol + 1]), [PC, epsb], [t_])
                P.op("scalar", lambda e, t_=t_, L=L: e.activation(out=t_.ap[:, 0:L], in_=t_.ap[:, 0:L], func=AF.Exp, scale=-0.5),
                     [t_], [t_])
                P.op("vector", lambda e, s_=s_, t_=t_, o_=o_, L=L: e.tensor_tensor(
                    out=o_.ap[:, 0:L], in0=s_.ap[:, 0:L], in1=t_.ap[:, 0:L], op=ALU.mult), [s_, t_], [o_])
                dstT = QT_dn if kind == "q" else KT_dn
                P.dma(dstT.ap[h, :, g0:g0 + L], o_.ap[:, 0:L], reads=[o_], writes=[dstT], partial=True,
                      key=o_.b.name + "s", eng=STQ)
            else:
                P.op("scalar", lambda e, pb=pb, o_=o_, L=L: e.activation(out=o_.ap[:, 0:L], in_=pb.ap[:, 0:L], func=AF.Silu), [pb], [o_])
            if kind in ("k", "v"):
                tk_ = tok.next()
                nj = L // 128
                for j in range(nj):
                    P.op("tensor", lambda e, j=j, o_=o_: e.transpose(PT.ap[:, j, :], o_.ap[:, 128 * j:128 * j + 128],
                                                                     identb.ap[:]), [o_, identb], [PT], partial=(j > 0))
                P.op("vector", lambda e, tk_=tk_, nj=nj: e.tensor_copy(out=tk_.ap[:, 0:nj, :], in_=PT.ap[:, 0:nj, :]), [PT], [tk_])
                dst = K_tok if kind == "k" else V_tok
                P.dma(dst.ap[h, g0:g0 + L, :].rearrange("(j p) d -> p j d", p=128), tk_.ap[:, 0:nj, :], reads=[tk_],
                      writes=[dst], partial=True, key=tk_.b.name + "s", eng=STQ)


def kernel(**inputs):
    inputs = {k: np.asarray(v) for k, v in inputs.items()}
    B, S, _ = inputs["x"].shape
    C = inputs["ctx"].shape[1]
    cfg = Cfg(S=S, C=C, debug=False)
    nc = build(cfg)
    in_names = {a.memorylocations[0].name for a in nc.m.functions[0].allocations
                if isinstance(a, mybir.MemoryLocationSet) and a.kind == "ExternalInput"}
    in_maps = []
    for b in range(B):
        for h in range(2):
            m = core_inputs(cfg, inputs, b, h)
            in_maps.append({k: v for k, v in m.items() if k in in_names})
    res = run_bass_kernel_spmd(nc, in_maps, core_ids=list(range(2 * B)))
    T = S // 2
    out = np.empty((B, S, D), np.float32)
    for b in range(B):
        out[b, :T] = res.results[2 * b]["out"]
        out[b, T:] = res.results[2 * b + 1]["out"][::-1]
    return out
```
